# Optimizing a Trainium2 kernel written in Bass

```python
import math
import jax, jax.numpy as jnp
from jax import lax
import numpy as np

D_MODEL = 1024
BATCH = 2
SEQ = 8192
DEPTH = 1

CHUNK = 64
Q_BLOCK = 128
HG_HEADS = 8
HG_DK = 128
HG_DV = D_MODEL // HG_HEADS
HG_KW = HG_HEADS * HG_DK
HG_VW = HG_HEADS * HG_DV
FOX_HEADS = 8
FOX_DH = 128
FOX_W = FOX_HEADS * FOX_DH
MEM_LEN = 256
MEM_HEADS = 4
MEM_DH = D_MODEL // MEM_HEADS
D_FF = 2816
N_BRANCH = 2
EPS = 1e-6
IN_SPLITS = [HG_KW, HG_KW, HG_VW, HG_VW, FOX_W, FOX_W, FOX_W, FOX_HEADS, N_BRANCH * D_MODEL]
IN_COLS = sum(IN_SPLITS)
IN_OFFSETS = list(np.cumsum(IN_SPLITS)[:-1])

kernel_name = "hgrn2_fox_macaron_sandwich_hybrid"


def rmsnorm(x, g):
    x32 = x.astype(jnp.float32)
    y = x32 * lax.rsqrt(jnp.mean(x32 * x32, axis=-1, keepdims=True) + EPS)
    return (y * g.astype(jnp.float32)).astype(x.dtype)


def swiglu(h, w_in, w_down):
    gate, up = jnp.split(h @ w_in, 2, axis=-1)
    return (jax.nn.silu(gate) * up) @ w_down


def hgrn2_chunkwise(q, f_logit, inp, lb):
    B, S = q.shape[0], q.shape[1]
    n = S // CHUNK
    f = lb + (1.0 - lb) * jax.nn.sigmoid(f_logit.astype(jnp.float32))
    logf = jnp.log(f)
    k = 1.0 - f
    q = jax.nn.silu(q.astype(jnp.float32))
    inp = inp.astype(jnp.float32)

    def to_chunks(t):
        return t.reshape(B, n, CHUNK, t.shape[2], t.shape[3]).transpose(1, 0, 3, 2, 4)

    qc, kc, ic = to_chunks(q), to_chunks(k), to_chunks(inp)
    bc = jnp.cumsum(to_chunks(logf), axis=3)
    mask = jnp.tril(jnp.ones((CHUNK, CHUNK), dtype=bool))[:, :, None]

    def step(state, xs):
        qt, kt, it, bt = xs
        diff = bt[:, :, :, None, :] - bt[:, :, None, :, :]
        decay = jnp.exp(jnp.where(mask, diff, -jnp.inf))
        a = jnp.einsum('bhtd,bhsd,bhtsd->bhts', qt, kt, decay)
        o_intra = jnp.einsum('bhts,bhsv->bhtv', a, it)
        o_inter = jnp.einsum('bhtd,bhdv->bhtv', qt * jnp.exp(bt), state)
        b_last = bt[:, :, -1:, :]
        new_state = jnp.exp(b_last[:, :, 0, :])[..., None] * state + jnp.einsum(
            'bhsd,bhsv->bhdv', kt * jnp.exp(b_last - bt), it)
        return new_state, o_intra + o_inter

    s0 = jnp.zeros((B, q.shape[2], HG_DK, HG_DV), jnp.float32)
    _, o = lax.scan(step, s0, (qc, kc, ic, bc))
    return o.transpose(1, 0, 3, 2, 4).reshape(B, S, HG_VW)


def fox_attention(q, k, v, f_logit):
    B, S, H, dh = q.shape
    n_blk = S // Q_BLOCK
    scale = 1.0 / math.sqrt(dh)
    c = jnp.cumsum(jax.nn.log_sigmoid(f_logit.astype(jnp.float32)), axis=1).transpose(0, 2, 1)
    k32 = k.astype(jnp.float32)
    v32 = v.astype(jnp.float32)
    qb = q.astype(jnp.float32).reshape(B, n_blk, Q_BLOCK, H, dh).transpose(1, 0, 3, 2, 4)
    cb = c.reshape(B, H, n_blk, Q_BLOCK).transpose(2, 0, 1, 3)
    kpos = jnp.arange(S)

    def block(args):
        qi, ci, blk = args
        s = jnp.einsum('bhqd,bkhd->bhqk', qi, k32) * scale
        s = s + ci[..., None] - c[:, :, None, :]
        qpos = blk * Q_BLOCK + jnp.arange(Q_BLOCK)
        s = jnp.where(kpos[None, :] <= qpos[:, None], s, -jnp.inf)
        p = jax.nn.softmax(s, axis=-1)
        return jnp.einsum('bhqk,bkhd->bqhd', p, v32)

    o = lax.map(block, (qb, cb, jnp.arange(n_blk)))
    return o.transpose(1, 0, 2, 3, 4).reshape(B, S, H * dh)


def mem_cross_attention(h, mem_n, w_mq, w_mkv, w_mo):
    B, S, _ = h.shape
    q = (h @ w_mq).reshape(B, S, MEM_HEADS, MEM_DH).astype(jnp.float32)
    k, v = jnp.split(mem_n @ w_mkv, 2, axis=-1)
    k = k.reshape(B, MEM_LEN, MEM_HEADS, MEM_DH).astype(jnp.float32)
    v = v.reshape(B, MEM_LEN, MEM_HEADS, MEM_DH).astype(jnp.float32)
    s = jnp.einsum('bqhd,bkhd->bhqk', q, k) * (1.0 / math.sqrt(MEM_DH))
    p = jax.nn.softmax(s, axis=-1)
    o = jnp.einsum('bhqk,bkhd->bqhd', p, v).reshape(B, S, D_MODEL).astype(h.dtype)
    return o @ w_mo


def setup_inputs(seed: int = 0) -> dict:
    key = jax.random.key(seed)
    ks = iter(jax.random.split(key, 40))
    f32 = jnp.float32

    def w(shape, fan_in):
        return jax.random.normal(next(ks), shape, f32) * (fan_in ** -0.5)

    def gain(shape):
        return 1.0 + 0.05 * jax.random.normal(next(ks), shape, f32)

    L, D = DEPTH, D_MODEL
    return {
        "x": jax.random.normal(next(ks), (BATCH, SEQ, D), f32),
        "mem": jax.random.normal(next(ks), (BATCH, MEM_LEN, D), f32),
        "ffn1_pre_g": gain((L, D)),
        "ffn1_w_in": w((L, D, 2 * D_FF), D),
        "ffn1_w_down": w((L, D_FF, D), D_FF),
        "ffn1_post_g": gain((L, D)),
        "mix_pre_g": gain((L, D)),
        "w_in": w((L, D, IN_COLS), D),
        "hg_lb_logits": 0.5 * jax.random.normal(next(ks), (L + 1, HG_HEADS, HG_DK), f32),
        "hg_norm_g": gain((L, HG_VW)),
        "fox_f_bias": 1.0 + 0.5 * jax.random.normal(next(ks), (L, FOX_HEADS), f32),
        "w_branch_a": w((L, HG_VW, D), HG_VW),
        "w_branch_b": w((L, FOX_W, D), FOX_W),
        "b_gate": 0.02 * jax.random.normal(next(ks), (L, N_BRANCH * D), f32),
        "w_out": w((L, D, D), D),
        "mix_post_g": gain((L, D)),
        "mem_pre_g": gain((L, D)),
        "mem_kv_g": gain((L, D)),
        "w_mq": w((L, D, D), D),
        "w_mkv": w((L, D, 2 * D), D),
        "w_mo": w((L, D, D), D),
        "mem_post_g": gain((L, D)),
        "ffn2_pre_g": gain((L, D)),
        "ffn2_w_in": w((L, D, 2 * D_FF), D),
        "ffn2_w_down": w((L, D_FF, D), D_FF),
        "ffn2_post_g": gain((L, D)),
    }


def reference(x, mem, ffn1_pre_g, ffn1_w_in, ffn1_w_down, ffn1_post_g,
              mix_pre_g, w_in, hg_lb_logits, hg_norm_g, fox_f_bias,
              w_branch_a, w_branch_b, b_gate, w_out, mix_post_g,
              mem_pre_g, mem_kv_g, w_mq, w_mkv, w_mo, mem_post_g,
              ffn2_pre_g, ffn2_w_in, ffn2_w_down, ffn2_post_g):
    B, S, D = x.shape
    lb_all = jnp.cumsum(jax.nn.softmax(hg_lb_logits.astype(jnp.float32), axis=0), axis=0)
    for l in range(DEPTH):
        h = rmsnorm(x, ffn1_pre_g[l])
        x = x + 0.5 * rmsnorm(swiglu(h, ffn1_w_in[l], ffn1_w_down[l]), ffn1_post_g[l])

        h = rmsnorm(x, mix_pre_g[l])
        q_a, f_a, i_a, g_a, q_b, k_b, v_b, f_b, gates = jnp.split(h @ w_in[l], IN_OFFSETS, axis=-1)

        o_a = hgrn2_chunkwise(q_a.reshape(B, S, HG_HEADS, HG_DK),
                              f_a.reshape(B, S, HG_HEADS, HG_DK),
                              i_a.reshape(B, S, HG_HEADS, HG_DV), lb_all[l])
        o_a = rmsnorm(o_a.astype(x.dtype), hg_norm_g[l]) * jax.nn.silu(g_a)
        y_a = o_a @ w_branch_a[l]

        o_b = fox_attention(q_b.reshape(B, S, FOX_HEADS, FOX_DH),
                            k_b.reshape(B, S, FOX_HEADS, FOX_DH),
                            v_b.reshape(B, S, FOX_HEADS, FOX_DH),
                            f_b + fox_f_bias[l])
        y_b = o_b.astype(x.dtype) @ w_branch_b[l]

        gate = jax.nn.sigmoid((gates + b_gate[l]).astype(jnp.float32)).reshape(B, S, N_BRANCH, D)
        y = (gate[:, :, 0] * y_a.astype(jnp.float32) + gate[:, :, 1] * y_b.astype(jnp.float32)).astype(x.dtype)
        x = x + rmsnorm(y @ w_out[l], mix_post_g[l])

        h = rmsnorm(x, mem_pre_g[l])
        mem_n = rmsnorm(mem, mem_kv_g[l])
        x = x + rmsnorm(mem_cross_attention(h, mem_n, w_mq[l], w_mkv[l], w_mo[l]), mem_post_g[l])

        h = rmsnorm(x, ffn2_pre_g[l])
        x = x + 0.5 * rmsnorm(swiglu(h, ffn2_w_in[l], ffn2_w_down[l]), ffn2_post_g[l])
    return x
```

```python
import contextlib
import math
import numpy as np
import concourse.bass as bass
import concourse.mybir as mybir
from concourse.bass_utils import run_bass_kernel_spmd

F32 = mybir.dt.float32
BF16 = mybir.dt.bfloat16
AF = mybir.ActivationFunctionType
ALU = mybir.AluOpType

SAME_ENGINE_SYNC = True
STOP_AFTER = 9
MIX_PARTS = 7
NOAG2 = False
EPS = 1e-6
D = 1024
S_OWN = 2048
S_ALL = 8192
DFF = 2816
NEG = -1.0e9


class Op:
    __slots__ = ("eng", "fn", "deps", "is_dma", "semkey", "idx", "signal", "count",
                 "dma_waits", "inc", "seg")


class Prog:
    def __init__(self, nc, es):
        self.nc = nc
        self.es = es
        self.ops = []
        self.lastw = {}
        self.readers = {}
        self.dma_tot = {}
        self.final_waits = {}
        self.last_on = {}
        self.seg = 0

    def add(self, eng, fn, reads=(), writes=(), dma=None, inc=16):
        if dma is not None:
            dma = "cc" if inc == 1 else ("q_" + eng)
        deps = {}
        for r in reads:
            w = self.lastw.get(r)
            if w is not None:
                deps[w.idx] = w
        for r in writes:
            w = self.lastw.get(r)
            if w is not None:
                deps[w.idx] = w
            for rd in self.readers.get(r, {}).values():
                deps[rd.idx] = rd
        op = Op()
        op.eng = eng
        op.fn = fn
        op.is_dma = dma is not None
        op.semkey = dma
        op.inc = inc
        op.idx = len(self.ops)
        op.signal = False
        op.count = 0
        op.seg = self.seg
        dma_waits = {}
        cdeps = []
        for d in deps.values():
            if d.is_dma:
                dma_waits[d.semkey] = self.dma_tot[d.semkey]
            elif d.fn is not None:
                cdeps.append(d)
        op.deps = cdeps
        op.dma_waits = dma_waits
        if dma is not None:
            self.dma_tot[dma] = self.dma_tot.get(dma, 0) + inc
        for r in reads:
            self.readers.setdefault(r, {})[(eng, op.idx) if op.is_dma else eng] = op
        for r in writes:
            self.lastw[r] = op
            self.readers[r] = {}
        self.ops.append(op)
        if not op.is_dma and fn is not None:
            self.last_on[eng] = op
        return op

    def barrier(self):
        lasts = list(self.last_on.values())
        tot = dict(self.dma_tot)
        for e in ["pe", "act", "dve", "pool", "sp"]:
            op = Op()
            op.eng = e
            op.fn = None
            op.is_dma = False
            op.semkey = None
            op.inc = 0
            op.idx = len(self.ops)
            op.signal = False
            op.count = 0
            op.deps = [d for d in lasts if d.eng != e]
            op.dma_waits = dict(tot)
            op.seg = self.seg
            self.ops.append(op)
        self.lastw = {}
        self.readers = {}
        self.last_on = {}
        self.seg += 1

    def wait_dma_at_end(self, key):
        for k in self.dma_tot:
            self.final_waits[k] = self.dma_tot[k]

    def emit(self):
        nc = self.nc
        engs = ["pe", "act", "dve", "pool", "sp"]
        for op in self.ops:
            for d in op.deps:
                if d.eng == op.eng and (op.eng == "pe" or not SAME_ENGINE_SYNC):
                    continue
                d.signal = True
        cnt = {}
        for op in self.ops:
            if op.is_dma or op.fn is None:
                continue
            if op.signal:
                k_ = (0, op.eng)
                cnt[k_] = cnt.get(k_, 0) + 1
                op.count = cnt[k_]
        sems = {k_: self.es.enter_context(nc.semaphore("s_%d_%s" % k_)) for k_ in cnt}
        dsems = {}
        for k in self.dma_tot:
            dsems[k] = self.es.enter_context(nc.semaphore("d_" + str(len(dsems))))
        per_eng = {e: [o for o in self.ops if o.eng == e] for e in engs}
        self.stats = {e: len(per_eng[e]) for e in engs}
        self.stats["sems"] = len(dsems) + len(sems)
        self.stats["counts"] = max(cnt.values()) if cnt else 0
        final_waits = self.final_waits

        def run(e, engine):
            waited = {}
            dwaited = {}
            for op in per_eng[e]:
                need = {}
                for d in op.deps:
                    if d.eng == e and (e == "pe" or not SAME_ENGINE_SYNC):
                        continue
                    k_ = (0, d.eng)
                    if d.count > need.get(k_, 0):
                        need[k_] = d.count
                for x, v in need.items():
                    if v > waited.get(x, 0):
                        engine.wait_ge(sems[x], v)
                        waited[x] = v
                for k, v in op.dma_waits.items():
                    if v > dwaited.get(k, 0):
                        engine.wait_ge(dsems[k], v)
                        dwaited[k] = v
                if op.fn is None:
                    continue
                inst = op.fn(engine)
                if op.is_dma:
                    inst.then_inc(dsems[op.semkey], op.inc)
                elif op.signal:
                    inst.then_inc(sems[(0, e)], 1)
            if e == "sp":
                for k, v in final_waits.items():
                    engine.wait_ge(dsems[k], v)

        with nc.Block() as block:
            @block.tensor
            def _(t):
                run("pe", t)

            @block.scalar
            def _(t):
                run("act", t)

            @block.vector
            def _(t):
                run("dve", t)

            @block.gpsimd
            def _(t):
                run("pool", t)

            @block.sync
            def _(t):
                run("sp", t)


_uid = [0]


def _nm(n):
    _uid[0] += 1
    return "%s_u%d" % (n, _uid[0])


_RANK = {}


def _rank(e):
    if id(e) not in _RANK:
        _RANK[id(e)] = (e, (e.partition_id() % 4) * 512)
    return _RANK[id(e)][1]


class Ring:
    def __init__(self, nc, es, name, n, shape, dtype, psum=False):
        self.bufs = []
        for i in range(n):
            nm = "%s%d" % (name, i)
            if psum:
                t = es.enter_context(nc.psum_tensor(_nm(nm), shape, dtype))
            else:
                t = es.enter_context(nc.sbuf_tensor(_nm(nm), shape, dtype))
            self.bufs.append((t, nm))
        self.i = 0

    def next(self):
        b = self.bufs[self.i % len(self.bufs)]
        self.i += 1
        return b


def MM(P, out, lhsT, rhs, start, stop, reads, writes):
    P.add("pe", lambda e: e.matmul(out, lhsT, rhs, start=start, stop=stop), reads, writes)


def TR(P, out, in_, ident, reads, writes):
    P.add("pe", lambda e: e.transpose(out, in_, ident), reads, writes)


def ACT(P, out, in_, func, reads, writes, bias=None, scale=None):
    kw = {}
    if bias is not None:
        kw["bias"] = bias
    if scale is not None:
        kw["scale"] = scale
    P.add("act", lambda e: e.activation(out=out, in_=in_, func=func, **kw), reads, writes)


def TT(P, out, in0, in1, op, reads, writes, eng="dve"):
    P.add(eng, lambda e: e.tensor_tensor(out=out, in0=in0, in1=in1, op=op), reads, writes)


def TS(P, out, in0, s1, s2, op0, op1, reads, writes, eng="dve"):
    if op1 is None:
        P.add(eng, lambda e: e.tensor_scalar(out=out, in0=in0, scalar1=s1, scalar2=None, op0=op0),
              reads, writes)
    else:
        P.add(eng, lambda e: e.tensor_scalar(out=out, in0=in0, scalar1=s1, scalar2=s2, op0=op0, op1=op1),
              reads, writes)


def STT(P, out, in0, scalar, in1, op0, op1, reads, writes):
    P.add("dve", lambda e: e.scalar_tensor_tensor(out=out, in0=in0, scalar=scalar, in1=in1,
                                                  op0=op0, op1=op1), reads, writes)


def CP(P, eng, out, in_, reads, writes):
    P.add(eng, lambda e: e.tensor_copy(out=out, in_=in_), reads, writes)


def RECIP(P, out, in_, reads, writes):
    P.add("dve", lambda e: e.reciprocal(out=out, in_=in_), reads, writes)


def MSET(P, eng, ap, val, writes):
    P.add(eng, lambda e: e.memset(ap, val), (), writes)


def DMA(P, q, out, in_, reads, writes, key):
    P.add(q, lambda e: e.dma_start(out=out, in_=in_), reads, writes, dma=key)


GAIN_NAMES = ["ffn1_pre_g", "ffn1_post_g", "mix_pre_g", "hg_norm_g", "mix_post_g",
              "mem_pre_g", "mem_kv_g", "mem_post_g", "ffn2_pre_g", "ffn2_post_g"]
C_ID, C_U, C_BO, C_UF, C_ONE, C_SEL, C_SL, C_MD = 0, 128, 256, 384, 512, 640, 642, 706
C_TOT = 706 + 512


def make_consts():
    c = np.zeros((128, C_TOT), np.float32)
    i = np.arange(128)
    c[:, C_ID:C_ID + 128] = np.eye(128)
    same = (i[:, None] // 64) == (i[None, :] // 64)
    c[:, C_U:C_U + 128] = ((i[:, None] <= i[None, :]) & same)
    c[:, C_BO:C_BO + 128] = same
    c[:, C_UF:C_UF + 128] = (i[:, None] <= i[None, :])
    c[:, C_ONE:C_ONE + 128] = 1.0
    c[:, C_SEL + 0] = (i < 64)
    c[:, C_SEL + 1] = (i >= 64)
    j = np.arange(64)
    c[:64, C_SL:C_SL + 64] = (j[:, None] < j[None, :])
    jj = np.arange(512)
    c[:, C_MD:C_MD + 512] = np.where(jj[None, :] < i[:, None], NEG, 0.0)
    return c


class Ctx:
    pass


def rms_rstd(P, C, src, src_key, cols, ncols=512, dim=1024.0):
    sq, sqk = C.sq.next()
    ACT(P, sq[:, :, 0:ncols], src[:, :, cols], AF.Square, [src_key], [sqk])
    ps, psk = C.ps.next()
    for kt in range(8):
        MM(P, ps[:, 0:ncols], C.ones_bf[:], sq[:, kt, 0:ncols], kt == 0, kt == 7, [sqk, "consts_bf"], [psk])
    r, rk = C.rs.next()
    TS(P, r[:, 0:ncols], ps[:, 0:ncols], 1.0 / dim, EPS, ALU.mult, ALU.add, [psk], [rk])
    ACT(P, r[:, 0:ncols], r[:, 0:ncols], AF.Sqrt, [rk], [rk])
    RECIP(P, r[:, 0:ncols], r[:, 0:ncols], [rk], [rk])
    return r, rk


def load_w(P, C, w_d, kt_n, c0, ncols, dst_c0=0, buf=None):
    if buf is None:
        buf = C.wring.next()
    t, k = buf
    return buf


def wview(t, kt_n, width):
    return t[:, 0:kt_n * width].rearrange("p (k c) -> p k c", k=kt_n)


def wload(P, t, k, w_d, kt_n, width, c0, ncols, dst_c0):
    v = wview(t, kt_n, width)
    src = w_d[:, c0:c0 + ncols].rearrange("(k p) c -> p k c", p=128)
    DMA(P, "pool", v[:, :, dst_c0:dst_c0 + ncols], src, [], [k], "w_" + k)


def norm_h(P, C, hT, hk, hcols, xcols, g_col0, n=512):
    r, rk = rms_rstd(P, C, C.xT, "xT", xcols, n)
    for kt in range(8):
        STT(P, hT[:, kt, hcols], C.xT[:, kt, xcols], C.gains[:, g_col0 + kt:g_col0 + kt + 1],
            r[:, 0:n], ALU.mult, ALU.mult, ["xT", rk, "gains"], [hk])


def post_norm_add(P, C, yT, yk, ycols, xcols, g_col0, n=512):
    r, rk = rms_rstd(P, C, yT, yk, ycols, n)
    for kt in range(8):
        t, tk = C.tmp.next()
        STT(P, t[:, 0:n], yT[:, kt, ycols], C.gains[:, g_col0 + kt:g_col0 + kt + 1],
            r[:, 0:n], ALU.mult, ALU.mult, [yk, rk, "gains"], [tk])
        TT(P, C.xT[:, kt, xcols], C.xT[:, kt, xcols], t[:, 0:n], ALU.add, ["xT", tk], ["xT"])


def ffn(P, C, es, nc, w_in_d, w_dn_d, g_pre, g_post_h):
    yT = es.enter_context(nc.sbuf_tensor(_nm("ffn_hy"), [128, 8, 1024], F32))
    hT = yT.bitcast(BF16)
    actT = es.enter_context(nc.sbuf_tensor(_nm("ffn_actT"), [128, 22, 1024], BF16))
    sg_ring = Ring(nc, es, "ffn_sg", 2, [128, 512], F32)
    for half in range(2):
        t0 = half * 1024
        for tt in range(2):
            norm_h(P, C, hT, "ffn_hy", slice(tt * 512, (tt + 1) * 512),
                   slice(t0 + tt * 512, t0 + (tt + 1) * 512), g_pre)
        for ch in range(11):
            wt, wk = C.wring.next()
            wload(P, wt, wk, w_in_d, 8, 512, ch * 256, 256, 0)
            wload(P, wt, wk, w_in_d, 8, 512, DFF + ch * 256, 256, 256)
            wv = wview(wt, 8, 512)
            for jj in range(2):
                j = ch * 2 + jj
                for tt in range(2):
                    tc_ = slice(tt * 512, (tt + 1) * 512)
                    pg, pgk = C.ps.next()
                    for kt in range(8):
                        MM(P, pg[:], wv[:, kt, jj * 128:(jj + 1) * 128], hT[:, kt, tc_], kt == 0, kt == 7,
                           [wk, "ffn_hy"], [pgk])
                    pu, puk = C.ps.next()
                    for kt in range(8):
                        MM(P, pu[:], wv[:, kt, 256 + jj * 128:256 + (jj + 1) * 128], hT[:, kt, tc_],
                           kt == 0, kt == 7, [wk, "ffn_hy"], [puk])
                    sg, sgk = sg_ring.next()
                    ACT(P, sg[:], pg[:], AF.Silu, [pgk], [sgk])
                    TT(P, actT[:, j, tc_], pu[:], sg[:], ALU.mult, [puk, sgk], ["ffn_actT%d" % j])
        akeys = ["ffn_actT%d" % j for j in range(22)]
        for dc in range(4):
            wt, wk = C.wring.next()
            wload(P, wt, wk, w_dn_d, 22, 256, dc * 256, 256, 0)
            wv = wview(wt, 22, 256)
            for dd in range(2):
                d = dc * 2 + dd
                for tt in range(2):
                    tc_ = slice(tt * 512, (tt + 1) * 512)
                    py, pyk = C.ps.next()
                    for j in range(22):
                        MM(P, py[:], wv[:, j, dd * 128:(dd + 1) * 128], actT[:, j, tc_], j == 0, j == 21,
                           [wk] + akeys, [pyk])
                    ACT(P, yT[:, d, tc_], py[:], AF.Copy, [pyk], ["ffn_hy"])
        for tt in range(2):
            post_norm_add(P, C, yT, "ffn_hy", slice(tt * 512, (tt + 1) * 512),
                          slice(t0 + tt * 512, t0 + (tt + 1) * 512), g_post_h)


def build_program(dbg=False):
    _RANK.clear()
    nc = bass.Bass("TRN2", target_bir_lowering=False)

    def din(name, shape, dt=F32):
        return nc.dram_tensor(name, shape, dt, kind="ExternalInput").ap()

    xT_d = din("xT", [D, S_OWN])
    memT_d = din("memT", [D, 256])
    ffn1_in = din("ffn1_w_in", [D, 2 * DFF])
    ffn1_dn = din("ffn1_w_down", [DFF, D])
    ffn2_in = din("ffn2_w_in", [D, 2 * DFF])
    ffn2_dn = din("ffn2_w_down", [DFF, D])
    w_loc_d = din("w_loc", [D, 3072])
    w_hg_d = din("w_hg", [D, 768])
    w_fq_d = din("w_fq", [D, 256])
    w_fk_d = din("w_fk", [D, 256])
    w_fv_d = din("w_fv", [D, 258])
    w_a_d = din("w_branch_a", [D, D])
    w_b_d = din("w_branch_b", [D, D])
    w_o_d = din("w_out", [D, D])
    w_mq_d = din("w_mq", [D, D])
    w_mkv_d = din("w_mkv", [D, 2 * D])
    w_mo_d = din("w_mo", [D, D])
    gains_d = din("gains", [128, 96])
    consts_d = din("consts", [128, C_TOT])
    lbl_d = din("lbl", [128, 512])
    fbias_d = din("fbias", [128, 2])
    outT_d = nc.dram_tensor("outT", [D, S_OWN], F32, kind="ExternalOutput").ap()
    dbg_x1 = dbg_o = None
    if dbg:
        dbg_x1 = nc.dram_tensor("dbg_x1", [D, S_OWN], F32, kind="ExternalOutput").ap()
        dbg_o = nc.dram_tensor("dbg_o", [512, S_ALL], F32, kind="ExternalOutput").ap()
        dbg_x2 = nc.dram_tensor("dbg_x2", [D, S_OWN], F32, kind="ExternalOutput").ap()
        dbg_x3 = nc.dram_tensor("dbg_x3", [D, S_OWN], F32, kind="ExternalOutput").ap()

    ag1_in = [nc.dram_tensor("ag1_in%d" % i, [256, S_OWN], BF16).ap() for i in range(4)]
    ag1_out = [nc.dram_tensor("ag1_out%d" % i, [1024, S_OWN], BF16).ap() for i in range(4)]
    ag2_in = [[nc.dram_tensor("ag2_in%d_%d" % (q, tq), [128, S_OWN], F32).ap() for tq in range(4)] for q in range(4)]
    ag2_out = [nc.dram_tensor("ag2_out%d" % q, [4 * 512, S_OWN], F32).ap() for q in range(4)]
    x_spill = nc.dram_tensor("x_spill", [D, S_OWN], F32).ap()
    crow_d = nc.dram_tensor("crow_d", [3, 64 * 128], BF16).ap()
    RG = [[0, 1, 2, 3], [4, 5, 6, 7]]

    with contextlib.ExitStack() as es_all:
        P = Prog(nc, es_all)
        C = Ctx()
        C.consts = es_all.enter_context(nc.sbuf_tensor(_nm("consts_sb"), [128, C_TOT], F32))
        C.consts_bf = es_all.enter_context(nc.sbuf_tensor(_nm("consts_bf_sb"), [128, C_TOT], BF16))
        C.gains = es_all.enter_context(nc.sbuf_tensor(_nm("gains_sb"), [128, 96], F32))
        C.ones_bf = C.consts_bf[:, C_ONE:C_ONE + 128]
        C.ident_bf = C.consts_bf[:, C_ID:C_ID + 128]
        C.ps = Ring(nc, es_all, "ps", 8, [128, 512], F32, psum=True)
        DMA(P, "sp", C.consts[:], consts_d, [], ["consts"], "consts")
        DMA(P, "sp", C.gains[:], gains_d, [], ["gains"], "gains")
        CP(P, "dve", C.consts_bf[:], C.consts[:], ["consts"], ["consts_bf"])
        for g0 in (8, 72):
            TS(P, C.gains[:, g0:g0 + 8], C.gains[:, g0:g0 + 8], 0.5, None, ALU.mult, None, ["gains"], ["gains"])

        with contextlib.ExitStack() as es:
            C.xT = es.enter_context(nc.sbuf_tensor(_nm("xT_sb"), [128, 8, S_OWN], F32))
            DMA(P, "sp", C.xT[:], xT_d.rearrange("(k p) t -> p k t", p=128), [], ["xT"], "xT")
            C.sq = Ring(nc, es, "sq", 1, [128, 8, 512], BF16)
            C.rs = Ring(nc, es, "rs", 2, [128, 512], F32)
            C.tmp = Ring(nc, es, "tmp", 2, [128, 512], F32)
            C.wring = Ring(nc, es, "wr", 3, [128, 5632], BF16)
            with contextlib.ExitStack() as es2:
                ffn(P, C, es2, nc, ffn1_in, ffn1_dn, 0, 8)
            P.barrier()
            if dbg:
                DMA(P, "sp", dbg_x1.rearrange("(k p) t -> p k t", p=128), C.xT[:], ["xT"], [], "dbg")
            if STOP_AFTER == 1:
                DMA(P, "sp", outT_d.rearrange("(k p) t -> p k t", p=128), C.xT[:], ["xT"], [], "out")
                P.wait_dma_at_end("out")
                if dbg:
                    P.wait_dma_at_end("dbg")
                P.emit()
                build_program.stats = P.stats
                return nc
            with contextlib.ExitStack() as es2:
                hT = es2.enter_context(nc.sbuf_tensor(_nm("mix_hT"), [128, 8, S_OWN], BF16))
                for tt in range(4):
                    sl = slice(tt * 512, (tt + 1) * 512)
                    norm_h(P, C, hT, "mix_hT", sl, sl, 16)
                for ck in range(4):
                    DMA(P, "sp", ag1_in[ck].rearrange("(k p) t -> p k t", p=128), hT[:, 2 * ck:2 * ck + 2, :],
                        ["mix_hT"], ["ag1_in"], "ag1_in")
                DMA(P, "sp", x_spill.rearrange("(k p) t -> p k t", p=128), C.xT[:], ["xT"], ["x_spill"], "x_spill")
                for ck in range(4):
                    P.add("pool", lambda e, ck=ck: e.collective_compute(
                        "AllGather", ALU.bypass, replica_groups=RG, ins=[ag1_in[ck].opt()], outs=[ag1_out[ck].opt()]),
                        ["ag1_in"], ["ag1_out"], dma="agc", inc=1)
                P.barrier()
        if STOP_AFTER == 2:
            DMA(P, "sp", outT_d, x_spill, [], [], "out")
            P.wait_dma_at_end("out")
            if dbg:
                P.wait_dma_at_end("dbg")
            P.emit()
            build_program.stats = P.stats
            return nc
        with contextlib.ExitStack() as es:
            mixers(P, C, es, nc, ag1_out, ag2_in, w_hg_d, w_fq_d, w_fk_d, w_fv_d, lbl_d, fbias_d, crow_d)
            for q in range(0 if not NOAG2 else 4, 4):
                for tq in range(4):
                    P.add("pool", lambda e, q=q, tq=tq: e.collective_compute(
                        "AllGather", ALU.bypass, replica_groups=RG, ins=[ag2_in[q][tq].opt()],
                        outs=[ag2_out[q][tq * 512:(tq + 1) * 512, :].opt()]),
                        ["ag2_in"], ["ag2_out"], dma="agc", inc=1)
            P.barrier()
        if dbg:
            for q in range(4):
                for tq in range(4):
                    DMA(P, "sp", dbg_o[q * 128:(q + 1) * 128, tq * 2048:(tq + 1) * 2048], ag2_in[q][tq], [], [], "dbg")
        if STOP_AFTER == 3:
            DMA(P, "sp", outT_d, x_spill, [], [], "out")
            P.wait_dma_at_end("out")
            if dbg:
                P.wait_dma_at_end("dbg")
            P.emit()
            build_program.stats = P.stats
            return nc
        with contextlib.ExitStack() as es:
            C.xT = es.enter_context(nc.sbuf_tensor(_nm("xTb"), [128, 8, S_OWN], F32))
            DMA(P, "sp", C.xT[:], x_spill.rearrange("(k p) t -> p k t", p=128), [], ["xT"], "xT2")
            C.sq = Ring(nc, es, "sqb", 1, [128, 8, 512], BF16)
            C.rs = Ring(nc, es, "rsb", 2, [128, 512], F32)
            C.tmp = Ring(nc, es, "tmpb", 2, [128, 512], F32)
            C.wring = Ring(nc, es, "wrb", 3, [128, 5632], BF16)
            with contextlib.ExitStack() as es2:
                post_mixer(P, C, es2, nc, ag2_out, w_loc_d, w_a_d, w_b_d, w_o_d)
            P.barrier()
            if dbg:
                DMA(P, "sp", dbg_x2.rearrange("(k p) t -> p k t", p=128), C.xT[:], ["xT"], [], "dbg")
            with contextlib.ExitStack() as es2:
                mem_attn(P, C, es2, nc, memT_d, w_mq_d, w_mkv_d, w_mo_d)
            P.barrier()
            if dbg:
                DMA(P, "sp", dbg_x3.rearrange("(k p) t -> p k t", p=128), C.xT[:], ["xT"], [], "dbg")
            with contextlib.ExitStack() as es2:
                ffn(P, C, es2, nc, ffn2_in, ffn2_dn, 64, 72)
            DMA(P, "sp", outT_d.rearrange("(k p) t -> p k t", p=128), C.xT[:], ["xT"], [], "out")
            P.wait_dma_at_end("out")
            if dbg:
                P.wait_dma_at_end("dbg")
        P.emit()
        build_program.stats = P.stats
    return nc


def stream_proj(P, C, w_d, c0, ncols_total, rhs, rhs_keys, epilogue, kt_n=8):
    done = 0
    i = 0
    while done < ncols_total:
        n = min(512, ncols_total - done)
        wt, wk = C.wring.next()
        wload(P, wt, wk, w_d, kt_n, 512, c0 + done, n, 0)
        wv = wview(wt, kt_n, 512)
        for jj in range(n // 128):
            ps, psk = C.ps.next()
            for kt in range(kt_n):
                MM(P, ps[:], wv[:, kt, jj * 128:(jj + 1) * 128], rhs[:, kt, :], kt == 0, kt == kt_n - 1,
                   [wk] + rhs_keys, [psk])
            epilogue(i, ps, psk)
            i += 1
        done += n


def post_mixer(P, C, es, nc, ag2_out, w_loc_d, w_a_d, w_b_d, w_o_d):
    sb = lambda n, s, d: es.enter_context(nc.sbuf_tensor(_nm(n), s, d))
    oa = sb("pm_oa", [128, 8, 512], F32)
    ob = sb("pm_ob", [128, 8, 512], BF16)
    hT = sb("pm_hT", [128, 8, 512], BF16)
    ga = sb("pm_ga", [128, 8, 512], F32)
    gB = sb("pm_gB", [128, 8, 512], F32)
    oan = sb("pm_oan", [128, 8, 512], BF16)
    yg = sb("pm_yg", [128, 8, 512], BF16)
    ya = ga
    zT = oa
    o_own = nc.dram_tensor("o_own", [2048, S_OWN], F32).ap()
    dst = o_own.rearrange("(a r q p) t -> a q r p t", a=2, r=4, q=2, p=128)
    for q in range(4):
        def ld(e, q=q):
            roff = _rank(e)
            return e.dma_start(out=dst[q // 2, q % 2],
                               in_=ag2_out[q][bass.ds(roff, 512), :].rearrange("(r p) t -> r p t", p=128))
        P.add("pool", ld, [], ["o_own"], dma="o_own")
    for tt in range(4):
        sl = slice(tt * 512, (tt + 1) * 512)

        DMA(P, "sp", oa[:], o_own[0:1024, sl].rearrange("(h p) t -> p h t", p=128), ["o_own"], ["pm_oa"], "pm_oa")
        DMA(P, "pool", ob[:], o_own[1024:2048, sl].rearrange("(h p) t -> p h t", p=128), ["o_own"], ["pm_ob"], "pm_ob")
        norm_h(P, C, hT, "pm_hT", slice(0, 512), sl, 16)

        def ep_ga(i, ps, psk):
            ACT(P, ga[:, i, :], ps[:], AF.Silu, [psk], ["pm_ga"])
        stream_proj(P, C, w_loc_d, 0, 1024, hT, ["pm_hT"], ep_ga)
        r, rk = rms_rstd(P, C, oa, "pm_oa", slice(0, 512))
        for kt in range(8):
            t, tk = C.tmp.next()
            STT(P, t[:], oa[:, kt, :], C.gains[:, 24 + kt:25 + kt], r[:], ALU.mult, ALU.mult,
                ["pm_oa", rk, "gains"], [tk])
            TT(P, oan[:, kt, :], t[:], ga[:, kt, :], ALU.mult, [tk, "pm_ga"], ["pm_oan"])

        def ep_a(i, ps, psk):
            ACT(P, ya[:, i, :], ps[:], AF.Copy, [psk], ["pm_ga"])
        stream_proj(P, C, w_a_d, 0, 1024, oan, ["pm_oan"], ep_a)

        def ep_gA(i, ps, psk):
            t, tk = C.tmp.next()
            ACT(P, t[:], ps[:], AF.Sigmoid, [psk, "gains"], [tk], bias=C.gains[:, 80 + i:81 + i])
            TT(P, ya[:, i, :], ya[:, i, :], t[:], ALU.mult, [tk, "pm_ga"], ["pm_ga"])
        stream_proj(P, C, w_loc_d, 1024, 1024, hT, ["pm_hT"], ep_gA)

        def ep_gB(i, ps, psk):
            ACT(P, gB[:, i, :], ps[:], AF.Sigmoid, [psk, "gains"], ["pm_gB"], bias=C.gains[:, 88 + i:89 + i])
        stream_proj(P, C, w_loc_d, 2048, 1024, hT, ["pm_hT"], ep_gB)

        def ep_b(i, ps, psk):
            t, tk = C.tmp.next()
            TT(P, t[:], ps[:], gB[:, i, :], ALU.mult, [psk, "pm_gB"], [tk])
            TT(P, yg[:, i, :], t[:], ya[:, i, :], ALU.add, [tk, "pm_ga"], ["pm_yg"])
        stream_proj(P, C, w_b_d, 0, 1024, ob, ["pm_ob"], ep_b)

        def ep_o(i, ps, psk):
            ACT(P, zT[:, i, :], ps[:], AF.Copy, [psk], ["pm_oa"])
        stream_proj(P, C, w_o_d, 0, 1024, yg, ["pm_yg"], ep_o)
        post_norm_add(P, C, zT, "pm_oa", slice(0, 512), sl, 32)


def mem_attn(P, C, es, nc, memT_d, w_mq_d, w_mkv_d, w_mo_d):
    sb = lambda n, s, d: es.enter_context(nc.sbuf_tensor(_nm(n), s, d))
    memT = sb("ma_memT", [128, 8, 256], F32)
    memn = sb("ma_memn", [128, 8, 256], BF16)
    kmT = sb("ma_kmT", [128, 8, 256], BF16)
    vm = sb("ma_vm", [128, 2, 1024], BF16)
    hT = sb("ma_hT", [128, 8, 512], BF16)
    qmT = sb("ma_qmT", [128, 8, 512], BF16)
    omT = sb("ma_omT", [128, 8, 512], BF16)
    yT = sb("ma_yT", [128, 8, 512], F32)
    pT = Ring(nc, es, "ma_pT", 2, [128, 2, 512], BF16)
    rec = Ring(nc, es, "ma_rec", 2, [128, 512], F32)
    DMA(P, "sp", memT[:], memT_d.rearrange("(k p) t -> p k t", p=128), [], ["ma_memT"], "ma_memT")
    sq, sqk = C.sq.next()
    ACT(P, sq[:, :, 0:256], memT[:], AF.Square, ["ma_memT"], [sqk])
    ps, psk = C.ps.next()
    for kt in range(8):
        MM(P, ps[:, 0:256], C.ones_bf[:], sq[:, kt, 0:256], kt == 0, kt == 7, [sqk, "consts_bf"], [psk])
    r, rk = C.rs.next()
    TS(P, r[:, 0:256], ps[:, 0:256], 1.0 / 1024.0, EPS, ALU.mult, ALU.add, [psk], [rk])
    ACT(P, r[:, 0:256], r[:, 0:256], AF.Sqrt, [rk], [rk])
    RECIP(P, r[:, 0:256], r[:, 0:256], [rk], [rk])
    for kt in range(8):
        STT(P, memn[:, kt, :], memT[:, kt, :], C.gains[:, 48 + kt:49 + kt], r[:, 0:256], ALU.mult, ALU.mult,
            ["ma_memT", rk, "gains"], ["ma_memn"])
    for ch in range(2):
        wt, wk = C.wring.next()
        wload(P, wt, wk, w_mkv_d, 8, 512, ch * 512, 512, 0)
        wv = wview(wt, 8, 512)
        for jj in range(4):
            ps, psk = C.ps.next()
            for kt in range(8):
                MM(P, ps[:, 0:256], wv[:, kt, jj * 128:(jj + 1) * 128], memn[:, kt, :], kt == 0, kt == 7,
                   [wk, "ma_memn"], [psk])
            ACT(P, kmT[:, ch * 4 + jj, :], ps[:, 0:256], AF.Copy, [psk], ["ma_kmT"])
    for ch in range(2):
        wt, wk = C.wring.next()
        wload(P, wt, wk, w_mkv_d, 8, 512, 1024 + ch * 512, 512, 0)
        wv = wview(wt, 8, 512)
        for mt in range(2):
            ps, psk = C.ps.next()
            for kt in range(8):
                MM(P, ps[:], memn[:, kt, mt * 128:(mt + 1) * 128], wv[:, kt, :], kt == 0, kt == 7,
                   [wk, "ma_memn"], [psk])
            ACT(P, vm[:, mt, ch * 512:(ch + 1) * 512], ps[:], AF.Copy, [psk], ["ma_vm"])
    scale = 1.0 / math.sqrt(256.0)
    for tt in range(4):
        sl = slice(tt * 512, (tt + 1) * 512)
        norm_h(P, C, hT, "ma_hT", slice(0, 512), sl, 40)

        def ep_q(i, ps, psk):
            ACT(P, qmT[:, i, :], ps[:], AF.Copy, [psk], ["ma_qmT"])
        stream_proj(P, C, w_mq_d, 0, 1024, hT, ["ma_hT"], ep_q)
        for hh in range(4):
            p_t, pk = pT.next()
            for mt in range(2):
                ps, psk = C.ps.next()
                for a in range(2):
                    MM(P, ps[:], kmT[:, 2 * hh + a, mt * 128:(mt + 1) * 128], qmT[:, 2 * hh + a, :],
                       a == 0, a == 1, ["ma_kmT", "ma_qmT"], [psk])
                ACT(P, p_t[:, mt, :], ps[:], AF.Exp, [psk], [pk], scale=scale)
            ps, psk = C.ps.next()
            for mt in range(2):
                MM(P, ps[:], C.ones_bf[:], p_t[:, mt, :], mt == 0, mt == 1, [pk, "consts_bf"], [psk])
            rc, rck = rec.next()
            RECIP(P, rc[:], ps[:], [psk], [rck])
            for a in range(2):
                ps, psk = C.ps.next()
                for mt in range(2):
                    MM(P, ps[:], vm[:, mt, (2 * hh + a) * 128:(2 * hh + a + 1) * 128], p_t[:, mt, :],
                       mt == 0, mt == 1, [pk, "ma_vm"], [psk])
                TT(P, omT[:, 2 * hh + a, :], ps[:], rc[:], ALU.mult, [psk, rck], ["ma_omT"])

        def ep_o(i, ps, psk):
            ACT(P, yT[:, i, :], ps[:], AF.Copy, [psk], ["ma_yT"])
        stream_proj(P, C, w_mo_d, 0, 1024, omT, ["ma_omT"], ep_o)
        post_norm_add(P, C, yT, "ma_yT", slice(0, 512), sl, 56)


def mixers(P, C, es, nc, ag1_out, ag2_in, w_hg_d, w_fq_d, w_fk_d, w_fv_d, lbl_d, fbias_d, crow_d):
    sb = lambda n, s, d: es.enter_context(nc.sbuf_tensor(_nm(n), s, d))
    cf = C.consts
    U_f = cf[:, C_U:C_U + 128]
    BO_f = cf[:, C_BO:C_BO + 128]
    UF_f = cf[:, C_UF:C_UF + 128]
    ONE_f = cf[:, C_ONE:C_ONE + 128]
    SEL_f = cf[:, C_SEL:C_SEL + 2]
    SL_f = cf[0:64, C_SL:C_SL + 64]
    MD_bf = C.consts_bf[:, C_MD:C_MD + 512]
    whg = sb("mx_whg", [128, 8, 768], BF16)
    wfq = sb("mx_wfq", [128, 8, 256], BF16)
    wfk = sb("mx_wfk", [128, 8, 256], BF16)
    wfv = sb("mx_wfv", [128, 8, 258], BF16)
    for t, d_, n in ((whg, w_hg_d, "mx_whg"), (wfq, w_fq_d, "mx_wfq"), (wfk, w_fk_d, "mx_wfk"), (wfv, w_fv_d, "mx_wfv")):
        DMA(P, "pool", t[:], d_.rearrange("(k p) c -> p k c", p=128), [], [n], n)
    lbl = sb("mx_lbl", [128, 512], F32)
    fbias = sb("mx_fbias", [128, 2], F32)
    DMA(P, "sp", lbl[:], lbl_d, [], ["mx_lbl"], "mx_lbl")
    DMA(P, "sp", fbias[:], fbias_d, [], ["mx_fbias"], "mx_fbias")
    lb = sb("mx_lb", [128, 256], F32)
    oml = sb("mx_oml", [128, 256], F32)
    TT(P, lb[:], lbl[:, 0:256], lbl[:, 256:512], ALU.subtract, ["mx_lbl"], ["mx_lb"])
    ACT(P, lb[:], lb[:], AF.Sigmoid, ["mx_lb"], ["mx_lb"])
    TS(P, oml[:], lb[:], -1.0, 1.0, ALU.mult, ALU.add, ["mx_lb"], ["mx_oml"])
    lbw = sb("mx_lbw", [128, 4, 128], F32)
    omlw = sb("mx_omlw", [128, 4, 128], F32)
    U4 = sb("mx_U4", [128, 4, 128], F32)
    for s_ in range(4):
        CP(P, "dve", U4[:, s_, :], U_f, ["consts"], ["mx_U4"])
    qT = sb("mx_qT", [128, S_ALL], BF16)
    kT = sb("mx_kT", [128, S_ALL], BF16)
    Vp = sb("mx_Vp", [128, 64, 130], BF16)
    fb = sb("mx_fb", [128, 64], F32)
    MSET(P, "dve", Vp[:, :, 128:130], 1.0, ["mx_Vp"])
    hring = Ring(nc, es, "mx_h", 2, [128, 8, 512], BF16)
    f32r = Ring(nc, es, "mx_f", 11, [128, 4, 128], F32)
    bfr = Ring(nc, es, "mx_b", 11, [128, 4, 128], BF16)
    S32 = sb("mx_S32", [128, 128], F32)
    Sbf = Ring(nc, es, "mx_Sbf", 2, [128, 128], BF16)
    dT = Ring(nc, es, "mx_dT", 2, [128, 8], F32)
    ostage = Ring(nc, es, "mx_ost", 2, [128, 512], F32)
    crow = sb("mx_crow", [96, S_ALL], BF16)
    biasG = sb("mx_biasG", [128, 16, 64], F32)
    ccol = sb("mx_ccol", [128, 64], F32)
    offbc = sb("mx_offbc", [128, 64], F32)
    ls = sb("mx_ls", [128, 64], F32)
    totbc = sb("mx_totbc", [64, 128], F32)
    cd = sb("mx_cd", [128, 64], F32)
    c3 = sb("mx_c3", [128, 3, 64], BF16)
    c3T = sb("mx_c3T", [64, 3, 128], BF16)
    r1 = sb("mx_r1", [128, 64], F32)
    l3 = sb("mx_l3", [128, 3, 64], BF16)
    tb3 = sb("mx_tb3", [64, 3, 128], BF16)
    tbr = sb("mx_tbr", [64, 128], F32)
    pTr = Ring(nc, es, "mx_pT", 3, [128, 512], BF16)
    obr = Ring(nc, es, "mx_ob", 2, [128, 128], BF16)
    recr = Ring(nc, es, "mx_rec", 2, [128, 1], F32)
    obT = Ring(nc, es, "mx_obT", 2, [128, 512], F32)
    MSET(P, "dve", crow[:], 0.0, ["mx_crow"])
    ones96 = C.consts_bf[0:96, C_ONE:C_ONE + 128]
    psb = C.ps.bufs
    PS = lambda i: psb[i]
    scale = 1.0 / math.sqrt(128.0)

    for h in range(2):
        for s_ in range(4):
            CP(P, "dve", lbw[:, s_, :], lb[:, h * 128:(h + 1) * 128], ["mx_lb"], ["mx_lbw"])
            CP(P, "dve", omlw[:, s_, :], oml[:, h * 128:(h + 1) * 128], ["mx_oml"], ["mx_omlw"])
        MSET(P, "dve", S32[:], 0.0, ["mx_S32"])
        sb_t, sb_k = Sbf.next()
        MSET(P, "dve", sb_t[:], 0.0, [sb_k])
        for T in range(16):
            rr, t0 = T // 4, (T % 4) * 512
            ht, hk = hring.next()
            for ck in range(4):
                srch = ag1_out[ck][rr * 256:(rr + 1) * 256, t0:t0 + 512].rearrange("(k p) t -> p k t", p=128)
                DMA(P, "sp", ht[:, 2 * ck:2 * ck + 2, :], srch, ["ag1_out"], [hk], "h_" + hk)
            if not (MIX_PARTS & 1):
                continue
            for (w_, dst, nm, wn) in ((wfq, qT, "mx_qT", "mx_wfq"), (wfk, kT, "mx_kT", "mx_wfk")):
                ps, psk = PS(6)
                for kt in range(8):
                    MM(P, ps[:], w_[:, kt, h * 128:(h + 1) * 128], ht[:, kt, :], kt == 0, kt == 7,
                       [wn, hk], [psk])
                ACT(P, dst[:, T * 512:(T + 1) * 512], ps[:], AF.Copy, [psk], [nm])
            ps, psk = PS(7)
            for s_ in range(4):
                for kt in range(8):
                    MM(P, ps[:, s_ * 128:s_ * 128 + 128], ht[:, kt, s_ * 128:(s_ + 1) * 128],
                       wfv[:, kt, h * 129:h * 129 + 128], kt == 0, kt == 7, ["mx_wfv", hk], [psk])
            ACT(P, Vp[:, T * 4:(T + 1) * 4, 0:128], ps[:].rearrange("p (s c) -> p s c", s=4), AF.Copy,
                [psk], ["mx_Vp"])
            ps, psk = PS(7)
            for s_ in range(4):
                for kt in range(8):
                    MM(P, ps[:, s_:s_ + 1], ht[:, kt, s_ * 128:(s_ + 1) * 128],
                       wfv[:, kt, h * 129 + 128:h * 129 + 129], kt == 0, kt == 7, ["mx_wfv", hk], [psk])
            ACT(P, fb[:, T * 4:(T + 1) * 4], ps[:, 0:4], AF.Copy, [psk], ["mx_fb"])
            if not (MIX_PARTS & 2):
                continue
            zq, zqk = PS(0)
            zf, zfk = PS(1)
            zi, zik = PS(2)
            for (z_, zk_, off) in ((zq, zqk, 0), (zf, zfk, 128), (zi, zik, 256)):
                for s_ in range(4):
                    for kt in range(8):
                        MM(P, z_[:, s_ * 128:(s_ + 1) * 128], ht[:, kt, s_ * 128:(s_ + 1) * 128],
                           whg[:, kt, h * 384 + off:h * 384 + off + 128], kt == 0, kt == 7,
                           ["mx_whg", hk], [zk_])
            qs, qsk = f32r.next()
            sg, sgk = f32r.next()
            Ib, Ibk = bfr.next()
            W = lambda t: t[:].rearrange("p s c -> p (s c)")
            ACT(P, W(qs), zq[:], AF.Silu, [zqk], [qsk])
            ACT(P, W(sg), zf[:], AF.Sigmoid, [zfk], [sgk])
            ACT(P, W(Ib), zi[:], AF.Copy, [zik], [Ibk])
            f_, fk_ = f32r.next()
            TT(P, W(f_), W(sg), W(omlw), ALU.mult, [sgk, "mx_omlw"], [fk_])
            TT(P, W(f_), W(f_), W(lbw), ALU.add, [fk_, "mx_lbw"], [fk_])
            lf, lfk = f32r.next()
            ACT(P, W(lf), W(f_), AF.Ln, [fk_], [lfk])
            kk, kkk = f32r.next()
            TS(P, W(kk), W(f_), -1.0, 1.0, ALU.mult, ALU.add, [fk_], [kkk])
            lfh, lfhk = bfr.next()
            lfl, lflk = bfr.next()
            lfr, lfrk = f32r.next()
            CP(P, "dve", W(lfh), W(lf), [lfk], [lfhk])
            TT(P, W(lfr), W(lf), W(lfh), ALU.subtract, [lfk, lfhk], [lfrk])
            CP(P, "dve", W(lfl), W(lfr), [lfrk], [lflk])
            U_b = C.consts_bf[:, C_U:C_U + 128]
            BO_b = C.consts_bf[:, C_BO:C_BO + 128]
            SEL_b = C.consts_bf[:, C_SEL:C_SEL + 2]
            bps, bpsk = PS(3)
            MM(P, bps[:], U_b, W(lfh), True, False, [lfhk, "consts_bf"], [bpsk])
            MM(P, bps[:], U_b, W(lfl), False, True, [lflk, "consts_bf"], [bpsk])
            blps, blpsk = PS(5)
            MM(P, blps[:], BO_b, W(lfh), True, False, [lfhk, "consts_bf"], [blpsk])
            MM(P, blps[:], BO_b, W(lfl), False, True, [lflk, "consts_bf"], [blpsk])
            btps, btpsk = PS(6)
            for s_ in range(4):
                MM(P, btps[:, 2 * s_:2 * s_ + 2], lfh[:, s_, :], SEL_b, True, False, [lfhk, "consts_bf"], [btpsk])
                MM(P, btps[:, 2 * s_:2 * s_ + 2], lfl[:, s_, :], SEL_b, False, True, [lflk, "consts_bf"], [btpsk])
            eb, ebk = f32r.next()
            enb, enbk = f32r.next()
            ebl, eblk = f32r.next()
            d_t, dk_ = dT.next()
            ACT(P, W(eb), bps[:], AF.Exp, [bpsk], [ebk])
            ACT(P, W(enb), bps[:], AF.Exp, [bpsk], [enbk], scale=-1.0)
            ACT(P, W(ebl), blps[:], AF.Exp, [blpsk], [eblk])
            ACT(P, d_t[:], btps[:, 0:8], AF.Exp, [btpsk], [dk_])
            Qt, Qtk = bfr.next()
            Kt32, Kt32k = f32r.next()
            Ktb, Ktbk = bfr.next()
            Kh0, Kh0k = bfr.next()
            Kh1, Kh1k = bfr.next()
            TT(P, W(Qt), W(qs), W(eb), ALU.mult, [qsk, ebk], [Qtk])
            TT(P, W(Kt32), W(kk), W(enb), ALU.mult, [kkk, enbk], [Kt32k])
            CP(P, "dve", W(Ktb), W(Kt32), [Kt32k], [Ktbk])
            STT(P, W(Kh0), W(Kt32), SEL_f[:, 0:1], W(ebl), ALU.mult, ALU.mult, [Kt32k, eblk, "consts"], [Kh0k])
            STT(P, W(Kh1), W(Kt32), SEL_f[:, 1:2], W(ebl), ALU.mult, ALU.mult, [Kt32k, eblk, "consts"], [Kh1k])
            tp, tpk = PS(0)
            tpb = tp.bitcast(BF16)
            tp2, tp2k = PS(7)
            tpb2 = tp2.bitcast(BF16)
            for s_ in range(4):
                TR(P, tpb[:, s_ * 128:(s_ + 1) * 128], Qt[:, s_, :], C.ident_bf, [Qtk, "consts_bf"], [tpk])
                TR(P, tpb2[:, s_ * 128:(s_ + 1) * 128], Ktb[:, s_, :], C.ident_bf,
                   [Ktbk, "consts_bf"], [tp2k])
            QtT, QtTk = bfr.next()
            KtT, KtTk = bfr.next()
            ACT(P, W(QtT), tpb[:, 0:512], AF.Copy, [tpk], [QtTk])
            CP(P, "dve", W(KtT), tpb2[:, 0:512], [tp2k], [KtTk])
            aps, apsk = PS(1)
            for s_ in range(4):
                MM(P, aps[:, s_ * 128:(s_ + 1) * 128], KtT[:, s_, :], QtT[:, s_, :], True, True,
                   [KtTk, QtTk], [apsk])
            Am, Amk = bfr.next()
            TT(P, W(Am), aps[:], W(U4), ALU.mult, [apsk, "mx_U4"], [Amk])
            ops_, opsk = PS(4)
            for s_ in range(4):
                MM(P, ops_[:, s_ * 128:(s_ + 1) * 128], Ib[:, s_, :], Am[:, s_, :], True, False,
                   [Ibk, Amk], [opsk])
                for c_ in range(2):
                    col = s_ * 128 + c_ * 64
                    MM(P, ops_[:, col:col + 64], sb_t[:], QtT[:, s_, c_ * 64:(c_ + 1) * 64], False,
                       c_ == 1, [sb_k, QtTk], [opsk])
                    sups, supsk = PS(5)
                    Khc, Khck = (Kh0, Kh0k) if c_ == 0 else (Kh1, Kh1k)
                    MM(P, sups[:, 0:128], Khc[:, s_, :], Ib[:, s_, :], True, True, [Khck, Ibk], [supsk])
                    STT(P, S32[:], S32[:], d_t[:, 2 * s_ + c_:2 * s_ + c_ + 1], sups[:, 0:128],
                        ALU.mult, ALU.add, ["mx_S32", dk_, supsk], ["mx_S32"])
                    sb_t, sb_k = Sbf.next()
                    ACT(P, sb_t[:], S32[:], AF.Copy, ["mx_S32"], [sb_k])
            ot, otk = ostage.next()
            ACT(P, ot[:], ops_[:], AF.Copy, [opsk], [otk])
            DMA(P, "sp", ag2_in[h][T // 4][:, (T % 4) * 512:(T % 4 + 1) * 512], ot[:], [otk], ["ag2_in"],
                "o_" + otk)
        P.barrier()
        if not (MIX_PARTS & 4):
            continue
        ACT(P, ls[:], fb[:], AF.Sigmoid, ["mx_fb", "mx_fbias"], ["mx_ls"], bias=fbias[:, h:h + 1])
        ACT(P, ls[:], ls[:], AF.Ln, ["mx_ls"], ["mx_ls"])
        def split3(dst, src, tmp, skey, dkey, tkey):
            CP(P, "dve", dst[:, 0, :], src, [skey], [dkey])
            TT(P, tmp, src, dst[:, 0, :], ALU.subtract, [skey, dkey], [tkey])
            CP(P, "dve", dst[:, 1, :], tmp, [tkey], [dkey])
            TT(P, tmp, tmp, dst[:, 1, :], ALU.subtract, [tkey, dkey], [tkey])
            CP(P, "dve", dst[:, 2, :], tmp, [tkey], [dkey])
        split3(l3, ls[:], r1[:], "mx_ls", "mx_l3", "mx_r1")
        UF_b = C.consts_bf[:, C_UF:C_UF + 128]
        SL_b = C.consts_bf[0:64, C_SL:C_SL + 64]
        ps, psk = PS(0)
        for a_ in range(3):
            MM(P, ps[0:64, 0:128], l3[:, a_, :], C.ones_bf, a_ == 0, a_ == 2, ["mx_l3", "consts_bf"], [psk])
        CP(P, "dve", totbc[:], ps[0:64, 0:128], [psk], ["mx_totbc"])
        split3(tb3, totbc[:], tbr[:], "mx_totbc", "mx_tb3", "mx_tbr")
        ps, psk = PS(1)
        for a_ in range(3):
            MM(P, ps[:, 0:64], tb3[:, a_, :], SL_b, a_ == 0, a_ == 2, ["mx_tb3", "consts_bf"], [psk])
        CP(P, "dve", offbc[:], ps[:, 0:64], [psk], ["mx_offbc"])
        ps2, ps2k = PS(2)
        for a_ in range(3):
            MM(P, ps2[:, 0:64], UF_b, l3[:, a_, :], a_ == 0, a_ == 2, ["mx_l3", "consts_bf"], [ps2k])
        TT(P, ccol[:], ps2[:, 0:64], offbc[:], ALU.add, [ps2k, "mx_offbc"], ["mx_ccol"])
        for G in range(16):
            nj = 4 * G + 4
            cref = offbc[:, 4 * G:4 * G + 1]
            TS(P, biasG[:, G, 0:nj], ccol[:, 0:nj], cref, -1.0, ALU.subtract, ALU.mult,
               ["mx_ccol", "mx_offbc"], ["mx_biasG"])
            TS(P, cd[:, 4 * G:4 * G + 4], ccol[:, 4 * G:4 * G + 4], cref, 1.0 / scale, ALU.subtract, ALU.mult,
               ["mx_ccol", "mx_offbc"], ["mx_cd"])
        CP(P, "dve", c3[:, 0, :], cd[:], ["mx_cd"], ["mx_c3"])
        TT(P, r1[:], cd[:], c3[:, 0, :], ALU.subtract, ["mx_cd", "mx_c3"], ["mx_r1"])
        CP(P, "dve", c3[:, 1, :], r1[:], ["mx_r1"], ["mx_c3"])
        TT(P, r1[:], r1[:], c3[:, 1, :], ALU.subtract, ["mx_r1", "mx_c3"], ["mx_r1"])
        CP(P, "dve", c3[:, 2, :], r1[:], ["mx_r1"], ["mx_c3"])
        tp, tpk = PS(3)
        tpb = tp.bitcast(BF16)
        for a in range(3):
            TR(P, tpb[0:64, a * 128:(a + 1) * 128], c3[:, a, :], C.ident_bf, ["mx_c3", "consts_bf"], [tpk])
        CP(P, "dve", c3T[:].rearrange("p a c -> p (a c)"), tpb[0:64, 0:384], [tpk], ["mx_c3T"])
        DMA(P, "sp", crow_d.rearrange("a (j t) -> j a t", j=64), c3T[:], ["mx_c3T"], ["crow_d"], "crow_d")
        for a in range(3):
            DMA(P, "sp", crow[32 * a:32 * a + 1, :], crow_d[a:a + 1, :], ["crow_d"], ["mx_crow"], "mx_crow")
        for G in range(16):
            q0 = G * 512
            o_ps = [PS(4), PS(5), PS(0), PS(1)]
            for j in range(4 * G + 4):
                i_ = j - 4 * G
                c0 = 0 if i_ < 0 else 128 * i_
                n = 512 - c0
                st, stk = PS(6 + (j % 2))
                MM(P, st[:, 0:n], kT[:, j * 128:(j + 1) * 128], qT[:, q0 + c0:q0 + 512], True, False,
                   ["mx_kT", "mx_qT"], [stk])
                MM(P, st[:, 0:n], ones96, crow[:, q0 + c0:q0 + 512], False, i_ < 0, ["mx_crow", "consts_bf"], [stk])
                if i_ >= 0:
                    MM(P, st[:, 0:n], C.ident_bf, MD_bf[:, 0:n], False, True, ["consts_bf"], [stk])
                p_t, pk = pTr.next()
                ACT(P, p_t[:, 0:n], st[:, 0:n], AF.Exp, [stk, "mx_biasG"], [pk], bias=biasG[:, G, j:j + 1], scale=scale)
                for qb in range(4):
                    if 128 * qb < c0:
                        continue
                    op_t, op_k = o_ps[qb]
                    oc = 0
                    MM(P, op_t[:, oc:oc + 129], p_t[:, 128 * qb - c0:128 * qb - c0 + 128], Vp[:, j, 0:129],
                       j == 0, j == 4 * G + qb, [pk, "mx_Vp"], [op_k])
                if i_ >= 0:
                    qb = i_
                    op_t, op_k = o_ps[qb]
                    oc = 0
                    okey = op_k
                    rc, rck = recr.next()
                    RECIP(P, rc[:], op_t[:, oc + 128:oc + 129], [okey], [rck])
                    ob_, obk = obr.next()
                    ACT(P, ob_[:], op_t[:, oc:oc + 128], AF.Identity, [okey, rck], [obk], scale=rc[:])
                    tp, tpk = PS(3)
                    tpb = tp.bitcast(BF16)
                    TR(P, tpb[:, 0:128], ob_[:], C.ident_bf, [obk, "consts_bf"], [tpk])
                    if qb == 0:
                        obT_t, obT_k = obT.next()
                    CP(P, "dve", obT_t[:, qb * 128:(qb + 1) * 128], tpb[:, 0:128], [tpk], [obT_k])
                    if qb == 3:
                        DMA(P, "sp", ag2_in[2 + h][G // 4][:, (G % 4) * 512:(G % 4 + 1) * 512], obT_t[:],
                            [obT_k], ["ag2_in"], "o_" + obT_k)
        P.barrier()


_CACHE = {}


def _prep_inputs(inp):
    f = lambda a: np.ascontiguousarray(np.asarray(a, dtype=np.float32))
    x = f(inp["x"])
    mem = f(inp["mem"])
    w_in = f(inp["w_in"])[0]
    offs = [0, 1024, 2048, 3072, 4096, 5120, 6144, 7168, 7176, 9224]
    q_a, f_a, i_a, g_a, q_b, k_b, v_b, f_b, gates = [w_in[:, offs[i]:offs[i + 1]] for i in range(9)]
    gains = np.zeros((128, 96), np.float32)
    for i, n in enumerate(GAIN_NAMES):
        gains[:, 8 * i:8 * i + 8] = f(inp[n])[0].reshape(8, 128).T
    gains[:, 80:96] = f(inp["b_gate"])[0].reshape(16, 128).T
    consts = make_consts()
    w_loc = np.ascontiguousarray(np.concatenate([g_a, gates], axis=1))
    lbl_all = f(inp["hg_lb_logits"])
    fbias_all = f(inp["fox_f_bias"])[0]
    shared = {
        "ffn1_w_in": f(inp["ffn1_w_in"])[0], "ffn1_w_down": f(inp["ffn1_w_down"])[0],
        "ffn2_w_in": f(inp["ffn2_w_in"])[0], "ffn2_w_down": f(inp["ffn2_w_down"])[0],
        "w_loc": w_loc, "w_branch_a": f(inp["w_branch_a"])[0], "w_branch_b": f(inp["w_branch_b"])[0],
        "w_out": f(inp["w_out"])[0], "w_mq": f(inp["w_mq"])[0], "w_mkv": f(inp["w_mkv"])[0],
        "w_mo": f(inp["w_mo"])[0], "gains": gains, "consts": consts,
    }
    maps = []
    for c in range(8):
        b, r = c // 4, c % 4
        hs = [2 * r, 2 * r + 1]
        m = dict(shared)
        m["xT"] = np.ascontiguousarray(x[b, r * S_OWN:(r + 1) * S_OWN, :].T)
        m["memT"] = np.ascontiguousarray(mem[b].T)
        sl = lambda w, h: w[:, h * 128:(h + 1) * 128]
        m["w_hg"] = np.ascontiguousarray(np.concatenate(
            [np.concatenate([sl(q_a, h), sl(f_a, h), sl(i_a, h)], axis=1) for h in hs], axis=1))
        m["w_fq"] = np.ascontiguousarray(np.concatenate([sl(q_b, h) for h in hs], axis=1))
        m["w_fk"] = np.ascontiguousarray(np.concatenate([sl(k_b, h) for h in hs], axis=1))
        m["w_fv"] = np.ascontiguousarray(np.concatenate(
            [np.concatenate([sl(v_b, h), f_b[:, h:h + 1]], axis=1) for h in hs], axis=1))
        l0 = np.concatenate([lbl_all[0, h] for h in hs])
        l1 = np.concatenate([lbl_all[1, h] for h in hs])
        m["lbl"] = np.ascontiguousarray(np.tile(np.concatenate([l0, l1])[None, :], (128, 1)))
        m["fbias"] = np.ascontiguousarray(np.tile(fbias_all[hs][None, :], (128, 1)))
        maps.append(m)
    return maps


def kernel(**inputs):
    dbg = bool(inputs.pop("_dbg", False))
    key = ("nc", dbg)
    if key not in _CACHE:
        _CACHE[key] = build_program(dbg)
    nc = _CACHE[key]
    maps = _prep_inputs(inputs)
    res = run_bass_kernel_spmd(nc, maps, core_ids=list(range(8)))
    out = np.zeros((2, S_ALL, D), np.float32)
    for c in range(8):
        b, r = c // 4, c % 4
        out[b, r * S_OWN:(r + 1) * S_OWN, :] = np.asarray(res.results[c]["outT"]).T
    if dbg:
        return out, res.results
    return out
```

```python
import contextlib
import math
import numpy as np
import concourse.bass as bass
import concourse.mybir as mybir
from concourse.bass_utils import run_bass_kernel_spmd

F32 = mybir.dt.float32
BF16 = mybir.dt.bfloat16
AF = mybir.ActivationFunctionType
ALU = mybir.AluOpType

SAME_ENGINE_SYNC = False
STOP_AFTER = 9
MIX_PARTS = 7
NOAG2 = False
EPS = 1e-6
D = 1024
S_OWN = 2048
S_ALL = 8192
DFF = 2816
NEG = -1.0e9


class Op:
    __slots__ = ("eng", "fn", "deps", "is_dma", "semkey", "idx", "signal", "count",
                 "dma_waits", "inc", "seg")


class Prog:
    def __init__(self, nc, es):
        self.nc = nc
        self.es = es
        self.ops = []
        self.lastw = {}
        self.readers = {}
        self.dma_tot = {}
        self.final_waits = {}
        self.last_on = {}
        self.seg = 0

    def add(self, eng, fn, reads=(), writes=(), dma=None, inc=16):
        if dma is not None:
            dma = "cc" if inc == 1 else ("q_" + eng)
        deps = {}
        for r in reads:
            w = self.lastw.get(r)
            if w is not None:
                deps[w.idx] = w
        for r in writes:
            w = self.lastw.get(r)
            if w is not None:
                deps[w.idx] = w
            for rd in self.readers.get(r, {}).values():
                deps[rd.idx] = rd
        op = Op()
        op.eng = eng
        op.fn = fn
        op.is_dma = dma is not None
        op.semkey = dma
        op.inc = inc
        op.idx = len(self.ops)
        op.signal = False
        op.count = 0
        op.seg = self.seg
        dma_waits = {}
        cdeps = []
        for d in deps.values():
            if d.is_dma:
                dma_waits[d.semkey] = self.dma_tot[d.semkey]
            elif d.fn is not None:
                cdeps.append(d)
        op.deps = cdeps
        op.dma_waits = dma_waits
        if dma is not None:
            self.dma_tot[dma] = self.dma_tot.get(dma, 0) + inc
        for r in reads:
            self.readers.setdefault(r, {})[(eng, op.idx) if op.is_dma else eng] = op
        for r in writes:
            self.lastw[r] = op
            self.readers[r] = {}
        self.ops.append(op)
        if not op.is_dma and fn is not None:
            self.last_on[eng] = op
        return op

    def barrier(self):
        lasts = list(self.last_on.values())
        tot = dict(self.dma_tot)
        for e in ["pe", "act", "dve", "pool", "sp"]:
            op = Op()
            op.eng = e
            op.fn = None
            op.is_dma = False
            op.semkey = None
            op.inc = 0
            op.idx = len(self.ops)
            op.signal = False
            op.count = 0
            op.deps = [d for d in lasts if d.eng != e]
            op.dma_waits = dict(tot)
            op.seg = self.seg
            self.ops.append(op)
        self.lastw = {}
        self.readers = {}
        self.last_on = {}
        self.seg += 1

    def wait_dma_at_end(self, key):
        for k in self.dma_tot:
            self.final_waits[k] = self.dma_tot[k]

    def emit(self):
        nc = self.nc
        engs = ["pe", "act", "dve", "pool", "sp"]
        for op in self.ops:
            for d in op.deps:
                if d.eng == op.eng and (op.eng == "pe" or not SAME_ENGINE_SYNC):
                    continue
                d.signal = True
        cnt = {}
        for op in self.ops:
            if op.is_dma or op.fn is None:
                continue
            if op.signal:
                k_ = (0, op.eng)
                cnt[k_] = cnt.get(k_, 0) + 1
                op.count = cnt[k_]
        sems = {k_: self.es.enter_context(nc.semaphore("s_%d_%s" % k_)) for k_ in cnt}
        dsems = {}
        for k in self.dma_tot:
            dsems[k] = self.es.enter_context(nc.semaphore("d_" + str(len(dsems))))
        per_eng = {e: [o for o in self.ops if o.eng == e] for e in engs}
        self.stats = {e: len(per_eng[e]) for e in engs}
        self.stats["sems"] = len(dsems) + len(sems)
        self.stats["counts"] = max(cnt.values()) if cnt else 0
        final_waits = self.final_waits

        def run(e, engine):
            waited = {}
            dwaited = {}
            for op in per_eng[e]:
                need = {}
                for d in op.deps:
                    if d.eng == e and (e == "pe" or not SAME_ENGINE_SYNC):
                        continue
                    k_ = (0, d.eng)
                    if d.count > need.get(k_, 0):
                        need[k_] = d.count
                for x, v in need.items():
                    if v > waited.get(x, 0):
                        engine.wait_ge(sems[x], v)
                        waited[x] = v
                for k, v in op.dma_waits.items():
                    if v > dwaited.get(k, 0):
                        engine.wait_ge(dsems[k], v)
                        dwaited[k] = v
                if op.fn is None:
                    continue
                inst = op.fn(engine)
                if op.is_dma:
                    inst.then_inc(dsems[op.semkey], op.inc)
                elif op.signal:
                    inst.then_inc(sems[(0, e)], 1)
            if e == "sp":
                for k, v in final_waits.items():
                    engine.wait_ge(dsems[k], v)

        with nc.Block() as block:
            @block.tensor
            def _(t):
                run("pe", t)

            @block.scalar
            def _(t):
                run("act", t)

            @block.vector
            def _(t):
                run("dve", t)

            @block.gpsimd
            def _(t):
                run("pool", t)

            @block.sync
            def _(t):
                run("sp", t)


_uid = [0]


def _nm(n):
    _uid[0] += 1
    return "%s_u%d" % (n, _uid[0])


_RANK = {}


def _rank(e):
    if id(e) not in _RANK:
        _RANK[id(e)] = (e, (e.partition_id() % 4) * 512)
    return _RANK[id(e)][1]


class Ring:
    def __init__(self, nc, es, name, n, shape, dtype, psum=False):
        self.bufs = []
        for i in range(n):
            nm = "%s%d" % (name, i)
            if psum:
                t = es.enter_context(nc.psum_tensor(_nm(nm), shape, dtype))
            else:
                t = es.enter_context(nc.sbuf_tensor(_nm(nm), shape, dtype))
            self.bufs.append((t, nm))
        self.i = 0

    def next(self):
        b = self.bufs[self.i % len(self.bufs)]
        self.i += 1
        return b


def MM(P, out, lhsT, rhs, start, stop, reads, writes):
    P.add("pe", lambda e: e.matmul(out, lhsT, rhs, start=start, stop=stop), reads, writes)


def TR(P, out, in_, ident, reads, writes):
    P.add("pe", lambda e: e.transpose(out, in_, ident), reads, writes)


def ACT(P, out, in_, func, reads, writes, bias=None, scale=None):
    kw = {}
    if bias is not None:
        kw["bias"] = bias
    if scale is not None:
        kw["scale"] = scale
    P.add("act", lambda e: e.activation(out=out, in_=in_, func=func, **kw), reads, writes)


def TT(P, out, in0, in1, op, reads, writes, eng="dve"):
    P.add(eng, lambda e: e.tensor_tensor(out=out, in0=in0, in1=in1, op=op), reads, writes)


def TS(P, out, in0, s1, s2, op0, op1, reads, writes, eng="dve"):
    if op1 is None:
        P.add(eng, lambda e: e.tensor_scalar(out=out, in0=in0, scalar1=s1, scalar2=None, op0=op0),
              reads, writes)
    else:
        P.add(eng, lambda e: e.tensor_scalar(out=out, in0=in0, scalar1=s1, scalar2=s2, op0=op0, op1=op1),
              reads, writes)


def STT(P, out, in0, scalar, in1, op0, op1, reads, writes):
    P.add("dve", lambda e: e.scalar_tensor_tensor(out=out, in0=in0, scalar=scalar, in1=in1,
                                                  op0=op0, op1=op1), reads, writes)


def CP(P, eng, out, in_, reads, writes):
    P.add(eng, lambda e: e.tensor_copy(out=out, in_=in_), reads, writes)


def RECIP(P, out, in_, reads, writes):
    P.add("dve", lambda e: e.reciprocal(out=out, in_=in_), reads, writes)


def MSET(P, eng, ap, val, writes):
    P.add(eng, lambda e: e.memset(ap, val), (), writes)


def DMA(P, q, out, in_, reads, writes, key):
    P.add(q, lambda e: e.dma_start(out=out, in_=in_), reads, writes, dma=key)


GAIN_NAMES = ["ffn1_pre_g", "ffn1_post_g", "mix_pre_g", "hg_norm_g", "mix_post_g",
              "mem_pre_g", "mem_kv_g", "mem_post_g", "ffn2_pre_g", "ffn2_post_g"]
C_ID, C_U, C_BO, C_UF, C_ONE, C_SEL, C_SL, C_MD = 0, 128, 256, 384, 512, 640, 642, 706
C_TOT = 706 + 512


def make_consts():
    c = np.zeros((128, C_TOT), np.float32)
    i = np.arange(128)
    c[:, C_ID:C_ID + 128] = np.eye(128)
    same = (i[:, None] // 64) == (i[None, :] // 64)
    c[:, C_U:C_U + 128] = ((i[:, None] <= i[None, :]) & same)
    c[:, C_BO:C_BO + 128] = same
    c[:, C_UF:C_UF + 128] = (i[:, None] <= i[None, :])
    c[:, C_ONE:C_ONE + 128] = 1.0
    c[:, C_SEL + 0] = (i < 64)
    c[:, C_SEL + 1] = (i >= 64)
    j = np.arange(64)
    c[:64, C_SL:C_SL + 64] = (j[:, None] < j[None, :])
    jj = np.arange(512)
    c[:, C_MD:C_MD + 512] = np.where(jj[None, :] < i[:, None], NEG, 0.0)
    return c


class Ctx:
    pass


def rms_rstd(P, C, src, src_key, cols, ncols=512, dim=1024.0):
    sq, sqk = C.sq.next()
    ACT(P, sq[:, :, 0:ncols], src[:, :, cols], AF.Square, [src_key], [sqk])
    ps, psk = C.ps.next()
    for kt in range(8):
        MM(P, ps[:, 0:ncols], C.ones_bf[:], sq[:, kt, 0:ncols], kt == 0, kt == 7, [sqk, "consts_bf"], [psk])
    r, rk = C.rs.next()
    TS(P, r[:, 0:ncols], ps[:, 0:ncols], 1.0 / dim, EPS, ALU.mult, ALU.add, [psk], [rk])
    ACT(P, r[:, 0:ncols], r[:, 0:ncols], AF.Sqrt, [rk], [rk])
    RECIP(P, r[:, 0:ncols], r[:, 0:ncols], [rk], [rk])
    return r, rk


def load_w(P, C, w_d, kt_n, c0, ncols, dst_c0=0, buf=None):
    if buf is None:
        buf = C.wring.next()
    t, k = buf
    return buf


def wview(t, kt_n, width):
    return t[:, 0:kt_n * width].rearrange("p (k c) -> p k c", k=kt_n)


def wload(P, t, k, w_d, kt_n, width, c0, ncols, dst_c0):
    v = wview(t, kt_n, width)
    src = w_d[:, c0:c0 + ncols].rearrange("(k p) c -> p k c", p=128)
    DMA(P, "pool", v[:, :, dst_c0:dst_c0 + ncols], src, [], [k], "w_" + k)


def norm_h(P, C, hT, hk, hcols, xcols, g_col0, n=512):
    r, rk = rms_rstd(P, C, C.xT, "xT", xcols, n)
    for kt in range(8):
        STT(P, hT[:, kt, hcols], C.xT[:, kt, xcols], C.gains[:, g_col0 + kt:g_col0 + kt + 1],
            r[:, 0:n], ALU.mult, ALU.mult, ["xT", rk, "gains"], [hk])


def post_norm_add(P, C, yT, yk, ycols, xcols, g_col0, n=512):
    r, rk = rms_rstd(P, C, yT, yk, ycols, n)
    for kt in range(8):
        t, tk = C.tmp.next()
        STT(P, t[:, 0:n], yT[:, kt, ycols], C.gains[:, g_col0 + kt:g_col0 + kt + 1],
            r[:, 0:n], ALU.mult, ALU.mult, [yk, rk, "gains"], [tk])
        TT(P, C.xT[:, kt, xcols], C.xT[:, kt, xcols], t[:, 0:n], ALU.add, ["xT", tk], ["xT"])


def ffn(P, C, es, nc, w_in_d, w_dn_d, g_pre, g_post_h):
    yT = es.enter_context(nc.sbuf_tensor(_nm("ffn_hy"), [128, 8, 1024], F32))
    hT = yT.bitcast(BF16)
    actT = es.enter_context(nc.sbuf_tensor(_nm("ffn_actT"), [128, 22, 1024], BF16))
    sg_ring = Ring(nc, es, "ffn_sg", 2, [128, 512], F32)
    for half in range(2):
        t0 = half * 1024
        for tt in range(2):
            norm_h(P, C, hT, "ffn_hy", slice(tt * 512, (tt + 1) * 512),
                   slice(t0 + tt * 512, t0 + (tt + 1) * 512), g_pre)
        for ch in range(11):
            wt, wk = C.wring.next()
            wload(P, wt, wk, w_in_d, 8, 512, ch * 256, 256, 0)
            wload(P, wt, wk, w_in_d, 8, 512, DFF + ch * 256, 256, 256)
            wv = wview(wt, 8, 512)
            for jj in range(2):
                j = ch * 2 + jj
                for tt in range(2):
                    tc_ = slice(tt * 512, (tt + 1) * 512)
                    pg, pgk = C.ps.next()
                    for kt in range(8):
                        MM(P, pg[:], wv[:, kt, jj * 128:(jj + 1) * 128], hT[:, kt, tc_], kt == 0, kt == 7,
                           [wk, "ffn_hy"], [pgk])
                    pu, puk = C.ps.next()
                    for kt in range(8):
                        MM(P, pu[:], wv[:, kt, 256 + jj * 128:256 + (jj + 1) * 128], hT[:, kt, tc_],
                           kt == 0, kt == 7, [wk, "ffn_hy"], [puk])
                    sg, sgk = sg_ring.next()
                    ACT(P, sg[:], pg[:], AF.Silu, [pgk], [sgk])
                    TT(P, actT[:, j, tc_], pu[:], sg[:], ALU.mult, [puk, sgk], ["ffn_actT%d" % j])
        akeys = ["ffn_actT%d" % j for j in range(22)]
        for dc in range(4):
            wt, wk = C.wring.next()
            wload(P, wt, wk, w_dn_d, 22, 256, dc * 256, 256, 0)
            wv = wview(wt, 22, 256)
            for dd in range(2):
                d = dc * 2 + dd
                for tt in range(2):
                    tc_ = slice(tt * 512, (tt + 1) * 512)
                    py, pyk = C.ps.next()
                    for j in range(22):
                        MM(P, py[:], wv[:, j, dd * 128:(dd + 1) * 128], actT[:, j, tc_], j == 0, j == 21,
                           [wk] + akeys, [pyk])
                    ACT(P, yT[:, d, tc_], py[:], AF.Copy, [pyk], ["ffn_hy"])
        for tt in range(2):
            post_norm_add(P, C, yT, "ffn_hy", slice(tt * 512, (tt + 1) * 512),
                          slice(t0 + tt * 512, t0 + (tt + 1) * 512), g_post_h)


def build_program(dbg=False):
    _RANK.clear()
    nc = bass.Bass("TRN2", target_bir_lowering=False)

    def din(name, shape, dt=F32):
        return nc.dram_tensor(name, shape, dt, kind="ExternalInput").ap()

    xT_d = din("xT", [D, S_OWN])
    memT_d = din("memT", [D, 256])
    ffn1_in = din("ffn1_w_in", [D, 2 * DFF])
    ffn1_dn = din("ffn1_w_down", [DFF, D])
    ffn2_in = din("ffn2_w_in", [D, 2 * DFF])
    ffn2_dn = din("ffn2_w_down", [DFF, D])
    w_loc_d = din("w_loc", [D, 3072])
    w_hg_d = din("w_hg", [D, 768])
    w_fq_d = din("w_fq", [D, 256])
    w_fk_d = din("w_fk", [D, 256])
    w_fv_d = din("w_fv", [D, 258])
    w_a_d = din("w_branch_a", [D, D])
    w_b_d = din("w_branch_b", [D, D])
    w_o_d = din("w_out", [D, D])
    w_mq_d = din("w_mq", [D, D])
    w_mkv_d = din("w_mkv", [D, 2 * D])
    w_mo_d = din("w_mo", [D, D])
    gains_d = din("gains", [128, 96])
    consts_d = din("consts", [128, C_TOT])
    lbl_d = din("lbl", [128, 512])
    fbias_d = din("fbias", [128, 2])
    outT_d = nc.dram_tensor("outT", [D, S_OWN], F32, kind="ExternalOutput").ap()
    dbg_x1 = dbg_o = None
    if dbg:
        dbg_x1 = nc.dram_tensor("dbg_x1", [D, S_OWN], F32, kind="ExternalOutput").ap()
        dbg_o = nc.dram_tensor("dbg_o", [512, S_ALL], F32, kind="ExternalOutput").ap()
        dbg_x2 = nc.dram_tensor("dbg_x2", [D, S_OWN], F32, kind="ExternalOutput").ap()
        dbg_x3 = nc.dram_tensor("dbg_x3", [D, S_OWN], F32, kind="ExternalOutput").ap()

    ag1_in = [nc.dram_tensor("ag1_in%d" % i, [256, S_OWN], BF16).ap() for i in range(4)]
    ag1_out = [nc.dram_tensor("ag1_out%d" % i, [1024, S_OWN], BF16).ap() for i in range(4)]
    ag2_in = [[nc.dram_tensor("ag2_in%d_%d" % (q, tq), [128, S_OWN], F32).ap() for tq in range(4)] for q in range(4)]
    ag2_out = [nc.dram_tensor("ag2_out%d" % q, [4 * 512, S_OWN], F32).ap() for q in range(4)]
    x_spill = nc.dram_tensor("x_spill", [D, S_OWN], F32).ap()
    crow_d = nc.dram_tensor("crow_d", [3, 64 * 128], BF16).ap()
    RG = [[0, 1, 2, 3], [4, 5, 6, 7]]

    with contextlib.ExitStack() as es_all:
        P = Prog(nc, es_all)
        C = Ctx()
        C.consts = es_all.enter_context(nc.sbuf_tensor(_nm("consts_sb"), [128, C_TOT], F32))
        C.consts_bf = es_all.enter_context(nc.sbuf_tensor(_nm("consts_bf_sb"), [128, C_TOT], BF16))
        C.gains = es_all.enter_context(nc.sbuf_tensor(_nm("gains_sb"), [128, 96], F32))
        C.ones_bf = C.consts_bf[:, C_ONE:C_ONE + 128]
        C.ident_bf = C.consts_bf[:, C_ID:C_ID + 128]
        C.ps = Ring(nc, es_all, "ps", 8, [128, 512], F32, psum=True)
        DMA(P, "sp", C.consts[:], consts_d, [], ["consts"], "consts")
        DMA(P, "sp", C.gains[:], gains_d, [], ["gains"], "gains")
        CP(P, "dve", C.consts_bf[:], C.consts[:], ["consts"], ["consts_bf"])
        for g0 in (8, 72):
            TS(P, C.gains[:, g0:g0 + 8], C.gains[:, g0:g0 + 8], 0.5, None, ALU.mult, None, ["gains"], ["gains"])

        with contextlib.ExitStack() as es:
            C.xT = es.enter_context(nc.sbuf_tensor(_nm("xT_sb"), [128, 8, S_OWN], F32))
            DMA(P, "sp", C.xT[:], xT_d.rearrange("(k p) t -> p k t", p=128), [], ["xT"], "xT")
            C.sq = Ring(nc, es, "sq", 1, [128, 8, 512], BF16)
            C.rs = Ring(nc, es, "rs", 2, [128, 512], F32)
            C.tmp = Ring(nc, es, "tmp", 2, [128, 512], F32)
            C.wring = Ring(nc, es, "wr", 3, [128, 5632], BF16)
            with contextlib.ExitStack() as es2:
                ffn(P, C, es2, nc, ffn1_in, ffn1_dn, 0, 8)
            P.barrier()
            if dbg:
                DMA(P, "sp", dbg_x1.rearrange("(k p) t -> p k t", p=128), C.xT[:], ["xT"], [], "dbg")
            if STOP_AFTER == 1:
                DMA(P, "sp", outT_d.rearrange("(k p) t -> p k t", p=128), C.xT[:], ["xT"], [], "out")
                P.wait_dma_at_end("out")
                if dbg:
                    P.wait_dma_at_end("dbg")
                P.emit()
                build_program.stats = P.stats
                return nc
            with contextlib.ExitStack() as es2:
                hT = es2.enter_context(nc.sbuf_tensor(_nm("mix_hT"), [128, 8, S_OWN], BF16))
                for tt in range(4):
                    sl = slice(tt * 512, (tt + 1) * 512)
                    norm_h(P, C, hT, "mix_hT", sl, sl, 16)
                for ck in range(4):
                    DMA(P, "sp", ag1_in[ck].rearrange("(k p) t -> p k t", p=128), hT[:, 2 * ck:2 * ck + 2, :],
                        ["mix_hT"], ["ag1_in"], "ag1_in")
                DMA(P, "sp", x_spill.rearrange("(k p) t -> p k t", p=128), C.xT[:], ["xT"], ["x_spill"], "x_spill")
                for ck in range(4):
                    P.add("pool", lambda e, ck=ck: e.collective_compute(
                        "AllGather", ALU.bypass, replica_groups=RG, ins=[ag1_in[ck].opt()], outs=[ag1_out[ck].opt()]),
                        ["ag1_in"], ["ag1_out"], dma="agc", inc=1)
                P.barrier()
        if STOP_AFTER == 2:
            DMA(P, "sp", outT_d, x_spill, [], [], "out")
            P.wait_dma_at_end("out")
            if dbg:
                P.wait_dma_at_end("dbg")
            P.emit()
            build_program.stats = P.stats
            return nc
        with contextlib.ExitStack() as es:
            mixers(P, C, es, nc, ag1_out, ag2_in, w_hg_d, w_fq_d, w_fk_d, w_fv_d, lbl_d, fbias_d, crow_d)
            for q in range(0 if not NOAG2 else 4, 4):
                for tq in range(4):
                    P.add("pool", lambda e, q=q, tq=tq: e.collective_compute(
                        "AllGather", ALU.bypass, replica_groups=RG, ins=[ag2_in[q][tq].opt()],
                        outs=[ag2_out[q][tq * 512:(tq + 1) * 512, :].opt()]),
                        ["ag2_in"], ["ag2_out"], dma="agc", inc=1)
            P.barrier()
        if dbg:
            for q in range(4):
                for tq in range(4):
                    DMA(P, "sp", dbg_o[q * 128:(q + 1) * 128, tq * 2048:(tq + 1) * 2048], ag2_in[q][tq], [], [], "dbg")
        if STOP_AFTER == 3:
            DMA(P, "sp", outT_d, x_spill, [], [], "out")
            P.wait_dma_at_end("out")
            if dbg:
                P.wait_dma_at_end("dbg")
            P.emit()
            build_program.stats = P.stats
            return nc
        with contextlib.ExitStack() as es:
            C.xT = es.enter_context(nc.sbuf_tensor(_nm("xTb"), [128, 8, S_OWN], F32))
            DMA(P, "sp", C.xT[:], x_spill.rearrange("(k p) t -> p k t", p=128), [], ["xT"], "xT2")
            C.sq = Ring(nc, es, "sqb", 1, [128, 8, 512], BF16)
            C.rs = Ring(nc, es, "rsb", 2, [128, 512], F32)
            C.tmp = Ring(nc, es, "tmpb", 2, [128, 512], F32)
            C.wring = Ring(nc, es, "wrb", 3, [128, 5632], BF16)
            with contextlib.ExitStack() as es2:
                post_mixer(P, C, es2, nc, ag2_out, w_loc_d, w_a_d, w_b_d, w_o_d)
            P.barrier()
            if dbg:
                DMA(P, "sp", dbg_x2.rearrange("(k p) t -> p k t", p=128), C.xT[:], ["xT"], [], "dbg")
            with contextlib.ExitStack() as es2:
                mem_attn(P, C, es2, nc, memT_d, w_mq_d, w_mkv_d, w_mo_d)
            P.barrier()
            if dbg:
                DMA(P, "sp", dbg_x3.rearrange("(k p) t -> p k t", p=128), C.xT[:], ["xT"], [], "dbg")
            with contextlib.ExitStack() as es2:
                ffn(P, C, es2, nc, ffn2_in, ffn2_dn, 64, 72)
            DMA(P, "sp", outT_d.rearrange("(k p) t -> p k t", p=128), C.xT[:], ["xT"], [], "out")
            P.wait_dma_at_end("out")
            if dbg:
                P.wait_dma_at_end("dbg")
        P.emit()
        build_program.stats = P.stats
    return nc


def stream_proj(P, C, w_d, c0, ncols_total, rhs, rhs_keys, epilogue, kt_n=8):
    done = 0
    i = 0
    while done < ncols_total:
        n = min(512, ncols_total - done)
        wt, wk = C.wring.next()
        wload(P, wt, wk, w_d, kt_n, 512, c0 + done, n, 0)
        wv = wview(wt, kt_n, 512)
        for jj in range(n // 128):
            ps, psk = C.ps.next()
            for kt in range(kt_n):
                MM(P, ps[:], wv[:, kt, jj * 128:(jj + 1) * 128], rhs[:, kt, :], kt == 0, kt == kt_n - 1,
                   [wk] + rhs_keys, [psk])
            epilogue(i, ps, psk)
            i += 1
        done += n


def post_mixer(P, C, es, nc, ag2_out, w_loc_d, w_a_d, w_b_d, w_o_d):
    sb = lambda n, s, d: es.enter_context(nc.sbuf_tensor(_nm(n), s, d))
    oa = sb("pm_oa", [128, 8, 512], F32)
    ob = sb("pm_ob", [128, 8, 512], BF16)
    hT = sb("pm_hT", [128, 8, 512], BF16)
    ga = sb("pm_ga", [128, 8, 512], F32)
    gB = sb("pm_gB", [128, 8, 512], F32)
    oan = sb("pm_oan", [128, 8, 512], BF16)
    yg = sb("pm_yg", [128, 8, 512], BF16)
    ya = ga
    zT = oa
    o_own = nc.dram_tensor("o_own", [2048, S_OWN], F32).ap()
    dst = o_own.rearrange("(a r q p) t -> a q r p t", a=2, r=4, q=2, p=128)
    for q in range(4):
        def ld(e, q=q):
            roff = _rank(e)
            return e.dma_start(out=dst[q // 2, q % 2],
                               in_=ag2_out[q][bass.ds(roff, 512), :].rearrange("(r p) t -> r p t", p=128))
        P.add("pool", ld, [], ["o_own"], dma="o_own")
    for tt in range(4):
        sl = slice(tt * 512, (tt + 1) * 512)

        DMA(P, "sp", oa[:], o_own[0:1024, sl].rearrange("(h p) t -> p h t", p=128), ["o_own"], ["pm_oa"], "pm_oa")
        DMA(P, "pool", ob[:], o_own[1024:2048, sl].rearrange("(h p) t -> p h t", p=128), ["o_own"], ["pm_ob"], "pm_ob")
        norm_h(P, C, hT, "pm_hT", slice(0, 512), sl, 16)

        def ep_ga(i, ps, psk):
            ACT(P, ga[:, i, :], ps[:], AF.Silu, [psk], ["pm_ga"])
        stream_proj(P, C, w_loc_d, 0, 1024, hT, ["pm_hT"], ep_ga)
        r, rk = rms_rstd(P, C, oa, "pm_oa", slice(0, 512))
        for kt in range(8):
            t, tk = C.tmp.next()
            STT(P, t[:], oa[:, kt, :], C.gains[:, 24 + kt:25 + kt], r[:], ALU.mult, ALU.mult,
                ["pm_oa", rk, "gains"], [tk])
            TT(P, oan[:, kt, :], t[:], ga[:, kt, :], ALU.mult, [tk, "pm_ga"], ["pm_oan"])

        def ep_a(i, ps, psk):
            ACT(P, ya[:, i, :], ps[:], AF.Copy, [psk], ["pm_ga"])
        stream_proj(P, C, w_a_d, 0, 1024, oan, ["pm_oan"], ep_a)

        def ep_gA(i, ps, psk):
            t, tk = C.tmp.next()
            ACT(P, t[:], ps[:], AF.Sigmoid, [psk, "gains"], [tk], bias=C.gains[:, 80 + i:81 + i])
            TT(P, ya[:, i, :], ya[:, i, :], t[:], ALU.mult, [tk, "pm_ga"], ["pm_ga"])
        stream_proj(P, C, w_loc_d, 1024, 1024, hT, ["pm_hT"], ep_gA)

        def ep_gB(i, ps, psk):
            ACT(P, gB[:, i, :], ps[:], AF.Sigmoid, [psk, "gains"], ["pm_gB"], bias=C.gains[:, 88 + i:89 + i])
        stream_proj(P, C, w_loc_d, 2048, 1024, hT, ["pm_hT"], ep_gB)

        def ep_b(i, ps, psk):
            t, tk = C.tmp.next()
            TT(P, t[:], ps[:], gB[:, i, :], ALU.mult, [psk, "pm_gB"], [tk])
            TT(P, yg[:, i, :], t[:], ya[:, i, :], ALU.add, [tk, "pm_ga"], ["pm_yg"])
        stream_proj(P, C, w_b_d, 0, 1024, ob, ["pm_ob"], ep_b)

        def ep_o(i, ps, psk):
            ACT(P, zT[:, i, :], ps[:], AF.Copy, [psk], ["pm_oa"])
        stream_proj(P, C, w_o_d, 0, 1024, yg, ["pm_yg"], ep_o)
        post_norm_add(P, C, zT, "pm_oa", slice(0, 512), sl, 32)


def mem_attn(P, C, es, nc, memT_d, w_mq_d, w_mkv_d, w_mo_d):
    sb = lambda n, s, d: es.enter_context(nc.sbuf_tensor(_nm(n), s, d))
    memT = sb("ma_memT", [128, 8, 256], F32)
    memn = sb("ma_memn", [128, 8, 256], BF16)
    kmT = sb("ma_kmT", [128, 8, 256], BF16)
    vm = sb("ma_vm", [128, 2, 1024], BF16)
    hT = sb("ma_hT", [128, 8, 512], BF16)
    qmT = sb("ma_qmT", [128, 8, 512], BF16)
    omT = sb("ma_omT", [128, 8, 512], BF16)
    yT = sb("ma_yT", [128, 8, 512], F32)
    pT = Ring(nc, es, "ma_pT", 2, [128, 2, 512], BF16)
    rec = Ring(nc, es, "ma_rec", 2, [128, 512], F32)
    DMA(P, "sp", memT[:], memT_d.rearrange("(k p) t -> p k t", p=128), [], ["ma_memT"], "ma_memT")
    sq, sqk = C.sq.next()
    ACT(P, sq[:, :, 0:256], memT[:], AF.Square, ["ma_memT"], [sqk])
    ps, psk = C.ps.next()
    for kt in range(8):
        MM(P, ps[:, 0:256], C.ones_bf[:], sq[:, kt, 0:256], kt == 0, kt == 7, [sqk, "consts_bf"], [psk])
    r, rk = C.rs.next()
    TS(P, r[:, 0:256], ps[:, 0:256], 1.0 / 1024.0, EPS, ALU.mult, ALU.add, [psk], [rk])
    ACT(P, r[:, 0:256], r[:, 0:256], AF.Sqrt, [rk], [rk])
    RECIP(P, r[:, 0:256], r[:, 0:256], [rk], [rk])
    for kt in range(8):
        STT(P, memn[:, kt, :], memT[:, kt, :], C.gains[:, 48 + kt:49 + kt], r[:, 0:256], ALU.mult, ALU.mult,
            ["ma_memT", rk, "gains"], ["ma_memn"])
    for ch in range(2):
        wt, wk = C.wring.next()
        wload(P, wt, wk, w_mkv_d, 8, 512, ch * 512, 512, 0)
        wv = wview(wt, 8, 512)
        for jj in range(4):
            ps, psk = C.ps.next()
            for kt in range(8):
                MM(P, ps[:, 0:256], wv[:, kt, jj * 128:(jj + 1) * 128], memn[:, kt, :], kt == 0, kt == 7,
                   [wk, "ma_memn"], [psk])
            ACT(P, kmT[:, ch * 4 + jj, :], ps[:, 0:256], AF.Copy, [psk], ["ma_kmT"])
    for ch in range(2):
        wt, wk = C.wring.next()
        wload(P, wt, wk, w_mkv_d, 8, 512, 1024 + ch * 512, 512, 0)
        wv = wview(wt, 8, 512)
        for mt in range(2):
            ps, psk = C.ps.next()
            for kt in range(8):
                MM(P, ps[:], memn[:, kt, mt * 128:(mt + 1) * 128], wv[:, kt, :], kt == 0, kt == 7,
                   [wk, "ma_memn"], [psk])
            ACT(P, vm[:, mt, ch * 512:(ch + 1) * 512], ps[:], AF.Copy, [psk], ["ma_vm"])
    scale = 1.0 / math.sqrt(256.0)
    for tt in range(4):
        sl = slice(tt * 512, (tt + 1) * 512)
        norm_h(P, C, hT, "ma_hT", slice(0, 512), sl, 40)

        def ep_q(i, ps, psk):
            ACT(P, qmT[:, i, :], ps[:], AF.Copy, [psk], ["ma_qmT"])
        stream_proj(P, C, w_mq_d, 0, 1024, hT, ["ma_hT"], ep_q)
        for hh in range(4):
            p_t, pk = pT.next()
            for mt in range(2):
                ps, psk = C.ps.next()
                for a in range(2):
                    MM(P, ps[:], kmT[:, 2 * hh + a, mt * 128:(mt + 1) * 128], qmT[:, 2 * hh + a, :],
                       a == 0, a == 1, ["ma_kmT", "ma_qmT"], [psk])
                ACT(P, p_t[:, mt, :], ps[:], AF.Exp, [psk], [pk], scale=scale)
            ps, psk = C.ps.next()
            for mt in range(2):
                MM(P, ps[:], C.ones_bf[:], p_t[:, mt, :], mt == 0, mt == 1, [pk, "consts_bf"], [psk])
            rc, rck = rec.next()
            RECIP(P, rc[:], ps[:], [psk], [rck])
            for a in range(2):
                ps, psk = C.ps.next()
                for mt in range(2):
                    MM(P, ps[:], vm[:, mt, (2 * hh + a) * 128:(2 * hh + a + 1) * 128], p_t[:, mt, :],
                       mt == 0, mt == 1, [pk, "ma_vm"], [psk])
                TT(P, omT[:, 2 * hh + a, :], ps[:], rc[:], ALU.mult, [psk, rck], ["ma_omT"])

        def ep_o(i, ps, psk):
            ACT(P, yT[:, i, :], ps[:], AF.Copy, [psk], ["ma_yT"])
        stream_proj(P, C, w_mo_d, 0, 1024, omT, ["ma_omT"], ep_o)
        post_norm_add(P, C, yT, "ma_yT", slice(0, 512), sl, 56)


def mixers(P, C, es, nc, ag1_out, ag2_in, w_hg_d, w_fq_d, w_fk_d, w_fv_d, lbl_d, fbias_d, crow_d):
    sb = lambda n, s, d: es.enter_context(nc.sbuf_tensor(_nm(n), s, d))
    cf = C.consts
    U_f = cf[:, C_U:C_U + 128]
    BO_f = cf[:, C_BO:C_BO + 128]
    UF_f = cf[:, C_UF:C_UF + 128]
    ONE_f = cf[:, C_ONE:C_ONE + 128]
    SEL_f = cf[:, C_SEL:C_SEL + 2]
    SL_f = cf[0:64, C_SL:C_SL + 64]
    MD_bf = C.consts_bf[:, C_MD:C_MD + 512]
    whg = sb("mx_whg", [128, 8, 768], BF16)
    wfq = sb("mx_wfq", [128, 8, 256], BF16)
    wfk = sb("mx_wfk", [128, 8, 256], BF16)
    wfv = sb("mx_wfv", [128, 8, 258], BF16)
    for t, d_, n in ((whg, w_hg_d, "mx_whg"), (wfq, w_fq_d, "mx_wfq"), (wfk, w_fk_d, "mx_wfk"), (wfv, w_fv_d, "mx_wfv")):
        DMA(P, "pool", t[:], d_.rearrange("(k p) c -> p k c", p=128), [], [n], n)
    lbl = sb("mx_lbl", [128, 512], F32)
    fbias = sb("mx_fbias", [128, 2], F32)
    DMA(P, "sp", lbl[:], lbl_d, [], ["mx_lbl"], "mx_lbl")
    DMA(P, "sp", fbias[:], fbias_d, [], ["mx_fbias"], "mx_fbias")
    lb = sb("mx_lb", [128, 256], F32)
    oml = sb("mx_oml", [128, 256], F32)
    TT(P, lb[:], lbl[:, 0:256], lbl[:, 256:512], ALU.subtract, ["mx_lbl"], ["mx_lb"])
    ACT(P, lb[:], lb[:], AF.Sigmoid, ["mx_lb"], ["mx_lb"])
    TS(P, oml[:], lb[:], -1.0, 1.0, ALU.mult, ALU.add, ["mx_lb"], ["mx_oml"])
    lbw = sb("mx_lbw", [128, 4, 128], F32)
    omlw = sb("mx_omlw", [128, 4, 128], F32)
    U4 = sb("mx_U4", [128, 4, 128], F32)
    for s_ in range(4):
        CP(P, "dve", U4[:, s_, :], U_f, ["consts"], ["mx_U4"])
    qT = sb("mx_qT", [128, S_ALL], BF16)
    kT = sb("mx_kT", [128, S_ALL], BF16)
    Vp = sb("mx_Vp", [128, 64, 130], BF16)
    fb = sb("mx_fb", [128, 64], F32)
    MSET(P, "dve", Vp[:, :, 128:130], 1.0, ["mx_Vp"])
    hring = Ring(nc, es, "mx_h", 2, [128, 8, 512], BF16)
    f32r = Ring(nc, es, "mx_f", 11, [128, 4, 128], F32)
    bfr = Ring(nc, es, "mx_b", 11, [128, 4, 128], BF16)
    S32 = sb("mx_S32", [128, 128], F32)
    Sbf = Ring(nc, es, "mx_Sbf", 2, [128, 128], BF16)
    dT = Ring(nc, es, "mx_dT", 2, [128, 8], F32)
    ostage = Ring(nc, es, "mx_ost", 2, [128, 512], F32)
    crow = sb("mx_crow", [96, S_ALL], BF16)
    biasG = sb("mx_biasG", [128, 16, 64], F32)
    ccol = sb("mx_ccol", [128, 64], F32)
    offbc = sb("mx_offbc", [128, 64], F32)
    ls = sb("mx_ls", [128, 64], F32)
    totbc = sb("mx_totbc", [64, 128], F32)
    cd = sb("mx_cd", [128, 64], F32)
    c3 = sb("mx_c3", [128, 3, 64], BF16)
    c3T = sb("mx_c3T", [64, 3, 128], BF16)
    r1 = sb("mx_r1", [128, 64], F32)
    l3 = sb("mx_l3", [128, 3, 64], BF16)
    tb3 = sb("mx_tb3", [64, 3, 128], BF16)
    tbr = sb("mx_tbr", [64, 128], F32)
    pTr = Ring(nc, es, "mx_pT", 3, [128, 512], BF16)
    obr = Ring(nc, es, "mx_ob", 2, [128, 128], BF16)
    recr = Ring(nc, es, "mx_rec", 2, [128, 1], F32)
    obT = Ring(nc, es, "mx_obT", 2, [128, 512], F32)
    MSET(P, "dve", crow[:], 0.0, ["mx_crow"])
    ones96 = C.consts_bf[0:96, C_ONE:C_ONE + 128]
    psb = C.ps.bufs
    PS = lambda i: psb[i]
    scale = 1.0 / math.sqrt(128.0)

    for h in range(2):
        for s_ in range(4):
            CP(P, "dve", lbw[:, s_, :], lb[:, h * 128:(h + 1) * 128], ["mx_lb"], ["mx_lbw"])
            CP(P, "dve", omlw[:, s_, :], oml[:, h * 128:(h + 1) * 128], ["mx_oml"], ["mx_omlw"])
        MSET(P, "dve", S32[:], 0.0, ["mx_S32"])
        sb_t, sb_k = Sbf.next()
        MSET(P, "dve", sb_t[:], 0.0, [sb_k])
        for T in range(16):
            rr, t0 = T // 4, (T % 4) * 512
            ht, hk = hring.next()
            for ck in range(4):
                srch = ag1_out[ck][rr * 256:(rr + 1) * 256, t0:t0 + 512].rearrange("(k p) t -> p k t", p=128)
                DMA(P, "sp", ht[:, 2 * ck:2 * ck + 2, :], srch, ["ag1_out"], [hk], "h_" + hk)
            if not (MIX_PARTS & 1):
                continue
            for (w_, dst, nm, wn) in ((wfq, qT, "mx_qT", "mx_wfq"), (wfk, kT, "mx_kT", "mx_wfk")):
                ps, psk = PS(6)
                for kt in range(8):
                    MM(P, ps[:], w_[:, kt, h * 128:(h + 1) * 128], ht[:, kt, :], kt == 0, kt == 7,
                       [wn, hk], [psk])
                ACT(P, dst[:, T * 512:(T + 1) * 512], ps[:], AF.Copy, [psk], [nm])
            ps, psk = PS(7)
            for s_ in range(4):
                for kt in range(8):
                    MM(P, ps[:, s_ * 128:s_ * 128 + 128], ht[:, kt, s_ * 128:(s_ + 1) * 128],
                       wfv[:, kt, h * 129:h * 129 + 128], kt == 0, kt == 7, ["mx_wfv", hk], [psk])
            ACT(P, Vp[:, T * 4:(T + 1) * 4, 0:128], ps[:].rearrange("p (s c) -> p s c", s=4), AF.Copy,
                [psk], ["mx_Vp"])
            ps, psk = PS(7)
            for s_ in range(4):
                for kt in range(8):
                    MM(P, ps[:, s_:s_ + 1], ht[:, kt, s_ * 128:(s_ + 1) * 128],
                       wfv[:, kt, h * 129 + 128:h * 129 + 129], kt == 0, kt == 7, ["mx_wfv", hk], [psk])
            ACT(P, fb[:, T * 4:(T + 1) * 4], ps[:, 0:4], AF.Copy, [psk], ["mx_fb"])
            if not (MIX_PARTS & 2):
                continue
            zq, zqk = PS(0)
            zf, zfk = PS(1)
            zi, zik = PS(2)
            for (z_, zk_, off) in ((zq, zqk, 0), (zf, zfk, 128), (zi, zik, 256)):
                for s_ in range(4):
                    for kt in range(8):
                        MM(P, z_[:, s_ * 128:(s_ + 1) * 128], ht[:, kt, s_ * 128:(s_ + 1) * 128],
                           whg[:, kt, h * 384 + off:h * 384 + off + 128], kt == 0, kt == 7,
                           ["mx_whg", hk], [zk_])
            qs, qsk = f32r.next()
            sg, sgk = f32r.next()
            Ib, Ibk = bfr.next()
            W = lambda t: t[:].rearrange("p s c -> p (s c)")
            ACT(P, W(qs), zq[:], AF.Silu, [zqk], [qsk])
            ACT(P, W(sg), zf[:], AF.Sigmoid, [zfk], [sgk])
            ACT(P, W(Ib), zi[:], AF.Copy, [zik], [Ibk])
            f_, fk_ = f32r.next()
            TT(P, W(f_), W(sg), W(omlw), ALU.mult, [sgk, "mx_omlw"], [fk_])
            TT(P, W(f_), W(f_), W(lbw), ALU.add, [fk_, "mx_lbw"], [fk_])
            lf, lfk = f32r.next()
            ACT(P, W(lf), W(f_), AF.Ln, [fk_], [lfk])
            kk, kkk = f32r.next()
            TS(P, W(kk), W(f_), -1.0, 1.0, ALU.mult, ALU.add, [fk_], [kkk])
            lfh, lfhk = bfr.next()
            lfl, lflk = bfr.next()
            lfr, lfrk = f32r.next()
            CP(P, "dve", W(lfh), W(lf), [lfk], [lfhk])
            TT(P, W(lfr), W(lf), W(lfh), ALU.subtract, [lfk, lfhk], [lfrk])
            CP(P, "dve", W(lfl), W(lfr), [lfrk], [lflk])
            U_b = C.consts_bf[:, C_U:C_U + 128]
            BO_b = C.consts_bf[:, C_BO:C_BO + 128]
            SEL_b = C.consts_bf[:, C_SEL:C_SEL + 2]
            bps, bpsk = PS(3)
            MM(P, bps[:], U_b, W(lfh), True, False, [lfhk, "consts_bf"], [bpsk])
            MM(P, bps[:], U_b, W(lfl), False, True, [lflk, "consts_bf"], [bpsk])
            blps, blpsk = PS(5)
            MM(P, blps[:], BO_b, W(lfh), True, False, [lfhk, "consts_bf"], [blpsk])
            MM(P, blps[:], BO_b, W(lfl), False, True, [lflk, "consts_bf"], [blpsk])
            btps, btpsk = PS(6)
            for s_ in range(4):
                MM(P, btps[:, 2 * s_:2 * s_ + 2], lfh[:, s_, :], SEL_b, True, False, [lfhk, "consts_bf"], [btpsk])
                MM(P, btps[:, 2 * s_:2 * s_ + 2], lfl[:, s_, :], SEL_b, False, True, [lflk, "consts_bf"], [btpsk])
            eb, ebk = f32r.next()
            enb, enbk = f32r.next()
            ebl, eblk = f32r.next()
            d_t, dk_ = dT.next()
            ACT(P, W(eb), bps[:], AF.Exp, [bpsk], [ebk])
            ACT(P, W(enb), bps[:], AF.Exp, [bpsk], [enbk], scale=-1.0)
            ACT(P, W(ebl), blps[:], AF.Exp, [blpsk], [eblk])
            ACT(P, d_t[:], btps[:, 0:8], AF.Exp, [btpsk], [dk_])
            Qt, Qtk = bfr.next()
            Kt32, Kt32k = f32r.next()
            Ktb, Ktbk = bfr.next()
            Kh0, Kh0k = bfr.next()
            Kh1, Kh1k = bfr.next()
            TT(P, W(Qt), W(qs), W(eb), ALU.mult, [qsk, ebk], [Qtk])
            TT(P, W(Kt32), W(kk), W(enb), ALU.mult, [kkk, enbk], [Kt32k])
            CP(P, "dve", W(Ktb), W(Kt32), [Kt32k], [Ktbk])
            STT(P, W(Kh0), W(Kt32), SEL_f[:, 0:1], W(ebl), ALU.mult, ALU.mult, [Kt32k, eblk, "consts"], [Kh0k])
            STT(P, W(Kh1), W(Kt32), SEL_f[:, 1:2], W(ebl), ALU.mult, ALU.mult, [Kt32k, eblk, "consts"], [Kh1k])
            tp, tpk = PS(0)
            tpb = tp.bitcast(BF16)
            tp2, tp2k = PS(7)
            tpb2 = tp2.bitcast(BF16)
            for s_ in range(4):
                TR(P, tpb[:, s_ * 128:(s_ + 1) * 128], Qt[:, s_, :], C.ident_bf, [Qtk, "consts_bf"], [tpk])
                TR(P, tpb2[:, s_ * 128:(s_ + 1) * 128], Ktb[:, s_, :], C.ident_bf,
                   [Ktbk, "consts_bf"], [tp2k])
            QtT, QtTk = bfr.next()
            KtT, KtTk = bfr.next()
            ACT(P, W(QtT), tpb[:, 0:512], AF.Copy, [tpk], [QtTk])
            CP(P, "dve", W(KtT), tpb2[:, 0:512], [tp2k], [KtTk])
            aps, apsk = PS(1)
            for s_ in range(4):
                MM(P, aps[:, s_ * 128:(s_ + 1) * 128], KtT[:, s_, :], QtT[:, s_, :], True, True,
                   [KtTk, QtTk], [apsk])
            Am, Amk = bfr.next()
            TT(P, W(Am), aps[:], W(U4), ALU.mult, [apsk, "mx_U4"], [Amk])
            ops_, opsk = PS(4)
            for s_ in range(4):
                MM(P, ops_[:, s_ * 128:(s_ + 1) * 128], Ib[:, s_, :], Am[:, s_, :], True, False,
                   [Ibk, Amk], [opsk])
                for c_ in range(2):
                    col = s_ * 128 + c_ * 64
                    MM(P, ops_[:, col:col + 64], sb_t[:], QtT[:, s_, c_ * 64:(c_ + 1) * 64], False,
                       c_ == 1, [sb_k, QtTk], [opsk])
                    sups, supsk = PS(5)
                    Khc, Khck = (Kh0, Kh0k) if c_ == 0 else (Kh1, Kh1k)
                    MM(P, sups[:, 0:128], Khc[:, s_, :], Ib[:, s_, :], True, True, [Khck, Ibk], [supsk])
                    STT(P, S32[:], S32[:], d_t[:, 2 * s_ + c_:2 * s_ + c_ + 1], sups[:, 0:128],
                        ALU.mult, ALU.add, ["mx_S32", dk_, supsk], ["mx_S32"])
                    sb_t, sb_k = Sbf.next()
                    ACT(P, sb_t[:], S32[:], AF.Copy, ["mx_S32"], [sb_k])
            ot, otk = ostage.next()
            ACT(P, ot[:], ops_[:], AF.Copy, [opsk], [otk])
            DMA(P, "sp", ag2_in[h][T // 4][:, (T % 4) * 512:(T % 4 + 1) * 512], ot[:], [otk], ["ag2_in"],
                "o_" + otk)
        P.barrier()
        if not (MIX_PARTS & 4):
            continue
        ACT(P, ls[:], fb[:], AF.Sigmoid, ["mx_fb", "mx_fbias"], ["mx_ls"], bias=fbias[:, h:h + 1])
        ACT(P, ls[:], ls[:], AF.Ln, ["mx_ls"], ["mx_ls"])
        def split3(dst, src, tmp, skey, dkey, tkey):
            CP(P, "dve", dst[:, 0, :], src, [skey], [dkey])
            TT(P, tmp, src, dst[:, 0, :], ALU.subtract, [skey, dkey], [tkey])
            CP(P, "dve", dst[:, 1, :], tmp, [tkey], [dkey])
            TT(P, tmp, tmp, dst[:, 1, :], ALU.subtract, [tkey, dkey], [tkey])
            CP(P, "dve", dst[:, 2, :], tmp, [tkey], [dkey])
        split3(l3, ls[:], r1[:], "mx_ls", "mx_l3", "mx_r1")
        UF_b = C.consts_bf[:, C_UF:C_UF + 128]
        SL_b = C.consts_bf[0:64, C_SL:C_SL + 64]
        ps, psk = PS(0)
        for a_ in range(3):
            MM(P, ps[0:64, 0:128], l3[:, a_, :], C.ones_bf, a_ == 0, a_ == 2, ["mx_l3", "consts_bf"], [psk])
        CP(P, "dve", totbc[:], ps[0:64, 0:128], [psk], ["mx_totbc"])
        split3(tb3, totbc[:], tbr[:], "mx_totbc", "mx_tb3", "mx_tbr")
        ps, psk = PS(1)
        for a_ in range(3):
            MM(P, ps[:, 0:64], tb3[:, a_, :], SL_b, a_ == 0, a_ == 2, ["mx_tb3", "consts_bf"], [psk])
        CP(P, "dve", offbc[:], ps[:, 0:64], [psk], ["mx_offbc"])
        ps2, ps2k = PS(2)
        for a_ in range(3):
            MM(P, ps2[:, 0:64], UF_b, l3[:, a_, :], a_ == 0, a_ == 2, ["mx_l3", "consts_bf"], [ps2k])
        TT(P, ccol[:], ps2[:, 0:64], offbc[:], ALU.add, [ps2k, "mx_offbc"], ["mx_ccol"])
        for G in range(16):
            nj = 4 * G + 4
            cref = offbc[:, 4 * G:4 * G + 1]
            TS(P, biasG[:, G, 0:nj], ccol[:, 0:nj], cref, -1.0, ALU.subtract, ALU.mult,
               ["mx_ccol", "mx_offbc"], ["mx_biasG"])
            TS(P, cd[:, 4 * G:4 * G + 4], ccol[:, 4 * G:4 * G + 4], cref, 1.0 / scale, ALU.subtract, ALU.mult,
               ["mx_ccol", "mx_offbc"], ["mx_cd"])
        CP(P, "dve", c3[:, 0, :], cd[:], ["mx_cd"], ["mx_c3"])
        TT(P, r1[:], cd[:], c3[:, 0, :], ALU.subtract, ["mx_cd", "mx_c3"], ["mx_r1"])
        CP(P, "dve", c3[:, 1, :], r1[:], ["mx_r1"], ["mx_c3"])
        TT(P, r1[:], r1[:], c3[:, 1, :], ALU.subtract, ["mx_r1", "mx_c3"], ["mx_r1"])
        CP(P, "dve", c3[:, 2, :], r1[:], ["mx_r1"], ["mx_c3"])
        tp, tpk = PS(3)
        tpb = tp.bitcast(BF16)
        for a in range(3):
            TR(P, tpb[0:64, a * 128:(a + 1) * 128], c3[:, a, :], C.ident_bf, ["mx_c3", "consts_bf"], [tpk])
        CP(P, "dve", c3T[:].rearrange("p a c -> p (a c)"), tpb[0:64, 0:384], [tpk], ["mx_c3T"])
        DMA(P, "sp", crow_d.rearrange("a (j t) -> j a t", j=64), c3T[:], ["mx_c3T"], ["crow_d"], "crow_d")
        for a in range(3):
            DMA(P, "sp", crow[32 * a:32 * a + 1, :], crow_d[a:a + 1, :], ["crow_d"], ["mx_crow"], "mx_crow")
        for G in range(16):
            q0 = G * 512
            o_ps = [PS(4), PS(5), PS(0), PS(1)]
            for j in range(4 * G + 4):
                i_ = j - 4 * G
                c0 = 0 if i_ < 0 else 128 * i_
                n = 512 - c0
                st, stk = PS(6 + (j % 2))
                MM(P, st[:, 0:n], kT[:, j * 128:(j + 1) * 128], qT[:, q0 + c0:q0 + 512], True, False,
                   ["mx_kT", "mx_qT"], [stk])
                MM(P, st[:, 0:n], ones96, crow[:, q0 + c0:q0 + 512], False, i_ < 0, ["mx_crow", "consts_bf"], [stk])
                if i_ >= 0:
                    MM(P, st[:, 0:n], C.ident_bf, MD_bf[:, 0:n], False, True, ["consts_bf"], [stk])
                p_t, pk = pTr.next()
                ACT(P, p_t[:, 0:n], st[:, 0:n], AF.Exp, [stk, "mx_biasG"], [pk], bias=biasG[:, G, j:j + 1], scale=scale)
                for qb in range(4):
                    if 128 * qb < c0:
                        continue
                    op_t, op_k = o_ps[qb]
                    oc = 0
                    MM(P, op_t[:, oc:oc + 129], p_t[:, 128 * qb - c0:128 * qb - c0 + 128], Vp[:, j, 0:129],
                       j == 0, j == 4 * G + qb, [pk, "mx_Vp"], [op_k])
                if i_ >= 0:
                    qb = i_
                    op_t, op_k = o_ps[qb]
                    oc = 0
                    okey = op_k
                    rc, rck = recr.next()
                    RECIP(P, rc[:], op_t[:, oc + 128:oc + 129], [okey], [rck])
                    ob_, obk = obr.next()
                    ACT(P, ob_[:], op_t[:, oc:oc + 128], AF.Identity, [okey, rck], [obk], scale=rc[:])
                    tp, tpk = PS(3)
                    tpb = tp.bitcast(BF16)
                    TR(P, tpb[:, 0:128], ob_[:], C.ident_bf, [obk, "consts_bf"], [tpk])
                    if qb == 0:
                        obT_t, obT_k = obT.next()
                    CP(P, "dve", obT_t[:, qb * 128:(qb + 1) * 128], tpb[:, 0:128], [tpk], [obT_k])
                    if qb == 3:
                        DMA(P, "sp", ag2_in[2 + h][G // 4][:, (G % 4) * 512:(G % 4 + 1) * 512], obT_t[:],
                            [obT_k], ["ag2_in"], "o_" + obT_k)
        P.barrier()


_CACHE = {}


def _prep_inputs(inp):
    f = lambda a: np.ascontiguousarray(np.asarray(a, dtype=np.float32))
    x = f(inp["x"])
    mem = f(inp["mem"])
    w_in = f(inp["w_in"])[0]
    offs = [0, 1024, 2048, 3072, 4096, 5120, 6144, 7168, 7176, 9224]
    q_a, f_a, i_a, g_a, q_b, k_b, v_b, f_b, gates = [w_in[:, offs[i]:offs[i + 1]] for i in range(9)]
    gains = np.zeros((128, 96), np.float32)
    for i, n in enumerate(GAIN_NAMES):
        gains[:, 8 * i:8 * i + 8] = f(inp[n])[0].reshape(8, 128).T
    gains[:, 80:96] = f(inp["b_gate"])[0].reshape(16, 128).T
    consts = make_consts()
    w_loc = np.ascontiguousarray(np.concatenate([g_a, gates], axis=1))
    lbl_all = f(inp["hg_lb_logits"])
    fbias_all = f(inp["fox_f_bias"])[0]
    shared = {
        "ffn1_w_in": f(inp["ffn1_w_in"])[0], "ffn1_w_down": f(inp["ffn1_w_down"])[0],
        "ffn2_w_in": f(inp["ffn2_w_in"])[0], "ffn2_w_down": f(inp["ffn2_w_down"])[0],
        "w_loc": w_loc, "w_branch_a": f(inp["w_branch_a"])[0], "w_branch_b": f(inp["w_branch_b"])[0],
        "w_out": f(inp["w_out"])[0], "w_mq": f(inp["w_mq"])[0], "w_mkv": f(inp["w_mkv"])[0],
        "w_mo": f(inp["w_mo"])[0], "gains": gains, "consts": consts,
    }
    maps = []
    for c in range(8):
        b, r = c // 4, c % 4
        hs = [2 * r, 2 * r + 1]
        m = dict(shared)
        m["xT"] = np.ascontiguousarray(x[b, r * S_OWN:(r + 1) * S_OWN, :].T)
        m["memT"] = np.ascontiguousarray(mem[b].T)
        sl = lambda w, h: w[:, h * 128:(h + 1) * 128]
        m["w_hg"] = np.ascontiguousarray(np.concatenate(
            [np.concatenate([sl(q_a, h), sl(f_a, h), sl(i_a, h)], axis=1) for h in hs], axis=1))
        m["w_fq"] = np.ascontiguousarray(np.concatenate([sl(q_b, h) for h in hs], axis=1))
        m["w_fk"] = np.ascontiguousarray(np.concatenate([sl(k_b, h) for h in hs], axis=1))
        m["w_fv"] = np.ascontiguousarray(np.concatenate(
            [np.concatenate([sl(v_b, h), f_b[:, h:h + 1]], axis=1) for h in hs], axis=1))
        l0 = np.concatenate([lbl_all[0, h] for h in hs])
        l1 = np.concatenate([lbl_all[1, h] for h in hs])
        m["lbl"] = np.ascontiguousarray(np.tile(np.concatenate([l0, l1])[None, :], (128, 1)))
        m["fbias"] = np.ascontiguousarray(np.tile(fbias_all[hs][None, :], (128, 1)))
        maps.append(m)
    return maps


def kernel(**inputs):
    dbg = bool(inputs.pop("_dbg", False))
    key = ("nc", dbg)
    if key not in _CACHE:
        _CACHE[key] = build_program(dbg)
    nc = _CACHE[key]
    maps = _prep_inputs(inputs)
    res = run_bass_kernel_spmd(nc, maps, core_ids=list(range(8)))
    out = np.zeros((2, S_ALL, D), np.float32)
    for c in range(8):
        b, r = c // 4, c % 4
        out[b, r * S_OWN:(r + 1) * S_OWN, :] = np.asarray(res.results[c]["outT"]).T
    if dbg:
        return out, res.results
    return out
```

```python
import contextlib
import math
import numpy as np
import concourse.bass as bass
import concourse.mybir as mybir
from concourse.bass_utils import run_bass_kernel_spmd

F32 = mybir.dt.float32
BF16 = mybir.dt.bfloat16
AF = mybir.ActivationFunctionType
ALU = mybir.AluOpType

SAME_ENGINE_SYNC = False
STOP_AFTER = 9
MIX_PARTS = 7
NOAG2 = False
EPS = 1e-6
D = 1024
S_OWN = 2048
S_ALL = 8192
DFF = 2816
NEG = -1.0e9


class Op:
    __slots__ = ("eng", "fn", "deps", "is_dma", "semkey", "idx", "signal", "count",
                 "dma_waits", "inc", "seg")


class Prog:
    def __init__(self, nc, es):
        self.nc = nc
        self.es = es
        self.ops = []
        self.lastw = {}
        self.readers = {}
        self.dma_tot = {}
        self.final_waits = {}
        self.last_on = {}
        self.seg = 0

    def add(self, eng, fn, reads=(), writes=(), dma=None, inc=16):
        if dma is not None:
            dma = "cc" if inc == 1 else ("q_" + eng)
        deps = {}
        for r in reads:
            w = self.lastw.get(r)
            if w is not None:
                deps[w.idx] = w
        for r in writes:
            w = self.lastw.get(r)
            if w is not None:
                deps[w.idx] = w
            for rd in self.readers.get(r, {}).values():
                deps[rd.idx] = rd
        op = Op()
        op.eng = eng
        op.fn = fn
        op.is_dma = dma is not None
        op.semkey = dma
        op.inc = inc
        op.idx = len(self.ops)
        op.signal = False
        op.count = 0
        op.seg = self.seg
        dma_waits = {}
        cdeps = []
        for d in deps.values():
            if d.is_dma:
                dma_waits[d.semkey] = self.dma_tot[d.semkey]
            elif d.fn is not None:
                cdeps.append(d)
        op.deps = cdeps
        op.dma_waits = dma_waits
        if dma is not None:
            self.dma_tot[dma] = self.dma_tot.get(dma, 0) + inc
        for r in reads:
            self.readers.setdefault(r, {})[(eng, op.idx) if op.is_dma else eng] = op
        for r in writes:
            self.lastw[r] = op
            self.readers[r] = {}
        self.ops.append(op)
        if not op.is_dma and fn is not None:
            self.last_on[eng] = op
        return op

    def barrier(self):
        lasts = list(self.last_on.values())
        tot = dict(self.dma_tot)
        for e in ["pe", "act", "dve", "pool", "sp"]:
            op = Op()
            op.eng = e
            op.fn = None
            op.is_dma = False
            op.semkey = None
            op.inc = 0
            op.idx = len(self.ops)
            op.signal = False
            op.count = 0
            op.deps = [d for d in lasts if d.eng != e]
            op.dma_waits = dict(tot)
            op.seg = self.seg
            self.ops.append(op)
        self.lastw = {}
        self.readers = {}
        self.last_on = {}
        self.seg += 1

    def wait_dma_at_end(self, key):
        for k in self.dma_tot:
            self.final_waits[k] = self.dma_tot[k]

    def emit(self):
        nc = self.nc
        engs = ["pe", "act", "dve", "pool", "sp"]
        for op in self.ops:
            for d in op.deps:
                if d.eng == op.eng and (op.eng == "pe" or not SAME_ENGINE_SYNC):
                    continue
                d.signal = True
        cnt = {}
        for op in self.ops:
            if op.is_dma or op.fn is None:
                continue
            if op.signal:
                k_ = (0, op.eng)
                cnt[k_] = cnt.get(k_, 0) + 1
                op.count = cnt[k_]
        sems = {k_: self.es.enter_context(nc.semaphore("s_%d_%s" % k_)) for k_ in cnt}
        dsems = {}
        for k in self.dma_tot:
            dsems[k] = self.es.enter_context(nc.semaphore("d_" + str(len(dsems))))
        per_eng = {e: [o for o in self.ops if o.eng == e] for e in engs}
        self.stats = {e: len(per_eng[e]) for e in engs}
        self.stats["sems"] = len(dsems) + len(sems)
        self.stats["counts"] = max(cnt.values()) if cnt else 0
        final_waits = self.final_waits

        def run(e, engine):
            waited = {}
            dwaited = {}
            for op in per_eng[e]:
                need = {}
                for d in op.deps:
                    if d.eng == e and (e == "pe" or not SAME_ENGINE_SYNC):
                        continue
                    k_ = (0, d.eng)
                    if d.count > need.get(k_, 0):
                        need[k_] = d.count
                for x, v in need.items():
                    if v > waited.get(x, 0):
                        engine.wait_ge(sems[x], v)
                        waited[x] = v
                for k, v in op.dma_waits.items():
                    if v > dwaited.get(k, 0):
                        engine.wait_ge(dsems[k], v)
                        dwaited[k] = v
                if op.fn is None:
                    continue
                inst = op.fn(engine)
                if op.is_dma:
                    inst.then_inc(dsems[op.semkey], op.inc)
                elif op.signal:
                    inst.then_inc(sems[(0, e)], 1)
            if e == "sp":
                for k, v in final_waits.items():
                    engine.wait_ge(dsems[k], v)

        with nc.Block() as block:
            @block.tensor
            def _(t):
                run("pe", t)

            @block.scalar
            def _(t):
                run("act", t)

            @block.vector
            def _(t):
                run("dve", t)

            @block.gpsimd
            def _(t):
                run("pool", t)

            @block.sync
            def _(t):
                run("sp", t)


_uid = [0]


def _nm(n):
    _uid[0] += 1
    return "%s_u%d" % (n, _uid[0])


_RANK = {}


def _rank(e):
    if id(e) not in _RANK:
        _RANK[id(e)] = (e, (e.partition_id() % 4) * 512)
    return _RANK[id(e)][1]


class Ring:
    def __init__(self, nc, es, name, n, shape, dtype, psum=False):
        self.bufs = []
        for i in range(n):
            nm = "%s%d" % (name, i)
            if psum:
                t = es.enter_context(nc.psum_tensor(_nm(nm), shape, dtype))
            else:
                t = es.enter_context(nc.sbuf_tensor(_nm(nm), shape, dtype))
            self.bufs.append((t, nm))
        self.i = 0

    def next(self):
        b = self.bufs[self.i % len(self.bufs)]
        self.i += 1
        return b


def MM(P, out, lhsT, rhs, start, stop, reads, writes):
    P.add("pe", lambda e: e.matmul(out, lhsT, rhs, start=start, stop=stop), reads, writes)


def TR(P, out, in_, ident, reads, writes):
    P.add("pe", lambda e: e.transpose(out, in_, ident), reads, writes)


def ACT(P, out, in_, func, reads, writes, bias=None, scale=None):
    kw = {}
    if bias is not None:
        kw["bias"] = bias
    if scale is not None:
        kw["scale"] = scale
    P.add("act", lambda e: e.activation(out=out, in_=in_, func=func, **kw), reads, writes)


def TT(P, out, in0, in1, op, reads, writes, eng="dve"):
    P.add(eng, lambda e: e.tensor_tensor(out=out, in0=in0, in1=in1, op=op), reads, writes)


def TS(P, out, in0, s1, s2, op0, op1, reads, writes, eng="dve"):
    if op1 is None:
        P.add(eng, lambda e: e.tensor_scalar(out=out, in0=in0, scalar1=s1, scalar2=None, op0=op0),
              reads, writes)
    else:
        P.add(eng, lambda e: e.tensor_scalar(out=out, in0=in0, scalar1=s1, scalar2=s2, op0=op0, op1=op1),
              reads, writes)


def STT(P, out, in0, scalar, in1, op0, op1, reads, writes):
    P.add("dve", lambda e: e.scalar_tensor_tensor(out=out, in0=in0, scalar=scalar, in1=in1,
                                                  op0=op0, op1=op1), reads, writes)


def CP(P, eng, out, in_, reads, writes):
    P.add(eng, lambda e: e.tensor_copy(out=out, in_=in_), reads, writes)


def RECIP(P, out, in_, reads, writes):
    P.add("dve", lambda e: e.reciprocal(out=out, in_=in_), reads, writes)


def MSET(P, eng, ap, val, writes):
    P.add(eng, lambda e: e.memset(ap, val), (), writes)


def DMA(P, q, out, in_, reads, writes, key):
    P.add(q, lambda e: e.dma_start(out=out, in_=in_), reads, writes, dma=key)


GAIN_NAMES = ["ffn1_pre_g", "ffn1_post_g", "mix_pre_g", "hg_norm_g", "mix_post_g",
              "mem_pre_g", "mem_kv_g", "mem_post_g", "ffn2_pre_g", "ffn2_post_g"]
C_ID, C_U, C_BO, C_UF, C_ONE, C_SEL, C_SL, C_MD = 0, 128, 256, 384, 512, 640, 642, 706
C_TOT = 706 + 512


def make_consts():
    c = np.zeros((128, C_TOT), np.float32)
    i = np.arange(128)
    c[:, C_ID:C_ID + 128] = np.eye(128)
    same = (i[:, None] // 64) == (i[None, :] // 64)
    c[:, C_U:C_U + 128] = ((i[:, None] <= i[None, :]) & same)
    c[:, C_BO:C_BO + 128] = same
    c[:, C_UF:C_UF + 128] = (i[:, None] <= i[None, :])
    c[:, C_ONE:C_ONE + 128] = 1.0
    c[:, C_SEL + 0] = (i < 64)
    c[:, C_SEL + 1] = (i >= 64)
    j = np.arange(64)
    c[:64, C_SL:C_SL + 64] = (j[:, None] < j[None, :])
    jj = np.arange(512)
    c[:, C_MD:C_MD + 512] = np.where(jj[None, :] < i[:, None], NEG, 0.0)
    return c


class Ctx:
    pass


def rms_rstd(P, C, src, src_key, cols, ncols=512, dim=1024.0):
    sq, sqk = C.sq.next()
    ACT(P, sq[:, :, 0:ncols], src[:, :, cols], AF.Square, [src_key], [sqk])
    ps, psk = C.ps.next()
    for kt in range(8):
        MM(P, ps[:, 0:ncols], C.ones_bf[:], sq[:, kt, 0:ncols], kt == 0, kt == 7, [sqk, "consts_bf"], [psk])
    r, rk = C.rs.next()
    TS(P, r[:, 0:ncols], ps[:, 0:ncols], 1.0 / dim, EPS, ALU.mult, ALU.add, [psk], [rk])
    ACT(P, r[:, 0:ncols], r[:, 0:ncols], AF.Sqrt, [rk], [rk])
    RECIP(P, r[:, 0:ncols], r[:, 0:ncols], [rk], [rk])
    return r, rk


def load_w(P, C, w_d, kt_n, c0, ncols, dst_c0=0, buf=None):
    if buf is None:
        buf = C.wring.next()
    t, k = buf
    return buf


def wview(t, kt_n, width):
    return t[:, 0:kt_n * width].rearrange("p (k c) -> p k c", k=kt_n)


def wload(P, t, k, w_d, kt_n, width, c0, ncols, dst_c0):
    v = wview(t, kt_n, width)
    src = w_d[:, c0:c0 + ncols].rearrange("(k p) c -> p k c", p=128)
    DMA(P, "pool", v[:, :, dst_c0:dst_c0 + ncols], src, [], [k], "w_" + k)


def norm_h(P, C, hT, hk, hcols, xcols, g_col0, n=512):
    r, rk = rms_rstd(P, C, C.xT, "xT", xcols, n)
    for kt in range(8):
        STT(P, hT[:, kt, hcols], C.xT[:, kt, xcols], C.gains[:, g_col0 + kt:g_col0 + kt + 1],
            r[:, 0:n], ALU.mult, ALU.mult, ["xT", rk, "gains"], [hk])


def post_norm_add(P, C, yT, yk, ycols, xcols, g_col0, n=512):
    r, rk = rms_rstd(P, C, yT, yk, ycols, n)
    for kt in range(8):
        t, tk = C.tmp.next()
        STT(P, t[:, 0:n], yT[:, kt, ycols], C.gains[:, g_col0 + kt:g_col0 + kt + 1],
            r[:, 0:n], ALU.mult, ALU.mult, [yk, rk, "gains"], [tk])
        TT(P, C.xT[:, kt, xcols], C.xT[:, kt, xcols], t[:, 0:n], ALU.add, ["xT", tk], ["xT"])


def ffn(P, C, es, nc, w_in_d, w_dn_d, g_pre, g_post_h):
    yT = es.enter_context(nc.sbuf_tensor(_nm("ffn_hy"), [128, 8, 1024], F32))
    hT = yT.bitcast(BF16)
    actT = es.enter_context(nc.sbuf_tensor(_nm("ffn_actT"), [128, 22, 1024], BF16))
    sg_ring = Ring(nc, es, "ffn_sg", 2, [128, 512], F32)
    for half in range(2):
        t0 = half * 1024
        for tt in range(2):
            norm_h(P, C, hT, "ffn_hy", slice(tt * 512, (tt + 1) * 512),
                   slice(t0 + tt * 512, t0 + (tt + 1) * 512), g_pre)
        for ch in range(11):
            wt, wk = C.wring.next()
            wload(P, wt, wk, w_in_d, 8, 512, ch * 256, 256, 0)
            wload(P, wt, wk, w_in_d, 8, 512, DFF + ch * 256, 256, 256)
            wv = wview(wt, 8, 512)
            for jj in range(2):
                j = ch * 2 + jj
                for tt in range(2):
                    tc_ = slice(tt * 512, (tt + 1) * 512)
                    pg, pgk = C.ps.next()
                    for kt in range(8):
                        MM(P, pg[:], wv[:, kt, jj * 128:(jj + 1) * 128], hT[:, kt, tc_], kt == 0, kt == 7,
                           [wk, "ffn_hy"], [pgk])
                    pu, puk = C.ps.next()
                    for kt in range(8):
                        MM(P, pu[:], wv[:, kt, 256 + jj * 128:256 + (jj + 1) * 128], hT[:, kt, tc_],
                           kt == 0, kt == 7, [wk, "ffn_hy"], [puk])
                    sg, sgk = sg_ring.next()
                    ACT(P, sg[:], pg[:], AF.Silu, [pgk], [sgk])
                    TT(P, actT[:, j, tc_], pu[:], sg[:], ALU.mult, [puk, sgk], ["ffn_actT%d" % j])
        akeys = ["ffn_actT%d" % j for j in range(22)]
        for dc in range(4):
            wt, wk = C.wring.next()
            wload(P, wt, wk, w_dn_d, 22, 256, dc * 256, 256, 0)
            wv = wview(wt, 22, 256)
            for dd in range(2):
                d = dc * 2 + dd
                for tt in range(2):
                    tc_ = slice(tt * 512, (tt + 1) * 512)
                    py, pyk = C.ps.next()
                    for j in range(22):
                        MM(P, py[:], wv[:, j, dd * 128:(dd + 1) * 128], actT[:, j, tc_], j == 0, j == 21,
                           [wk] + akeys, [pyk])
                    ACT(P, yT[:, d, tc_], py[:], AF.Copy, [pyk], ["ffn_hy"])
        for tt in range(2):
            post_norm_add(P, C, yT, "ffn_hy", slice(tt * 512, (tt + 1) * 512),
                          slice(t0 + tt * 512, t0 + (tt + 1) * 512), g_post_h)


def build_program(dbg=False):
    _RANK.clear()
    nc = bass.Bass("TRN2", target_bir_lowering=False)

    def din(name, shape, dt=F32):
        return nc.dram_tensor(name, shape, dt, kind="ExternalInput").ap()

    xT_d = din("xT", [D, S_OWN])
    memT_d = din("memT", [D, 256])
    ffn1_in = din("ffn1_w_in", [D, 2 * DFF])
    ffn1_dn = din("ffn1_w_down", [DFF, D])
    ffn2_in = din("ffn2_w_in", [D, 2 * DFF])
    ffn2_dn = din("ffn2_w_down", [DFF, D])
    w_loc_d = din("w_loc", [D, 3072])
    w_hg_d = din("w_hg", [D, 768])
    w_fq_d = din("w_fq", [D, 256])
    w_fk_d = din("w_fk", [D, 256])
    w_fv_d = din("w_fv", [D, 258])
    w_a_d = din("w_branch_a", [D, D])
    w_b_d = din("w_branch_b", [D, D])
    w_o_d = din("w_out", [D, D])
    w_mq_d = din("w_mq", [D, D])
    w_mkv_d = din("w_mkv", [D, 2 * D])
    w_mo_d = din("w_mo", [D, D])
    gains_d = din("gains", [128, 96])
    consts_d = din("consts", [128, C_TOT])
    lbl_d = din("lbl", [128, 512])
    fbias_d = din("fbias", [128, 2])
    outT_d = nc.dram_tensor("outT", [D, S_OWN], F32, kind="ExternalOutput").ap()
    dbg_x1 = dbg_o = None
    if dbg:
        dbg_x1 = nc.dram_tensor("dbg_x1", [D, S_OWN], F32, kind="ExternalOutput").ap()
        dbg_o = nc.dram_tensor("dbg_o", [512, S_ALL], F32, kind="ExternalOutput").ap()
        dbg_x2 = nc.dram_tensor("dbg_x2", [D, S_OWN], F32, kind="ExternalOutput").ap()
        dbg_x3 = nc.dram_tensor("dbg_x3", [D, S_OWN], F32, kind="ExternalOutput").ap()

    ag1_in = [nc.dram_tensor("ag1_in%d" % i, [256, S_OWN], BF16).ap() for i in range(4)]
    ag1_out = [nc.dram_tensor("ag1_out%d" % i, [1024, S_OWN], BF16).ap() for i in range(4)]
    ag2_in = [[nc.dram_tensor("ag2_in%d_%d" % (q, tq), [128, S_OWN], F32).ap() for tq in range(4)] for q in range(4)]
    ag2_out = [nc.dram_tensor("ag2_out%d" % q, [4 * 512, S_OWN], F32).ap() for q in range(4)]
    x_spill = nc.dram_tensor("x_spill", [D, S_OWN], F32).ap()
    crow_d = nc.dram_tensor("crow_d", [3, 64 * 128], BF16).ap()
    RG = [[0, 1, 2, 3], [4, 5, 6, 7]]

    with contextlib.ExitStack() as es_all:
        P = Prog(nc, es_all)
        C = Ctx()
        C.consts = es_all.enter_context(nc.sbuf_tensor(_nm("consts_sb"), [128, C_TOT], F32))
        C.consts_bf = es_all.enter_context(nc.sbuf_tensor(_nm("consts_bf_sb"), [128, C_TOT], BF16))
        C.gains = es_all.enter_context(nc.sbuf_tensor(_nm("gains_sb"), [128, 96], F32))
        C.ones_bf = C.consts_bf[:, C_ONE:C_ONE + 128]
        C.ident_bf = C.consts_bf[:, C_ID:C_ID + 128]
        C.ps = Ring(nc, es_all, "ps", 8, [128, 512], F32, psum=True)
        DMA(P, "sp", C.consts[:], consts_d, [], ["consts"], "consts")
        DMA(P, "sp", C.gains[:], gains_d, [], ["gains"], "gains")
        CP(P, "dve", C.consts_bf[:], C.consts[:], ["consts"], ["consts_bf"])
        for g0 in (8, 72):
            TS(P, C.gains[:, g0:g0 + 8], C.gains[:, g0:g0 + 8], 0.5, None, ALU.mult, None, ["gains"], ["gains"])

        with contextlib.ExitStack() as es:
            C.xT = es.enter_context(nc.sbuf_tensor(_nm("xT_sb"), [128, 8, S_OWN], F32))
            DMA(P, "sp", C.xT[:], xT_d.rearrange("(k p) t -> p k t", p=128), [], ["xT"], "xT")
            C.sq = Ring(nc, es, "sq", 1, [128, 8, 512], BF16)
            C.rs = Ring(nc, es, "rs", 2, [128, 512], F32)
            C.tmp = Ring(nc, es, "tmp", 2, [128, 512], F32)
            C.wring = Ring(nc, es, "wr", 3, [128, 5632], BF16)
            with contextlib.ExitStack() as es2:
                ffn(P, C, es2, nc, ffn1_in, ffn1_dn, 0, 8)
            P.barrier()
            if dbg:
                DMA(P, "sp", dbg_x1.rearrange("(k p) t -> p k t", p=128), C.xT[:], ["xT"], [], "dbg")
            if STOP_AFTER == 1:
                DMA(P, "sp", outT_d.rearrange("(k p) t -> p k t", p=128), C.xT[:], ["xT"], [], "out")
                P.wait_dma_at_end("out")
                if dbg:
                    P.wait_dma_at_end("dbg")
                P.emit()
                build_program.stats = P.stats
                return nc
            with contextlib.ExitStack() as es2:
                hT = es2.enter_context(nc.sbuf_tensor(_nm("mix_hT"), [128, 8, S_OWN], BF16))
                for tt in range(4):
                    sl = slice(tt * 512, (tt + 1) * 512)
                    norm_h(P, C, hT, "mix_hT", sl, sl, 16)
                for ck in range(4):
                    DMA(P, "sp", ag1_in[ck].rearrange("(k p) t -> p k t", p=128), hT[:, 2 * ck:2 * ck + 2, :],
                        ["mix_hT"], ["ag1_in"], "ag1_in")
                DMA(P, "sp", x_spill.rearrange("(k p) t -> p k t", p=128), C.xT[:], ["xT"], ["x_spill"], "x_spill")
                for ck in range(4):
                    P.add("pool", lambda e, ck=ck: e.collective_compute(
                        "AllGather", ALU.bypass, replica_groups=RG, ins=[ag1_in[ck].opt()], outs=[ag1_out[ck].opt()]),
                        ["ag1_in"], ["ag1_out"], dma="agc", inc=1)
                P.barrier()
        if STOP_AFTER == 2:
            DMA(P, "sp", outT_d, x_spill, [], [], "out")
            P.wait_dma_at_end("out")
            if dbg:
                P.wait_dma_at_end("dbg")
            P.emit()
            build_program.stats = P.stats
            return nc
        with contextlib.ExitStack() as es:
            mixers(P, C, es, nc, ag1_out, ag2_in, w_hg_d, w_fq_d, w_fk_d, w_fv_d, lbl_d, fbias_d, crow_d, ag2_out, RG)
            P.barrier()
        if dbg:
            for q in range(4):
                for tq in range(4):
                    DMA(P, "sp", dbg_o[q * 128:(q + 1) * 128, tq * 2048:(tq + 1) * 2048], ag2_in[q][tq], [], [], "dbg")
        if STOP_AFTER == 3:
            DMA(P, "sp", outT_d, x_spill, [], [], "out")
            P.wait_dma_at_end("out")
            if dbg:
                P.wait_dma_at_end("dbg")
            P.emit()
            build_program.stats = P.stats
            return nc
        with contextlib.ExitStack() as es:
            C.xT = es.enter_context(nc.sbuf_tensor(_nm("xTb"), [128, 8, S_OWN], F32))
            DMA(P, "sp", C.xT[:], x_spill.rearrange("(k p) t -> p k t", p=128), [], ["xT"], "xT2")
            C.sq = Ring(nc, es, "sqb", 1, [128, 8, 512], BF16)
            C.rs = Ring(nc, es, "rsb", 2, [128, 512], F32)
            C.tmp = Ring(nc, es, "tmpb", 2, [128, 512], F32)
            C.wring = Ring(nc, es, "wrb", 3, [128, 5632], BF16)
            with contextlib.ExitStack() as es2:
                post_mixer(P, C, es2, nc, ag2_out, w_loc_d, w_a_d, w_b_d, w_o_d)
            P.barrier()
            if dbg:
                DMA(P, "sp", dbg_x2.rearrange("(k p) t -> p k t", p=128), C.xT[:], ["xT"], [], "dbg")
            with contextlib.ExitStack() as es2:
                mem_attn(P, C, es2, nc, memT_d, w_mq_d, w_mkv_d, w_mo_d)
            P.barrier()
            if dbg:
                DMA(P, "sp", dbg_x3.rearrange("(k p) t -> p k t", p=128), C.xT[:], ["xT"], [], "dbg")
            with contextlib.ExitStack() as es2:
                ffn(P, C, es2, nc, ffn2_in, ffn2_dn, 64, 72)
            DMA(P, "sp", outT_d.rearrange("(k p) t -> p k t", p=128), C.xT[:], ["xT"], [], "out")
            P.wait_dma_at_end("out")
            if dbg:
                P.wait_dma_at_end("dbg")
        P.emit()
        build_program.stats = P.stats
    return nc


def stream_proj(P, C, w_d, c0, ncols_total, rhs, rhs_keys, epilogue, kt_n=8):
    done = 0
    i = 0
    while done < ncols_total:
        n = min(512, ncols_total - done)
        wt, wk = C.wring.next()
        wload(P, wt, wk, w_d, kt_n, 512, c0 + done, n, 0)
        wv = wview(wt, kt_n, 512)
        for jj in range(n // 128):
            ps, psk = C.ps.next()
            for kt in range(kt_n):
                MM(P, ps[:], wv[:, kt, jj * 128:(jj + 1) * 128], rhs[:, kt, :], kt == 0, kt == kt_n - 1,
                   [wk] + rhs_keys, [psk])
            epilogue(i, ps, psk)
            i += 1
        done += n


def post_mixer(P, C, es, nc, ag2_out, w_loc_d, w_a_d, w_b_d, w_o_d):
    sb = lambda n, s, d: es.enter_context(nc.sbuf_tensor(_nm(n), s, d))
    oa = sb("pm_oa", [128, 8, 512], F32)
    ob = sb("pm_ob", [128, 8, 512], BF16)
    hT = sb("pm_hT", [128, 8, 512], BF16)
    ga = sb("pm_ga", [128, 8, 512], F32)
    gB = sb("pm_gB", [128, 8, 512], F32)
    oan = sb("pm_oan", [128, 8, 512], BF16)
    yg = sb("pm_yg", [128, 8, 512], BF16)
    ya = ga
    zT = oa
    o_own = nc.dram_tensor("o_own", [2048, S_OWN], F32).ap()
    dst = o_own.rearrange("(a r q p) t -> a q r p t", a=2, r=4, q=2, p=128)
    for q in range(4):
        def ld(e, q=q):
            roff = _rank(e)
            return e.dma_start(out=dst[q // 2, q % 2],
                               in_=ag2_out[q][bass.ds(roff, 512), :].rearrange("(r p) t -> r p t", p=128))
        P.add("pool", ld, [], ["o_own"], dma="o_own")
    for tt in range(4):
        sl = slice(tt * 512, (tt + 1) * 512)

        DMA(P, "sp", oa[:], o_own[0:1024, sl].rearrange("(h p) t -> p h t", p=128), ["o_own"], ["pm_oa"], "pm_oa")
        DMA(P, "pool", ob[:], o_own[1024:2048, sl].rearrange("(h p) t -> p h t", p=128), ["o_own"], ["pm_ob"], "pm_ob")
        norm_h(P, C, hT, "pm_hT", slice(0, 512), sl, 16)

        def ep_ga(i, ps, psk):
            ACT(P, ga[:, i, :], ps[:], AF.Silu, [psk], ["pm_ga"])
        stream_proj(P, C, w_loc_d, 0, 1024, hT, ["pm_hT"], ep_ga)
        r, rk = rms_rstd(P, C, oa, "pm_oa", slice(0, 512))
        for kt in range(8):
            t, tk = C.tmp.next()
            STT(P, t[:], oa[:, kt, :], C.gains[:, 24 + kt:25 + kt], r[:], ALU.mult, ALU.mult,
                ["pm_oa", rk, "gains"], [tk])
            TT(P, oan[:, kt, :], t[:], ga[:, kt, :], ALU.mult, [tk, "pm_ga"], ["pm_oan"])

        def ep_a(i, ps, psk):
            ACT(P, ya[:, i, :], ps[:], AF.Copy, [psk], ["pm_ga"])
        stream_proj(P, C, w_a_d, 0, 1024, oan, ["pm_oan"], ep_a)

        def ep_gA(i, ps, psk):
            t, tk = C.tmp.next()
            ACT(P, t[:], ps[:], AF.Sigmoid, [psk, "gains"], [tk], bias=C.gains[:, 80 + i:81 + i])
            TT(P, ya[:, i, :], ya[:, i, :], t[:], ALU.mult, [tk, "pm_ga"], ["pm_ga"])
        stream_proj(P, C, w_loc_d, 1024, 1024, hT, ["pm_hT"], ep_gA)

        def ep_gB(i, ps, psk):
            ACT(P, gB[:, i, :], ps[:], AF.Sigmoid, [psk, "gains"], ["pm_gB"], bias=C.gains[:, 88 + i:89 + i])
        stream_proj(P, C, w_loc_d, 2048, 1024, hT, ["pm_hT"], ep_gB)

        def ep_b(i, ps, psk):
            t, tk = C.tmp.next()
            TT(P, t[:], ps[:], gB[:, i, :], ALU.mult, [psk, "pm_gB"], [tk])
            TT(P, yg[:, i, :], t[:], ya[:, i, :], ALU.add, [tk, "pm_ga"], ["pm_yg"])
        stream_proj(P, C, w_b_d, 0, 1024, ob, ["pm_ob"], ep_b)

        def ep_o(i, ps, psk):
            ACT(P, zT[:, i, :], ps[:], AF.Copy, [psk], ["pm_oa"])
        stream_proj(P, C, w_o_d, 0, 1024, yg, ["pm_yg"], ep_o)
        post_norm_add(P, C, zT, "pm_oa", slice(0, 512), sl, 32)


def mem_attn(P, C, es, nc, memT_d, w_mq_d, w_mkv_d, w_mo_d):
    sb = lambda n, s, d: es.enter_context(nc.sbuf_tensor(_nm(n), s, d))
    memT = sb("ma_memT", [128, 8, 256], F32)
    memn = sb("ma_memn", [128, 8, 256], BF16)
    kmT = sb("ma_kmT", [128, 8, 256], BF16)
    vm = sb("ma_vm", [128, 2, 1024], BF16)
    hT = sb("ma_hT", [128, 8, 512], BF16)
    qmT = sb("ma_qmT", [128, 8, 512], BF16)
    omT = sb("ma_omT", [128, 8, 512], BF16)
    yT = sb("ma_yT", [128, 8, 512], F32)
    pT = Ring(nc, es, "ma_pT", 2, [128, 2, 512], BF16)
    rec = Ring(nc, es, "ma_rec", 2, [128, 512], F32)
    DMA(P, "sp", memT[:], memT_d.rearrange("(k p) t -> p k t", p=128), [], ["ma_memT"], "ma_memT")
    sq, sqk = C.sq.next()
    ACT(P, sq[:, :, 0:256], memT[:], AF.Square, ["ma_memT"], [sqk])
    ps, psk = C.ps.next()
    for kt in range(8):
        MM(P, ps[:, 0:256], C.ones_bf[:], sq[:, kt, 0:256], kt == 0, kt == 7, [sqk, "consts_bf"], [psk])
    r, rk = C.rs.next()
    TS(P, r[:, 0:256], ps[:, 0:256], 1.0 / 1024.0, EPS, ALU.mult, ALU.add, [psk], [rk])
    ACT(P, r[:, 0:256], r[:, 0:256], AF.Sqrt, [rk], [rk])
    RECIP(P, r[:, 0:256], r[:, 0:256], [rk], [rk])
    for kt in range(8):
        STT(P, memn[:, kt, :], memT[:, kt, :], C.gains[:, 48 + kt:49 + kt], r[:, 0:256], ALU.mult, ALU.mult,
            ["ma_memT", rk, "gains"], ["ma_memn"])
    for ch in range(2):
        wt, wk = C.wring.next()
        wload(P, wt, wk, w_mkv_d, 8, 512, ch * 512, 512, 0)
        wv = wview(wt, 8, 512)
        for jj in range(4):
            ps, psk = C.ps.next()
            for kt in range(8):
                MM(P, ps[:, 0:256], wv[:, kt, jj * 128:(jj + 1) * 128], memn[:, kt, :], kt == 0, kt == 7,
                   [wk, "ma_memn"], [psk])
            ACT(P, kmT[:, ch * 4 + jj, :], ps[:, 0:256], AF.Copy, [psk], ["ma_kmT"])
    for ch in range(2):
        wt, wk = C.wring.next()
        wload(P, wt, wk, w_mkv_d, 8, 512, 1024 + ch * 512, 512, 0)
        wv = wview(wt, 8, 512)
        for mt in range(2):
            ps, psk = C.ps.next()
            for kt in range(8):
                MM(P, ps[:], memn[:, kt, mt * 128:(mt + 1) * 128], wv[:, kt, :], kt == 0, kt == 7,
                   [wk, "ma_memn"], [psk])
            ACT(P, vm[:, mt, ch * 512:(ch + 1) * 512], ps[:], AF.Copy, [psk], ["ma_vm"])
    scale = 1.0 / math.sqrt(256.0)
    for tt in range(4):
        sl = slice(tt * 512, (tt + 1) * 512)
        norm_h(P, C, hT, "ma_hT", slice(0, 512), sl, 40)

        def ep_q(i, ps, psk):
            ACT(P, qmT[:, i, :], ps[:], AF.Copy, [psk], ["ma_qmT"])
        stream_proj(P, C, w_mq_d, 0, 1024, hT, ["ma_hT"], ep_q)
        for hh in range(4):
            p_t, pk = pT.next()
            for mt in range(2):
                ps, psk = C.ps.next()
                for a in range(2):
                    MM(P, ps[:], kmT[:, 2 * hh + a, mt * 128:(mt + 1) * 128], qmT[:, 2 * hh + a, :],
                       a == 0, a == 1, ["ma_kmT", "ma_qmT"], [psk])
                ACT(P, p_t[:, mt, :], ps[:], AF.Exp, [psk], [pk], scale=scale)
            ps, psk = C.ps.next()
            for mt in range(2):
                MM(P, ps[:], C.ones_bf[:], p_t[:, mt, :], mt == 0, mt == 1, [pk, "consts_bf"], [psk])
            rc, rck = rec.next()
            RECIP(P, rc[:], ps[:], [psk], [rck])
            for a in range(2):
                ps, psk = C.ps.next()
                for mt in range(2):
                    MM(P, ps[:], vm[:, mt, (2 * hh + a) * 128:(2 * hh + a + 1) * 128], p_t[:, mt, :],
                       mt == 0, mt == 1, [pk, "ma_vm"], [psk])
                TT(P, omT[:, 2 * hh + a, :], ps[:], rc[:], ALU.mult, [psk, rck], ["ma_omT"])

        def ep_o(i, ps, psk):
            ACT(P, yT[:, i, :], ps[:], AF.Copy, [psk], ["ma_yT"])
        stream_proj(P, C, w_mo_d, 0, 1024, omT, ["ma_omT"], ep_o)
        post_norm_add(P, C, yT, "ma_yT", slice(0, 512), sl, 56)


def mixers(P, C, es, nc, ag1_out, ag2_in, w_hg_d, w_fq_d, w_fk_d, w_fv_d, lbl_d, fbias_d, crow_d, ag2_out, RG):
    def gather(q, tq):
        P.add("pool", lambda e: e.collective_compute(
            "AllGather", ALU.bypass, replica_groups=RG, ins=[ag2_in[q][tq].opt()],
            outs=[ag2_out[q][tq * 512:(tq + 1) * 512, :].opt()]),
            ["ag2_in_%d_%d" % (q, tq)], ["ag2_out"], dma="agc", inc=1)

    sb = lambda n, s, d: es.enter_context(nc.sbuf_tensor(_nm(n), s, d))
    cf = C.consts
    U_f = cf[:, C_U:C_U + 128]
    BO_f = cf[:, C_BO:C_BO + 128]
    UF_f = cf[:, C_UF:C_UF + 128]
    ONE_f = cf[:, C_ONE:C_ONE + 128]
    SEL_f = cf[:, C_SEL:C_SEL + 2]
    SL_f = cf[0:64, C_SL:C_SL + 64]
    MD_bf = C.consts_bf[:, C_MD:C_MD + 512]
    whg = sb("mx_whg", [128, 8, 768], BF16)
    wfq = sb("mx_wfq", [128, 8, 256], BF16)
    wfk = sb("mx_wfk", [128, 8, 256], BF16)
    wfv = sb("mx_wfv", [128, 8, 258], BF16)
    for t, d_, n in ((whg, w_hg_d, "mx_whg"), (wfq, w_fq_d, "mx_wfq"), (wfk, w_fk_d, "mx_wfk"), (wfv, w_fv_d, "mx_wfv")):
        DMA(P, "pool", t[:], d_.rearrange("(k p) c -> p k c", p=128), [], [n], n)
    lbl = sb("mx_lbl", [128, 512], F32)
    fbias = sb("mx_fbias", [128, 2], F32)
    DMA(P, "sp", lbl[:], lbl_d, [], ["mx_lbl"], "mx_lbl")
    DMA(P, "sp", fbias[:], fbias_d, [], ["mx_fbias"], "mx_fbias")
    lb = sb("mx_lb", [128, 256], F32)
    oml = sb("mx_oml", [128, 256], F32)
    TT(P, lb[:], lbl[:, 0:256], lbl[:, 256:512], ALU.subtract, ["mx_lbl"], ["mx_lb"])
    ACT(P, lb[:], lb[:], AF.Sigmoid, ["mx_lb"], ["mx_lb"])
    TS(P, oml[:], lb[:], -1.0, 1.0, ALU.mult, ALU.add, ["mx_lb"], ["mx_oml"])
    lbw = sb("mx_lbw", [128, 4, 128], F32)
    omlw = sb("mx_omlw", [128, 4, 128], F32)
    U4 = sb("mx_U4", [128, 4, 128], F32)
    for s_ in range(4):
        CP(P, "dve", U4[:, s_, :], U_f, ["consts"], ["mx_U4"])
    qT = sb("mx_qT", [128, S_ALL], BF16)
    kT = sb("mx_kT", [128, S_ALL], BF16)
    Vp = sb("mx_Vp", [128, 64, 130], BF16)
    fb = sb("mx_fb", [128, 64], F32)
    MSET(P, "dve", Vp[:, :, 128:130], 1.0, ["mx_Vp"])
    hring = Ring(nc, es, "mx_h", 2, [128, 8, 512], BF16)
    f32r = Ring(nc, es, "mx_f", 11, [128, 4, 128], F32)
    bfr = Ring(nc, es, "mx_b", 11, [128, 4, 128], BF16)
    S32 = sb("mx_S32", [128, 128], F32)
    Sbf = Ring(nc, es, "mx_Sbf", 2, [128, 128], BF16)
    dT = Ring(nc, es, "mx_dT", 2, [128, 8], F32)
    ostage = Ring(nc, es, "mx_ost", 2, [128, 512], F32)
    crow = sb("mx_crow", [96, S_ALL], BF16)
    biasG = sb("mx_biasG", [128, 16, 64], F32)
    ccol = sb("mx_ccol", [128, 64], F32)
    offbc = sb("mx_offbc", [128, 64], F32)
    ls = sb("mx_ls", [128, 64], F32)
    totbc = sb("mx_totbc", [64, 128], F32)
    cd = sb("mx_cd", [128, 64], F32)
    c3 = sb("mx_c3", [128, 3, 64], BF16)
    c3T = sb("mx_c3T", [64, 3, 128], BF16)
    r1 = sb("mx_r1", [128, 64], F32)
    l3 = sb("mx_l3", [128, 3, 64], BF16)
    tb3 = sb("mx_tb3", [64, 3, 128], BF16)
    tbr = sb("mx_tbr", [64, 128], F32)
    pTr = Ring(nc, es, "mx_pT", 3, [128, 512], BF16)
    obr = Ring(nc, es, "mx_ob", 2, [128, 128], BF16)
    recr = Ring(nc, es, "mx_rec", 2, [128, 1], F32)
    obT = Ring(nc, es, "mx_obT", 2, [128, 512], F32)
    MSET(P, "dve", crow[:], 0.0, ["mx_crow"])
    ones96 = C.consts_bf[0:96, C_ONE:C_ONE + 128]
    psb = C.ps.bufs
    PS = lambda i: psb[i]
    scale = 1.0 / math.sqrt(128.0)

    for h in range(2):
        for s_ in range(4):
            CP(P, "dve", lbw[:, s_, :], lb[:, h * 128:(h + 1) * 128], ["mx_lb"], ["mx_lbw"])
            CP(P, "dve", omlw[:, s_, :], oml[:, h * 128:(h + 1) * 128], ["mx_oml"], ["mx_omlw"])
        MSET(P, "dve", S32[:], 0.0, ["mx_S32"])
        sb_t, sb_k = Sbf.next()
        MSET(P, "dve", sb_t[:], 0.0, [sb_k])
        for T in range(16):
            rr, t0 = T // 4, (T % 4) * 512
            ht, hk = hring.next()
            for ck in range(4):
                srch = ag1_out[ck][rr * 256:(rr + 1) * 256, t0:t0 + 512].rearrange("(k p) t -> p k t", p=128)
                DMA(P, "sp", ht[:, 2 * ck:2 * ck + 2, :], srch, ["ag1_out"], [hk], "h_" + hk)
            if not (MIX_PARTS & 1):
                continue
            for (w_, dst, nm, wn) in ((wfq, qT, "mx_qT", "mx_wfq"), (wfk, kT, "mx_kT", "mx_wfk")):
                ps, psk = PS(6)
                for kt in range(8):
                    MM(P, ps[:], w_[:, kt, h * 128:(h + 1) * 128], ht[:, kt, :], kt == 0, kt == 7,
                       [wn, hk], [psk])
                ACT(P, dst[:, T * 512:(T + 1) * 512], ps[:], AF.Copy, [psk], [nm])
            ps, psk = PS(7)
            for s_ in range(4):
                for kt in range(8):
                    MM(P, ps[:, s_ * 128:s_ * 128 + 128], ht[:, kt, s_ * 128:(s_ + 1) * 128],
                       wfv[:, kt, h * 129:h * 129 + 128], kt == 0, kt == 7, ["mx_wfv", hk], [psk])
            ACT(P, Vp[:, T * 4:(T + 1) * 4, 0:128], ps[:].rearrange("p (s c) -> p s c", s=4), AF.Copy,
                [psk], ["mx_Vp"])
            ps, psk = PS(7)
            for s_ in range(4):
                for kt in range(8):
                    MM(P, ps[:, s_:s_ + 1], ht[:, kt, s_ * 128:(s_ + 1) * 128],
                       wfv[:, kt, h * 129 + 128:h * 129 + 129], kt == 0, kt == 7, ["mx_wfv", hk], [psk])
            ACT(P, fb[:, T * 4:(T + 1) * 4], ps[:, 0:4], AF.Copy, [psk], ["mx_fb"])
            if not (MIX_PARTS & 2):
                continue
            zq, zqk = PS(0)
            zf, zfk = PS(1)
            zi, zik = PS(2)
            for (z_, zk_, off) in ((zq, zqk, 0), (zf, zfk, 128), (zi, zik, 256)):
                for s_ in range(4):
                    for kt in range(8):
                        MM(P, z_[:, s_ * 128:(s_ + 1) * 128], ht[:, kt, s_ * 128:(s_ + 1) * 128],
                           whg[:, kt, h * 384 + off:h * 384 + off + 128], kt == 0, kt == 7,
                           ["mx_whg", hk], [zk_])
            qs, qsk = f32r.next()
            sg, sgk = f32r.next()
            Ib, Ibk = bfr.next()
            W = lambda t: t[:].rearrange("p s c -> p (s c)")
            ACT(P, W(qs), zq[:], AF.Silu, [zqk], [qsk])
            ACT(P, W(sg), zf[:], AF.Sigmoid, [zfk], [sgk])
            ACT(P, W(Ib), zi[:], AF.Copy, [zik], [Ibk])
            f_, fk_ = f32r.next()
            TT(P, W(f_), W(sg), W(omlw), ALU.mult, [sgk, "mx_omlw"], [fk_])
            TT(P, W(f_), W(f_), W(lbw), ALU.add, [fk_, "mx_lbw"], [fk_])
            lf, lfk = f32r.next()
            ACT(P, W(lf), W(f_), AF.Ln, [fk_], [lfk])
            kk, kkk = f32r.next()
            TS(P, W(kk), W(f_), -1.0, 1.0, ALU.mult, ALU.add, [fk_], [kkk])
            lfh, lfhk = bfr.next()
            lfl, lflk = bfr.next()
            lfr, lfrk = f32r.next()
            CP(P, "dve", W(lfh), W(lf), [lfk], [lfhk])
            TT(P, W(lfr), W(lf), W(lfh), ALU.subtract, [lfk, lfhk], [lfrk])
            CP(P, "dve", W(lfl), W(lfr), [lfrk], [lflk])
            U_b = C.consts_bf[:, C_U:C_U + 128]
            BO_b = C.consts_bf[:, C_BO:C_BO + 128]
            SEL_b = C.consts_bf[:, C_SEL:C_SEL + 2]
            bps, bpsk = PS(3)
            MM(P, bps[:], U_b, W(lfh), True, False, [lfhk, "consts_bf"], [bpsk])
            MM(P, bps[:], U_b, W(lfl), False, True, [lflk, "consts_bf"], [bpsk])
            blps, blpsk = PS(5)
            MM(P, blps[:], BO_b, W(lfh), True, False, [lfhk, "consts_bf"], [blpsk])
            MM(P, blps[:], BO_b, W(lfl), False, True, [lflk, "consts_bf"], [blpsk])
            btps, btpsk = PS(6)
            for s_ in range(4):
                MM(P, btps[:, 2 * s_:2 * s_ + 2], lfh[:, s_, :], SEL_b, True, False, [lfhk, "consts_bf"], [btpsk])
                MM(P, btps[:, 2 * s_:2 * s_ + 2], lfl[:, s_, :], SEL_b, False, True, [lflk, "consts_bf"], [btpsk])
            eb, ebk = f32r.next()
            enb, enbk = f32r.next()
            ebl, eblk = f32r.next()
            d_t, dk_ = dT.next()
            ACT(P, W(eb), bps[:], AF.Exp, [bpsk], [ebk])
            ACT(P, W(enb), bps[:], AF.Exp, [bpsk], [enbk], scale=-1.0)
            ACT(P, W(ebl), blps[:], AF.Exp, [blpsk], [eblk])
            ACT(P, d_t[:], btps[:, 0:8], AF.Exp, [btpsk], [dk_])
            Qt, Qtk = bfr.next()
            Kt32, Kt32k = f32r.next()
            Ktb, Ktbk = bfr.next()
            Kh0, Kh0k = bfr.next()
            Kh1, Kh1k = bfr.next()
            TT(P, W(Qt), W(qs), W(eb), ALU.mult, [qsk, ebk], [Qtk])
            TT(P, W(Kt32), W(kk), W(enb), ALU.mult, [kkk, enbk], [Kt32k])
            CP(P, "dve", W(Ktb), W(Kt32), [Kt32k], [Ktbk])
            STT(P, W(Kh0), W(Kt32), SEL_f[:, 0:1], W(ebl), ALU.mult, ALU.mult, [Kt32k, eblk, "consts"], [Kh0k])
            STT(P, W(Kh1), W(Kt32), SEL_f[:, 1:2], W(ebl), ALU.mult, ALU.mult, [Kt32k, eblk, "consts"], [Kh1k])
            tp, tpk = PS(0)
            tpb = tp.bitcast(BF16)
            tp2, tp2k = PS(7)
            tpb2 = tp2.bitcast(BF16)
            for s_ in range(4):
                TR(P, tpb[:, s_ * 128:(s_ + 1) * 128], Qt[:, s_, :], C.ident_bf, [Qtk, "consts_bf"], [tpk])
                TR(P, tpb2[:, s_ * 128:(s_ + 1) * 128], Ktb[:, s_, :], C.ident_bf,
                   [Ktbk, "consts_bf"], [tp2k])
            QtT, QtTk = bfr.next()
            KtT, KtTk = bfr.next()
            ACT(P, W(QtT), tpb[:, 0:512], AF.Copy, [tpk], [QtTk])
            CP(P, "dve", W(KtT), tpb2[:, 0:512], [tp2k], [KtTk])
            aps, apsk = PS(1)
            for s_ in range(4):
                MM(P, aps[:, s_ * 128:(s_ + 1) * 128], KtT[:, s_, :], QtT[:, s_, :], True, True,
                   [KtTk, QtTk], [apsk])
            Am, Amk = bfr.next()
            TT(P, W(Am), aps[:], W(U4), ALU.mult, [apsk, "mx_U4"], [Amk])
            ops_, opsk = PS(4)
            for s_ in range(4):
                MM(P, ops_[:, s_ * 128:(s_ + 1) * 128], Ib[:, s_, :], Am[:, s_, :], True, False,
                   [Ibk, Amk], [opsk])
                for c_ in range(2):
                    col = s_ * 128 + c_ * 64
                    MM(P, ops_[:, col:col + 64], sb_t[:], QtT[:, s_, c_ * 64:(c_ + 1) * 64], False,
                       c_ == 1, [sb_k, QtTk], [opsk])
                    sups, supsk = PS(5)
                    Khc, Khck = (Kh0, Kh0k) if c_ == 0 else (Kh1, Kh1k)
                    MM(P, sups[:, 0:128], Khc[:, s_, :], Ib[:, s_, :], True, True, [Khck, Ibk], [supsk])
                    STT(P, S32[:], S32[:], d_t[:, 2 * s_ + c_:2 * s_ + c_ + 1], sups[:, 0:128],
                        ALU.mult, ALU.add, ["mx_S32", dk_, supsk], ["mx_S32"])
                    sb_t, sb_k = Sbf.next()
                    ACT(P, sb_t[:], S32[:], AF.Copy, ["mx_S32"], [sb_k])
            ot, otk = ostage.next()
            ACT(P, ot[:], ops_[:], AF.Copy, [opsk], [otk])
            DMA(P, "sp", ag2_in[h][T // 4][:, (T % 4) * 512:(T % 4 + 1) * 512], ot[:], [otk],
                ["ag2_in_%d_%d" % (h, T // 4)], "o_" + otk)
            if T % 4 == 3:
                gather(h, T // 4)
        P.barrier()
        if not (MIX_PARTS & 4):
            continue
        ACT(P, ls[:], fb[:], AF.Sigmoid, ["mx_fb", "mx_fbias"], ["mx_ls"], bias=fbias[:, h:h + 1])
        ACT(P, ls[:], ls[:], AF.Ln, ["mx_ls"], ["mx_ls"])
        def split3(dst, src, tmp, skey, dkey, tkey):
            CP(P, "dve", dst[:, 0, :], src, [skey], [dkey])
            TT(P, tmp, src, dst[:, 0, :], ALU.subtract, [skey, dkey], [tkey])
            CP(P, "dve", dst[:, 1, :], tmp, [tkey], [dkey])
            TT(P, tmp, tmp, dst[:, 1, :], ALU.subtract, [tkey, dkey], [tkey])
            CP(P, "dve", dst[:, 2, :], tmp, [tkey], [dkey])
        split3(l3, ls[:], r1[:], "mx_ls", "mx_l3", "mx_r1")
        UF_b = C.consts_bf[:, C_UF:C_UF + 128]
        SL_b = C.consts_bf[0:64, C_SL:C_SL + 64]
        ps, psk = PS(0)
        for a_ in range(3):
            MM(P, ps[0:64, 0:128], l3[:, a_, :], C.ones_bf, a_ == 0, a_ == 2, ["mx_l3", "consts_bf"], [psk])
        CP(P, "dve", totbc[:], ps[0:64, 0:128], [psk], ["mx_totbc"])
        split3(tb3, totbc[:], tbr[:], "mx_totbc", "mx_tb3", "mx_tbr")
        ps, psk = PS(1)
        for a_ in range(3):
            MM(P, ps[:, 0:64], tb3[:, a_, :], SL_b, a_ == 0, a_ == 2, ["mx_tb3", "consts_bf"], [psk])
        CP(P, "dve", offbc[:], ps[:, 0:64], [psk], ["mx_offbc"])
        ps2, ps2k = PS(2)
        for a_ in range(3):
            MM(P, ps2[:, 0:64], UF_b, l3[:, a_, :], a_ == 0, a_ == 2, ["mx_l3", "consts_bf"], [ps2k])
        TT(P, ccol[:], ps2[:, 0:64], offbc[:], ALU.add, [ps2k, "mx_offbc"], ["mx_ccol"])
        for G in range(16):
            nj = 4 * G + 4
            cref = offbc[:, 4 * G:4 * G + 1]
            TS(P, biasG[:, G, 0:nj], ccol[:, 0:nj], cref, -1.0, ALU.subtract, ALU.mult,
               ["mx_ccol", "mx_offbc"], ["mx_biasG"])
            TS(P, cd[:, 4 * G:4 * G + 4], ccol[:, 4 * G:4 * G + 4], cref, 1.0 / scale, ALU.subtract, ALU.mult,
               ["mx_ccol", "mx_offbc"], ["mx_cd"])
        CP(P, "dve", c3[:, 0, :], cd[:], ["mx_cd"], ["mx_c3"])
        TT(P, r1[:], cd[:], c3[:, 0, :], ALU.subtract, ["mx_cd", "mx_c3"], ["mx_r1"])
        CP(P, "dve", c3[:, 1, :], r1[:], ["mx_r1"], ["mx_c3"])
        TT(P, r1[:], r1[:], c3[:, 1, :], ALU.subtract, ["mx_r1", "mx_c3"], ["mx_r1"])
        CP(P, "dve", c3[:, 2, :], r1[:], ["mx_r1"], ["mx_c3"])
        tp, tpk = PS(3)
        tpb = tp.bitcast(BF16)
        for a in range(3):
            TR(P, tpb[0:64, a * 128:(a + 1) * 128], c3[:, a, :], C.ident_bf, ["mx_c3", "consts_bf"], [tpk])
        CP(P, "dve", c3T[:].rearrange("p a c -> p (a c)"), tpb[0:64, 0:384], [tpk], ["mx_c3T"])
        DMA(P, "sp", crow_d.rearrange("a (j t) -> j a t", j=64), c3T[:], ["mx_c3T"], ["crow_d"], "crow_d")
        for a in range(3):
            DMA(P, "sp", crow[32 * a:32 * a + 1, :], crow_d[a:a + 1, :], ["crow_d"], ["mx_crow"], "mx_crow")
        for G in range(16):
            q0 = G * 512
            o_ps = [PS(4), PS(5), PS(0), PS(1)]
            for j in range(4 * G + 4):
                i_ = j - 4 * G
                c0 = 0 if i_ < 0 else 128 * i_
                n = 512 - c0
                st, stk = PS(6 + (j % 2))
                MM(P, st[:, 0:n], kT[:, j * 128:(j + 1) * 128], qT[:, q0 + c0:q0 + 512], True, False,
                   ["mx_kT", "mx_qT"], [stk])
                MM(P, st[:, 0:n], ones96, crow[:, q0 + c0:q0 + 512], False, i_ < 0, ["mx_crow", "consts_bf"], [stk])
                if i_ >= 0:
                    MM(P, st[:, 0:n], C.ident_bf, MD_bf[:, 0:n], False, True, ["consts_bf"], [stk])
                p_t, pk = pTr.next()
                ACT(P, p_t[:, 0:n], st[:, 0:n], AF.Exp, [stk, "mx_biasG"], [pk], bias=biasG[:, G, j:j + 1], scale=scale)
                for qb in range(4):
                    if 128 * qb < c0:
                        continue
                    op_t, op_k = o_ps[qb]
                    oc = 0
                    MM(P, op_t[:, oc:oc + 129], p_t[:, 128 * qb - c0:128 * qb - c0 + 128], Vp[:, j, 0:129],
                       j == 0, j == 4 * G + qb, [pk, "mx_Vp"], [op_k])
                if i_ >= 0:
                    qb = i_
                    op_t, op_k = o_ps[qb]
                    oc = 0
                    okey = op_k
                    rc, rck = recr.next()
                    RECIP(P, rc[:], op_t[:, oc + 128:oc + 129], [okey], [rck])
                    ob_, obk = obr.next()
                    ACT(P, ob_[:], op_t[:, oc:oc + 128], AF.Identity, [okey, rck], [obk], scale=rc[:])
                    tp, tpk = PS(3)
                    tpb = tp.bitcast(BF16)
                    TR(P, tpb[:, 0:128], ob_[:], C.ident_bf, [obk, "consts_bf"], [tpk])
                    if qb == 0:
                        obT_t, obT_k = obT.next()
                    CP(P, "dve", obT_t[:, qb * 128:(qb + 1) * 128], tpb[:, 0:128], [tpk], [obT_k])
                    if qb == 3:
                        DMA(P, "sp", ag2_in[2 + h][G // 4][:, (G % 4) * 512:(G % 4 + 1) * 512], obT_t[:],
                            [obT_k], ["ag2_in_%d_%d" % (2 + h, G // 4)], "o_" + obT_k)
                        if G % 4 == 3:
                            gather(2 + h, G // 4)
        P.barrier()


_CACHE = {}


def _prep_inputs(inp):
    f = lambda a: np.ascontiguousarray(np.asarray(a, dtype=np.float32))
    x = f(inp["x"])
    mem = f(inp["mem"])
    w_in = f(inp["w_in"])[0]
    offs = [0, 1024, 2048, 3072, 4096, 5120, 6144, 7168, 7176, 9224]
    q_a, f_a, i_a, g_a, q_b, k_b, v_b, f_b, gates = [w_in[:, offs[i]:offs[i + 1]] for i in range(9)]
    gains = np.zeros((128, 96), np.float32)
    for i, n in enumerate(GAIN_NAMES):
        gains[:, 8 * i:8 * i + 8] = f(inp[n])[0].reshape(8, 128).T
    gains[:, 80:96] = f(inp["b_gate"])[0].reshape(16, 128).T
    consts = make_consts()
    w_loc = np.ascontiguousarray(np.concatenate([g_a, gates], axis=1))
    lbl_all = f(inp["hg_lb_logits"])
    fbias_all = f(inp["fox_f_bias"])[0]
    shared = {
        "ffn1_w_in": f(inp["ffn1_w_in"])[0], "ffn1_w_down": f(inp["ffn1_w_down"])[0],
        "ffn2_w_in": f(inp["ffn2_w_in"])[0], "ffn2_w_down": f(inp["ffn2_w_down"])[0],
        "w_loc": w_loc, "w_branch_a": f(inp["w_branch_a"])[0], "w_branch_b": f(inp["w_branch_b"])[0],
        "w_out": f(inp["w_out"])[0], "w_mq": f(inp["w_mq"])[0], "w_mkv": f(inp["w_mkv"])[0],
        "w_mo": f(inp["w_mo"])[0], "gains": gains, "consts": consts,
    }
    maps = []
    for c in range(8):
        b, r = c // 4, c % 4
        hs = [2 * r, 2 * r + 1]
        m = dict(shared)
        m["xT"] = np.ascontiguousarray(x[b, r * S_OWN:(r + 1) * S_OWN, :].T)
        m["memT"] = np.ascontiguousarray(mem[b].T)
        sl = lambda w, h: w[:, h * 128:(h + 1) * 128]
        m["w_hg"] = np.ascontiguousarray(np.concatenate(
            [np.concatenate([sl(q_a, h), sl(f_a, h), sl(i_a, h)], axis=1) for h in hs], axis=1))
        m["w_fq"] = np.ascontiguousarray(np.concatenate([sl(q_b, h) for h in hs], axis=1))
        m["w_fk"] = np.ascontiguousarray(np.concatenate([sl(k_b, h) for h in hs], axis=1))
        m["w_fv"] = np.ascontiguousarray(np.concatenate(
            [np.concatenate([sl(v_b, h), f_b[:, h:h + 1]], axis=1) for h in hs], axis=1))
        l0 = np.concatenate([lbl_all[0, h] for h in hs])
        l1 = np.concatenate([lbl_all[1, h] for h in hs])
        m["lbl"] = np.ascontiguousarray(np.tile(np.concatenate([l0, l1])[None, :], (128, 1)))
        m["fbias"] = np.ascontiguousarray(np.tile(fbias_all[hs][None, :], (128, 1)))
        maps.append(m)
    return maps


def kernel(**inputs):
    dbg = bool(inputs.pop("_dbg", False))
    key = ("nc", dbg)
    if key not in _CACHE:
        _CACHE[key] = build_program(dbg)
    nc = _CACHE[key]
    maps = _prep_inputs(inputs)
    res = run_bass_kernel_spmd(nc, maps, core_ids=list(range(8)))
    out = np.zeros((2, S_ALL, D), np.float32)
    for c in range(8):
        b, r = c // 4, c % 4
        out[b, r * S_OWN:(r + 1) * S_OWN, :] = np.asarray(res.results[c]["outT"]).T
    if dbg:
        return out, res.results
    return out
```

```python
import contextlib
import math
import numpy as np
import concourse.bass as bass
import concourse.mybir as mybir
from concourse.bass_utils import run_bass_kernel_spmd

F32 = mybir.dt.float32
BF16 = mybir.dt.bfloat16
AF = mybir.ActivationFunctionType
ALU = mybir.AluOpType

SAME_ENGINE_SYNC = False
STOP_AFTER = 9
MIX_PARTS = 7
NOAG2 = False
EPS = 1e-6
D = 1024
S_OWN = 2048
S_ALL = 8192
DFF = 2816
NEG = -1.0e9


class Op:
    __slots__ = ("eng", "fn", "deps", "is_dma", "semkey", "idx", "signal", "count",
                 "dma_waits", "inc", "seg")


class Prog:
    def __init__(self, nc, es):
        self.nc = nc
        self.es = es
        self.ops = []
        self.lastw = {}
        self.readers = {}
        self.dma_tot = {}
        self.final_waits = {}
        self.last_on = {}
        self.seg = 0

    def add(self, eng, fn, reads=(), writes=(), dma=None, inc=16):
        if dma is not None:
            dma = "cc" if inc == 1 else ("q_" + eng)
        deps = {}
        for r in reads:
            w = self.lastw.get(r)
            if w is not None:
                deps[w.idx] = w
        for r in writes:
            w = self.lastw.get(r)
            if w is not None:
                deps[w.idx] = w
            for rd in self.readers.get(r, {}).values():
                deps[rd.idx] = rd
        op = Op()
        op.eng = eng
        op.fn = fn
        op.is_dma = dma is not None
        op.semkey = dma
        op.inc = inc
        op.idx = len(self.ops)
        op.signal = False
        op.count = 0
        op.seg = self.seg
        dma_waits = {}
        cdeps = []
        for d in deps.values():
            if d.is_dma:
                dma_waits[d.semkey] = self.dma_tot[d.semkey]
            elif d.fn is not None:
                cdeps.append(d)
        op.deps = cdeps
        op.dma_waits = dma_waits
        if dma is not None:
            self.dma_tot[dma] = self.dma_tot.get(dma, 0) + inc
        for r in reads:
            self.readers.setdefault(r, {})[(eng, op.idx) if op.is_dma else eng] = op
        for r in writes:
            self.lastw[r] = op
            self.readers[r] = {}
        self.ops.append(op)
        if not op.is_dma and fn is not None:
            self.last_on[eng] = op
        return op

    def barrier(self):
        lasts = list(self.last_on.values())
        tot = dict(self.dma_tot)
        for e in ["pe", "act", "dve", "pool", "sp"]:
            op = Op()
            op.eng = e
            op.fn = None
            op.is_dma = False
            op.semkey = None
            op.inc = 0
            op.idx = len(self.ops)
            op.signal = False
            op.count = 0
            op.deps = [d for d in lasts if d.eng != e]
            op.dma_waits = dict(tot)
            op.seg = self.seg
            self.ops.append(op)
        self.lastw = {}
        self.readers = {}
        self.last_on = {}
        self.seg += 1

    def wait_dma_at_end(self, key):
        for k in self.dma_tot:
            self.final_waits[k] = self.dma_tot[k]

    def emit(self):
        nc = self.nc
        engs = ["pe", "act", "dve", "pool", "sp"]
        for op in self.ops:
            for d in op.deps:
                if d.eng == op.eng and (op.eng == "pe" or not SAME_ENGINE_SYNC):
                    continue
                d.signal = True
        cnt = {}
        for op in self.ops:
            if op.is_dma or op.fn is None:
                continue
            if op.signal:
                k_ = (0, op.eng)
                cnt[k_] = cnt.get(k_, 0) + 1
                op.count = cnt[k_]
        sems = {k_: self.es.enter_context(nc.semaphore("s_%d_%s" % k_)) for k_ in cnt}
        dsems = {}
        for k in self.dma_tot:
            dsems[k] = self.es.enter_context(nc.semaphore("d_" + str(len(dsems))))
        per_eng = {e: [o for o in self.ops if o.eng == e] for e in engs}
        self.stats = {e: len(per_eng[e]) for e in engs}
        self.stats["sems"] = len(dsems) + len(sems)
        self.stats["counts"] = max(cnt.values()) if cnt else 0
        final_waits = self.final_waits

        def run(e, engine):
            waited = {}
            dwaited = {}
            for op in per_eng[e]:
                need = {}
                for d in op.deps:
                    if d.eng == e and (e == "pe" or not SAME_ENGINE_SYNC):
                        continue
                    k_ = (0, d.eng)
                    if d.count > need.get(k_, 0):
                        need[k_] = d.count
                for x, v in need.items():
                    if v > waited.get(x, 0):
                        engine.wait_ge(sems[x], v)
                        waited[x] = v
                for k, v in op.dma_waits.items():
                    if v > dwaited.get(k, 0):
                        engine.wait_ge(dsems[k], v)
                        dwaited[k] = v
                if op.fn is None:
                    continue
                inst = op.fn(engine)
                if op.is_dma:
                    inst.then_inc(dsems[op.semkey], op.inc)
                elif op.signal:
                    inst.then_inc(sems[(0, e)], 1)
            if e == "sp":
                for k, v in final_waits.items():
                    engine.wait_ge(dsems[k], v)

        with nc.Block() as block:
            @block.tensor
            def _(t):
                run("pe", t)

            @block.scalar
            def _(t):
                run("act", t)

            @block.vector
            def _(t):
                run("dve", t)

            @block.gpsimd
            def _(t):
                run("pool", t)

            @block.sync
            def _(t):
                run("sp", t)


_uid = [0]


def _nm(n):
    _uid[0] += 1
    return "%s_u%d" % (n, _uid[0])


_RANK = {}


def _rank(e):
    if id(e) not in _RANK:
        _RANK[id(e)] = (e, (e.partition_id() % 4) * 512)
    return _RANK[id(e)][1]


class Ring:
    def __init__(self, nc, es, name, n, shape, dtype, psum=False):
        self.bufs = []
        for i in range(n):
            nm = "%s%d" % (name, i)
            if psum:
                t = es.enter_context(nc.psum_tensor(_nm(nm), shape, dtype))
            else:
                t = es.enter_context(nc.sbuf_tensor(_nm(nm), shape, dtype))
            self.bufs.append((t, nm))
        self.i = 0

    def next(self):
        b = self.bufs[self.i % len(self.bufs)]
        self.i += 1
        return b


def MM(P, out, lhsT, rhs, start, stop, reads, writes):
    P.add("pe", lambda e: e.matmul(out, lhsT, rhs, start=start, stop=stop), reads, writes)


def TR(P, out, in_, ident, reads, writes):
    P.add("pe", lambda e: e.transpose(out, in_, ident), reads, writes)


def ACT(P, out, in_, func, reads, writes, bias=None, scale=None):
    kw = {}
    if bias is not None:
        kw["bias"] = bias
    if scale is not None:
        kw["scale"] = scale
    P.add("act", lambda e: e.activation(out=out, in_=in_, func=func, **kw), reads, writes)


def TT(P, out, in0, in1, op, reads, writes, eng="dve"):
    P.add(eng, lambda e: e.tensor_tensor(out=out, in0=in0, in1=in1, op=op), reads, writes)


def TS(P, out, in0, s1, s2, op0, op1, reads, writes, eng="dve"):
    if op1 is None:
        P.add(eng, lambda e: e.tensor_scalar(out=out, in0=in0, scalar1=s1, scalar2=None, op0=op0),
              reads, writes)
    else:
        P.add(eng, lambda e: e.tensor_scalar(out=out, in0=in0, scalar1=s1, scalar2=s2, op0=op0, op1=op1),
              reads, writes)


def STT(P, out, in0, scalar, in1, op0, op1, reads, writes):
    P.add("dve", lambda e: e.scalar_tensor_tensor(out=out, in0=in0, scalar=scalar, in1=in1,
                                                  op0=op0, op1=op1), reads, writes)


def CP(P, eng, out, in_, reads, writes):
    P.add(eng, lambda e: e.tensor_copy(out=out, in_=in_), reads, writes)


def RECIP(P, out, in_, reads, writes):
    P.add("dve", lambda e: e.reciprocal(out=out, in_=in_), reads, writes)


def MSET(P, eng, ap, val, writes):
    P.add(eng, lambda e: e.memset(ap, val), (), writes)


def DMA(P, q, out, in_, reads, writes, key):
    P.add(q, lambda e: e.dma_start(out=out, in_=in_), reads, writes, dma=key)


GAIN_NAMES = ["ffn1_pre_g", "ffn1_post_g", "mix_pre_g", "hg_norm_g", "mix_post_g",
              "mem_pre_g", "mem_kv_g", "mem_post_g", "ffn2_pre_g", "ffn2_post_g"]
C_ID, C_U, C_BO, C_UF, C_ONE, C_SEL, C_SL, C_MD = 0, 128, 256, 384, 512, 640, 642, 706
C_TOT = 706 + 512


def make_consts():
    c = np.zeros((128, C_TOT), np.float32)
    i = np.arange(128)
    c[:, C_ID:C_ID + 128] = np.eye(128)
    same = (i[:, None] // 64) == (i[None, :] // 64)
    c[:, C_U:C_U + 128] = ((i[:, None] <= i[None, :]) & same)
    c[:, C_BO:C_BO + 128] = same
    c[:, C_UF:C_UF + 128] = (i[:, None] <= i[None, :])
    c[:, C_ONE:C_ONE + 128] = 1.0
    c[:, C_SEL + 0] = (i < 64)
    c[:, C_SEL + 1] = (i >= 64)
    j = np.arange(64)
    c[:64, C_SL:C_SL + 64] = (j[:, None] < j[None, :])
    jj = np.arange(512)
    c[:, C_MD:C_MD + 512] = np.where(jj[None, :] < i[:, None], NEG, 0.0)
    return c


class Ctx:
    pass


def rms_rstd(P, C, src, src_key, cols, ncols=512, dim=1024.0):
    sq, sqk = C.sq.next()
    ACT(P, sq[:, :, 0:ncols], src[:, :, cols], AF.Square, [src_key], [sqk])
    ps, psk = C.ps.next()
    for kt in range(8):
        MM(P, ps[:, 0:ncols], C.ones_bf[:], sq[:, kt, 0:ncols], kt == 0, kt == 7, [sqk, "consts_bf"], [psk])
    r, rk = C.rs.next()
    TS(P, r[:, 0:ncols], ps[:, 0:ncols], 1.0 / dim, EPS, ALU.mult, ALU.add, [psk], [rk])
    ACT(P, r[:, 0:ncols], r[:, 0:ncols], AF.Sqrt, [rk], [rk])
    RECIP(P, r[:, 0:ncols], r[:, 0:ncols], [rk], [rk])
    return r, rk


def load_w(P, C, w_d, kt_n, c0, ncols, dst_c0=0, buf=None):
    if buf is None:
        buf = C.wring.next()
    t, k = buf
    return buf


def wview(t, kt_n, width):
    return t[:, 0:kt_n * width].rearrange("p (k c) -> p k c", k=kt_n)


def wload(P, t, k, w_d, kt_n, width, c0, ncols, dst_c0):
    v = wview(t, kt_n, width)
    src = w_d[:, c0:c0 + ncols].rearrange("(k p) c -> p k c", p=128)
    DMA(P, "pool", v[:, :, dst_c0:dst_c0 + ncols], src, [], [k], "w_" + k)


def norm_h(P, C, hT, hk, hcols, xcols, g_col0, n=512):
    r, rk = rms_rstd(P, C, C.xT, "xT", xcols, n)
    for kt in range(8):
        STT(P, hT[:, kt, hcols], C.xT[:, kt, xcols], C.gains[:, g_col0 + kt:g_col0 + kt + 1],
            r[:, 0:n], ALU.mult, ALU.mult, ["xT", rk, "gains"], [hk])


def post_norm_add(P, C, yT, yk, ycols, xcols, g_col0, n=512):
    r, rk = rms_rstd(P, C, yT, yk, ycols, n)
    for kt in range(8):
        t, tk = C.tmp.next()
        STT(P, t[:, 0:n], yT[:, kt, ycols], C.gains[:, g_col0 + kt:g_col0 + kt + 1],
            r[:, 0:n], ALU.mult, ALU.mult, [yk, rk, "gains"], [tk])
        TT(P, C.xT[:, kt, xcols], C.xT[:, kt, xcols], t[:, 0:n], ALU.add, ["xT", tk], ["xT"])


def ffn(P, C, es, nc, w_in_d, w_dn_d, g_pre, g_post_h):
    yT = es.enter_context(nc.sbuf_tensor(_nm("ffn_hy"), [128, 8, 1024], F32))
    hT = yT.bitcast(BF16)
    actT = es.enter_context(nc.sbuf_tensor(_nm("ffn_actT"), [128, 22, 1024], BF16))
    sg_ring = Ring(nc, es, "ffn_sg", 2, [128, 512], F32)
    for half in range(2):
        t0 = half * 1024
        for tt in range(2):
            norm_h(P, C, hT, "ffn_hy", slice(tt * 512, (tt + 1) * 512),
                   slice(t0 + tt * 512, t0 + (tt + 1) * 512), g_pre)
        for ch in range(11):
            wt, wk = C.wring.next()
            wload(P, wt, wk, w_in_d, 8, 512, ch * 256, 256, 0)
            wload(P, wt, wk, w_in_d, 8, 512, DFF + ch * 256, 256, 256)
            wv = wview(wt, 8, 512)
            for jj in range(2):
                j = ch * 2 + jj
                for tt in range(2):
                    tc_ = slice(tt * 512, (tt + 1) * 512)
                    pg, pgk = C.ps.next()
                    for kt in range(8):
                        MM(P, pg[:], wv[:, kt, jj * 128:(jj + 1) * 128], hT[:, kt, tc_], kt == 0, kt == 7,
                           [wk, "ffn_hy"], [pgk])
                    pu, puk = C.ps.next()
                    for kt in range(8):
                        MM(P, pu[:], wv[:, kt, 256 + jj * 128:256 + (jj + 1) * 128], hT[:, kt, tc_],
                           kt == 0, kt == 7, [wk, "ffn_hy"], [puk])
                    sg, sgk = sg_ring.next()
                    ACT(P, sg[:], pg[:], AF.Silu, [pgk], [sgk])
                    TT(P, actT[:, j, tc_], pu[:], sg[:], ALU.mult, [puk, sgk], ["ffn_actT%d" % j])
        akeys = ["ffn_actT%d" % j for j in range(22)]
        for dc in range(4):
            wt, wk = C.wring.next()
            wload(P, wt, wk, w_dn_d, 22, 256, dc * 256, 256, 0)
            wv = wview(wt, 22, 256)
            for dd in range(2):
                d = dc * 2 + dd
                for tt in range(2):
                    tc_ = slice(tt * 512, (tt + 1) * 512)
                    py, pyk = C.ps.next()
                    for j in range(22):
                        MM(P, py[:], wv[:, j, dd * 128:(dd + 1) * 128], actT[:, j, tc_], j == 0, j == 21,
                           [wk] + akeys, [pyk])
                    ACT(P, yT[:, d, tc_], py[:], AF.Copy, [pyk], ["ffn_hy"])
        for tt in range(2):
            post_norm_add(P, C, yT, "ffn_hy", slice(tt * 512, (tt + 1) * 512),
                          slice(t0 + tt * 512, t0 + (tt + 1) * 512), g_post_h)


def build_program(dbg=False):
    _RANK.clear()
    nc = bass.Bass("TRN2", target_bir_lowering=False)

    def din(name, shape, dt=F32):
        return nc.dram_tensor(name, shape, dt, kind="ExternalInput").ap()

    xT_d = din("xT", [D, S_OWN])
    memT_d = din("memT", [D, 256])
    ffn1_in = din("ffn1_w_in", [D, 2 * DFF])
    ffn1_dn = din("ffn1_w_down", [DFF, D])
    ffn2_in = din("ffn2_w_in", [D, 2 * DFF])
    ffn2_dn = din("ffn2_w_down", [DFF, D])
    w_loc_d = din("w_loc", [D, 3072])
    w_hg_d = din("w_hg", [D, 768])
    w_fq_d = din("w_fq", [D, 256])
    w_fk_d = din("w_fk", [D, 256])
    w_fv_d = din("w_fv", [D, 258])
    w_a_d = din("w_branch_a", [D, D])
    w_b_d = din("w_branch_b", [D, D])
    w_o_d = din("w_out", [D, D])
    w_mq_d = din("w_mq", [D, D])
    w_mkv_d = din("w_mkv", [D, 2 * D])
    w_mo_d = din("w_mo", [D, D])
    gains_d = din("gains", [128, 96])
    consts_d = din("consts", [128, C_TOT])
    lbl_d = din("lbl", [128, 512])
    fbias_d = din("fbias", [128, 2])
    outT_d = nc.dram_tensor("outT", [D, S_OWN], F32, kind="ExternalOutput").ap()
    dbg_x1 = dbg_o = None
    if dbg:
        dbg_x1 = nc.dram_tensor("dbg_x1", [D, S_OWN], F32, kind="ExternalOutput").ap()
        dbg_o = nc.dram_tensor("dbg_o", [512, S_ALL], F32, kind="ExternalOutput").ap()
        dbg_x2 = nc.dram_tensor("dbg_x2", [D, S_OWN], F32, kind="ExternalOutput").ap()
        dbg_x3 = nc.dram_tensor("dbg_x3", [D, S_OWN], F32, kind="ExternalOutput").ap()

    ag1_in = [nc.dram_tensor("ag1_in%d" % i, [256, S_OWN], BF16).ap() for i in range(4)]
    ag1_out = [nc.dram_tensor("ag1_out%d" % i, [1024, S_OWN], BF16).ap() for i in range(4)]
    ag2_in = [[nc.dram_tensor("ag2_in%d_%d" % (q, tq), [128, S_OWN], F32).ap() for tq in range(4)] for q in range(4)]
    ag2_out = [nc.dram_tensor("ag2_out%d" % q, [4 * 512, S_OWN], F32).ap() for q in range(4)]
    x_spill = nc.dram_tensor("x_spill", [D, S_OWN], F32).ap()
    crow_d = nc.dram_tensor("crow_d", [3, 64 * 128], BF16).ap()
    RG = [[0, 1, 2, 3], [4, 5, 6, 7]]

    with contextlib.ExitStack() as es_all:
        P = Prog(nc, es_all)
        C = Ctx()
        C.consts = es_all.enter_context(nc.sbuf_tensor(_nm("consts_sb"), [128, C_TOT], F32))
        C.consts_bf = es_all.enter_context(nc.sbuf_tensor(_nm("consts_bf_sb"), [128, C_TOT], BF16))
        C.gains = es_all.enter_context(nc.sbuf_tensor(_nm("gains_sb"), [128, 96], F32))
        C.ones_bf = C.consts_bf[:, C_ONE:C_ONE + 128]
        C.ident_bf = C.consts_bf[:, C_ID:C_ID + 128]
        C.ps = Ring(nc, es_all, "ps", 8, [128, 512], F32, psum=True)
        DMA(P, "sp", C.consts[:], consts_d, [], ["consts"], "consts")
        DMA(P, "sp", C.gains[:], gains_d, [], ["gains"], "gains")
        CP(P, "dve", C.consts_bf[:], C.consts[:], ["consts"], ["consts_bf"])
        for g0 in (8, 72):
            TS(P, C.gains[:, g0:g0 + 8], C.gains[:, g0:g0 + 8], 0.5, None, ALU.mult, None, ["gains"], ["gains"])

        with contextlib.ExitStack() as es:
            C.xT = es.enter_context(nc.sbuf_tensor(_nm("xT_sb"), [128, 8, S_OWN], F32))
            DMA(P, "sp", C.xT[:], xT_d.rearrange("(k p) t -> p k t", p=128), [], ["xT"], "xT")
            C.sq = Ring(nc, es, "sq", 1, [128, 8, 512], BF16)
            C.rs = Ring(nc, es, "rs", 2, [128, 512], F32)
            C.tmp = Ring(nc, es, "tmp", 2, [128, 512], F32)
            C.wring = Ring(nc, es, "wr", 3, [128, 5632], BF16)
            with contextlib.ExitStack() as es2:
                ffn(P, C, es2, nc, ffn1_in, ffn1_dn, 0, 8)
            P.barrier()
            if dbg:
                DMA(P, "sp", dbg_x1.rearrange("(k p) t -> p k t", p=128), C.xT[:], ["xT"], [], "dbg")
            if STOP_AFTER == 1:
                DMA(P, "sp", outT_d.rearrange("(k p) t -> p k t", p=128), C.xT[:], ["xT"], [], "out")
                P.wait_dma_at_end("out")
                if dbg:
                    P.wait_dma_at_end("dbg")
                P.emit()
                build_program.stats = P.stats
                return nc
            with contextlib.ExitStack() as es2:
                hT = es2.enter_context(nc.sbuf_tensor(_nm("mix_hT"), [128, 8, S_OWN], BF16))
                for tt in range(4):
                    sl = slice(tt * 512, (tt + 1) * 512)
                    norm_h(P, C, hT, "mix_hT", sl, sl, 16)
                for ck in range(4):
                    DMA(P, "sp", ag1_in[ck].rearrange("(k p) t -> p k t", p=128), hT[:, 2 * ck:2 * ck + 2, :],
                        ["mix_hT"], ["ag1_in"], "ag1_in")
                DMA(P, "sp", x_spill.rearrange("(k p) t -> p k t", p=128), C.xT[:], ["xT"], ["x_spill"], "x_spill")
                for ck in range(4):
                    P.add("pool", lambda e, ck=ck: e.collective_compute(
                        "AllGather", ALU.bypass, replica_groups=RG, ins=[ag1_in[ck].opt()], outs=[ag1_out[ck].opt()]),
                        ["ag1_in"], ["ag1_out"], dma="agc", inc=1)
                P.barrier()
        if STOP_AFTER == 2:
            DMA(P, "sp", outT_d, x_spill, [], [], "out")
            P.wait_dma_at_end("out")
            if dbg:
                P.wait_dma_at_end("dbg")
            P.emit()
            build_program.stats = P.stats
            return nc
        with contextlib.ExitStack() as es:
            mixers(P, C, es, nc, ag1_out, ag2_in, w_hg_d, w_fq_d, w_fk_d, w_fv_d, lbl_d, fbias_d, crow_d, ag2_out, RG)
            P.barrier()
        if dbg:
            for q in range(4):
                for tq in range(4):
                    DMA(P, "sp", dbg_o[q * 128:(q + 1) * 128, tq * 2048:(tq + 1) * 2048], ag2_in[q][tq], [], [], "dbg")
        if STOP_AFTER == 3:
            DMA(P, "sp", outT_d, x_spill, [], [], "out")
            P.wait_dma_at_end("out")
            if dbg:
                P.wait_dma_at_end("dbg")
            P.emit()
            build_program.stats = P.stats
            return nc
        with contextlib.ExitStack() as es:
            C.xT = es.enter_context(nc.sbuf_tensor(_nm("xTb"), [128, 8, S_OWN], F32))
            DMA(P, "sp", C.xT[:], x_spill.rearrange("(k p) t -> p k t", p=128), [], ["xT"], "xT2")
            C.sq = Ring(nc, es, "sqb", 1, [128, 8, 512], BF16)
            C.rs = Ring(nc, es, "rsb", 2, [128, 512], F32)
            C.tmp = Ring(nc, es, "tmpb", 2, [128, 512], F32)
            C.wring = Ring(nc, es, "wrb", 3, [128, 5632], BF16)
            with contextlib.ExitStack() as es2:
                post_mixer(P, C, es2, nc, ag2_out, w_loc_d, w_a_d, w_b_d, w_o_d)
            P.barrier()
            if dbg:
                DMA(P, "sp", dbg_x2.rearrange("(k p) t -> p k t", p=128), C.xT[:], ["xT"], [], "dbg")
            with contextlib.ExitStack() as es2:
                mem_attn(P, C, es2, nc, memT_d, w_mq_d, w_mkv_d, w_mo_d)
            P.barrier()
            if dbg:
                DMA(P, "sp", dbg_x3.rearrange("(k p) t -> p k t", p=128), C.xT[:], ["xT"], [], "dbg")
            with contextlib.ExitStack() as es2:
                ffn(P, C, es2, nc, ffn2_in, ffn2_dn, 64, 72)
            DMA(P, "sp", outT_d.rearrange("(k p) t -> p k t", p=128), C.xT[:], ["xT"], [], "out")
            P.wait_dma_at_end("out")
            if dbg:
                P.wait_dma_at_end("dbg")
        P.emit()
        build_program.stats = P.stats
    return nc


def stream_proj(P, C, w_d, c0, ncols_total, rhs, rhs_keys, epilogue, kt_n=8):
    done = 0
    i = 0
    while done < ncols_total:
        n = min(512, ncols_total - done)
        wt, wk = C.wring.next()
        wload(P, wt, wk, w_d, kt_n, 512, c0 + done, n, 0)
        wv = wview(wt, kt_n, 512)
        for jj in range(n // 128):
            ps, psk = C.ps.next()
            for kt in range(kt_n):
                MM(P, ps[:], wv[:, kt, jj * 128:(jj + 1) * 128], rhs[:, kt, :], kt == 0, kt == kt_n - 1,
                   [wk] + rhs_keys, [psk])
            epilogue(i, ps, psk)
            i += 1
        done += n


def post_mixer(P, C, es, nc, ag2_out, w_loc_d, w_a_d, w_b_d, w_o_d):
    sb = lambda n, s, d: es.enter_context(nc.sbuf_tensor(_nm(n), s, d))
    oa = sb("pm_oa", [128, 8, 512], F32)
    ob = sb("pm_ob", [128, 8, 512], BF16)
    hT = sb("pm_hT", [128, 8, 512], BF16)
    ga = sb("pm_ga", [128, 8, 512], F32)
    gB = sb("pm_gB", [128, 8, 512], F32)
    oan = sb("pm_oan", [128, 8, 512], BF16)
    yg = sb("pm_yg", [128, 8, 512], BF16)
    ya = ga
    zT = oa
    o_own = nc.dram_tensor("o_own", [2048, S_OWN], F32).ap()
    dst = o_own.rearrange("(a r q p) t -> a q r p t", a=2, r=4, q=2, p=128)
    for q in range(4):
        def ld(e, q=q):
            roff = _rank(e)
            return e.dma_start(out=dst[q // 2, q % 2],
                               in_=ag2_out[q][bass.ds(roff, 512), :].rearrange("(r p) t -> r p t", p=128))
        P.add("pool", ld, [], ["o_own"], dma="o_own")
    for tt in range(4):
        sl = slice(tt * 512, (tt + 1) * 512)

        DMA(P, "sp", oa[:], o_own[0:1024, sl].rearrange("(h p) t -> p h t", p=128), ["o_own"], ["pm_oa"], "pm_oa")
        DMA(P, "pool", ob[:], o_own[1024:2048, sl].rearrange("(h p) t -> p h t", p=128), ["o_own"], ["pm_ob"], "pm_ob")
        norm_h(P, C, hT, "pm_hT", slice(0, 512), sl, 16)

        def ep_ga(i, ps, psk):
            ACT(P, ga[:, i, :], ps[:], AF.Silu, [psk], ["pm_ga"])
        stream_proj(P, C, w_loc_d, 0, 1024, hT, ["pm_hT"], ep_ga)
        r, rk = rms_rstd(P, C, oa, "pm_oa", slice(0, 512))
        for kt in range(8):
            t, tk = C.tmp.next()
            STT(P, t[:], oa[:, kt, :], C.gains[:, 24 + kt:25 + kt], r[:], ALU.mult, ALU.mult,
                ["pm_oa", rk, "gains"], [tk])
            TT(P, oan[:, kt, :], t[:], ga[:, kt, :], ALU.mult, [tk, "pm_ga"], ["pm_oan"])

        def ep_a(i, ps, psk):
            ACT(P, ya[:, i, :], ps[:], AF.Copy, [psk], ["pm_ga"])
        stream_proj(P, C, w_a_d, 0, 1024, oan, ["pm_oan"], ep_a)

        def ep_gA(i, ps, psk):
            t, tk = C.tmp.next()
            ACT(P, t[:], ps[:], AF.Sigmoid, [psk, "gains"], [tk], bias=C.gains[:, 80 + i:81 + i])
            TT(P, ya[:, i, :], ya[:, i, :], t[:], ALU.mult, [tk, "pm_ga"], ["pm_ga"])
        stream_proj(P, C, w_loc_d, 1024, 1024, hT, ["pm_hT"], ep_gA)

        def ep_gB(i, ps, psk):
            ACT(P, gB[:, i, :], ps[:], AF.Sigmoid, [psk, "gains"], ["pm_gB"], bias=C.gains[:, 88 + i:89 + i])
        stream_proj(P, C, w_loc_d, 2048, 1024, hT, ["pm_hT"], ep_gB)

        def ep_b(i, ps, psk):
            t, tk = C.tmp.next()
            TT(P, t[:], ps[:], gB[:, i, :], ALU.mult, [psk, "pm_gB"], [tk])
            TT(P, yg[:, i, :], t[:], ya[:, i, :], ALU.add, [tk, "pm_ga"], ["pm_yg"])
        stream_proj(P, C, w_b_d, 0, 1024, ob, ["pm_ob"], ep_b)

        def ep_o(i, ps, psk):
            ACT(P, zT[:, i, :], ps[:], AF.Copy, [psk], ["pm_oa"])
        stream_proj(P, C, w_o_d, 0, 1024, yg, ["pm_yg"], ep_o)
        post_norm_add(P, C, zT, "pm_oa", slice(0, 512), sl, 32)


def mem_attn(P, C, es, nc, memT_d, w_mq_d, w_mkv_d, w_mo_d):
    sb = lambda n, s, d: es.enter_context(nc.sbuf_tensor(_nm(n), s, d))
    memT = sb("ma_memT", [128, 8, 256], F32)
    memn = sb("ma_memn", [128, 8, 256], BF16)
    kmT = sb("ma_kmT", [128, 8, 256], BF16)
    vm = sb("ma_vm", [128, 2, 1024], BF16)
    hT = sb("ma_hT", [128, 8, 512], BF16)
    qmT = sb("ma_qmT", [128, 8, 512], BF16)
    omT = sb("ma_omT", [128, 8, 512], BF16)
    yT = sb("ma_yT", [128, 8, 512], F32)
    pT = Ring(nc, es, "ma_pT", 2, [128, 2, 512], BF16)
    rec = Ring(nc, es, "ma_rec", 2, [128, 512], F32)
    DMA(P, "sp", memT[:], memT_d.rearrange("(k p) t -> p k t", p=128), [], ["ma_memT"], "ma_memT")
    sq, sqk = C.sq.next()
    ACT(P, sq[:, :, 0:256], memT[:], AF.Square, ["ma_memT"], [sqk])
    ps, psk = C.ps.next()
    for kt in range(8):
        MM(P, ps[:, 0:256], C.ones_bf[:], sq[:, kt, 0:256], kt == 0, kt == 7, [sqk, "consts_bf"], [psk])
    r, rk = C.rs.next()
    TS(P, r[:, 0:256], ps[:, 0:256], 1.0 / 1024.0, EPS, ALU.mult, ALU.add, [psk], [rk])
    ACT(P, r[:, 0:256], r[:, 0:256], AF.Sqrt, [rk], [rk])
    RECIP(P, r[:, 0:256], r[:, 0:256], [rk], [rk])
    for kt in range(8):
        STT(P, memn[:, kt, :], memT[:, kt, :], C.gains[:, 48 + kt:49 + kt], r[:, 0:256], ALU.mult, ALU.mult,
            ["ma_memT", rk, "gains"], ["ma_memn"])
    for ch in range(2):
        wt, wk = C.wring.next()
        wload(P, wt, wk, w_mkv_d, 8, 512, ch * 512, 512, 0)
        wv = wview(wt, 8, 512)
        for jj in range(4):
            ps, psk = C.ps.next()
            for kt in range(8):
                MM(P, ps[:, 0:256], wv[:, kt, jj * 128:(jj + 1) * 128], memn[:, kt, :], kt == 0, kt == 7,
                   [wk, "ma_memn"], [psk])
            ACT(P, kmT[:, ch * 4 + jj, :], ps[:, 0:256], AF.Copy, [psk], ["ma_kmT"])
    for ch in range(2):
        wt, wk = C.wring.next()
        wload(P, wt, wk, w_mkv_d, 8, 512, 1024 + ch * 512, 512, 0)
        wv = wview(wt, 8, 512)
        for mt in range(2):
            ps, psk = C.ps.next()
            for kt in range(8):
                MM(P, ps[:], memn[:, kt, mt * 128:(mt + 1) * 128], wv[:, kt, :], kt == 0, kt == 7,
                   [wk, "ma_memn"], [psk])
            ACT(P, vm[:, mt, ch * 512:(ch + 1) * 512], ps[:], AF.Copy, [psk], ["ma_vm"])
    scale = 1.0 / math.sqrt(256.0)
    for tt in range(4):
        sl = slice(tt * 512, (tt + 1) * 512)
        norm_h(P, C, hT, "ma_hT", slice(0, 512), sl, 40)

        def ep_q(i, ps, psk):
            ACT(P, qmT[:, i, :], ps[:], AF.Copy, [psk], ["ma_qmT"])
        stream_proj(P, C, w_mq_d, 0, 1024, hT, ["ma_hT"], ep_q)
        for hh in range(4):
            p_t, pk = pT.next()
            for mt in range(2):
                ps, psk = C.ps.next()
                for a in range(2):
                    MM(P, ps[:], kmT[:, 2 * hh + a, mt * 128:(mt + 1) * 128], qmT[:, 2 * hh + a, :],
                       a == 0, a == 1, ["ma_kmT", "ma_qmT"], [psk])
                ACT(P, p_t[:, mt, :], ps[:], AF.Exp, [psk], [pk], scale=scale)
            ps, psk = C.ps.next()
            for mt in range(2):
                MM(P, ps[:], C.ones_bf[:], p_t[:, mt, :], mt == 0, mt == 1, [pk, "consts_bf"], [psk])
            rc, rck = rec.next()
            RECIP(P, rc[:], ps[:], [psk], [rck])
            for a in range(2):
                ps, psk = C.ps.next()
                for mt in range(2):
                    MM(P, ps[:], vm[:, mt, (2 * hh + a) * 128:(2 * hh + a + 1) * 128], p_t[:, mt, :],
                       mt == 0, mt == 1, [pk, "ma_vm"], [psk])
                TT(P, omT[:, 2 * hh + a, :], ps[:], rc[:], ALU.mult, [psk, rck], ["ma_omT"])

        def ep_o(i, ps, psk):
            ACT(P, yT[:, i, :], ps[:], AF.Copy, [psk], ["ma_yT"])
        stream_proj(P, C, w_mo_d, 0, 1024, omT, ["ma_omT"], ep_o)
        post_norm_add(P, C, yT, "ma_yT", slice(0, 512), sl, 56)


def mixers(P, C, es, nc, ag1_out, ag2_in, w_hg_d, w_fq_d, w_fk_d, w_fv_d, lbl_d, fbias_d, crow_d, ag2_out, RG):
    def gather(q, tq):
        P.add("pool", lambda e: e.collective_compute(
            "AllGather", ALU.bypass, replica_groups=RG, ins=[ag2_in[q][tq].opt()],
            outs=[ag2_out[q][tq * 512:(tq + 1) * 512, :].opt()]),
            ["ag2_in_%d_%d" % (q, tq)], ["ag2_out"], dma="agc", inc=1)

    sb = lambda n, s, d: es.enter_context(nc.sbuf_tensor(_nm(n), s, d))
    cf = C.consts
    U_f = cf[:, C_U:C_U + 128]
    BO_f = cf[:, C_BO:C_BO + 128]
    UF_f = cf[:, C_UF:C_UF + 128]
    ONE_f = cf[:, C_ONE:C_ONE + 128]
    SEL_f = cf[:, C_SEL:C_SEL + 2]
    SL_f = cf[0:64, C_SL:C_SL + 64]
    MD_bf = C.consts_bf[:, C_MD:C_MD + 512]
    whg = sb("mx_whg", [128, 8, 768], BF16)
    wfq = sb("mx_wfq", [128, 8, 256], BF16)
    wfk = sb("mx_wfk", [128, 8, 256], BF16)
    wfv = sb("mx_wfv", [128, 8, 258], BF16)
    for t, d_, n in ((whg, w_hg_d, "mx_whg"), (wfq, w_fq_d, "mx_wfq"), (wfk, w_fk_d, "mx_wfk"), (wfv, w_fv_d, "mx_wfv")):
        DMA(P, "pool", t[:], d_.rearrange("(k p) c -> p k c", p=128), [], [n], n)
    lbl = sb("mx_lbl", [128, 512], F32)
    fbias = sb("mx_fbias", [128, 2], F32)
    DMA(P, "sp", lbl[:], lbl_d, [], ["mx_lbl"], "mx_lbl")
    DMA(P, "sp", fbias[:], fbias_d, [], ["mx_fbias"], "mx_fbias")
    lb = sb("mx_lb", [128, 256], F32)
    oml = sb("mx_oml", [128, 256], F32)
    TT(P, lb[:], lbl[:, 0:256], lbl[:, 256:512], ALU.subtract, ["mx_lbl"], ["mx_lb"])
    ACT(P, lb[:], lb[:], AF.Sigmoid, ["mx_lb"], ["mx_lb"])
    TS(P, oml[:], lb[:], -1.0, 1.0, ALU.mult, ALU.add, ["mx_lb"], ["mx_oml"])
    lbw = sb("mx_lbw", [128, 4, 128], F32)
    omlw = sb("mx_omlw", [128, 4, 128], F32)
    U4 = sb("mx_U4", [128, 4, 128], F32)
    for s_ in range(4):
        CP(P, "dve", U4[:, s_, :], U_f, ["consts"], ["mx_U4"])
    qT = sb("mx_qT", [128, S_ALL], BF16)
    kT = sb("mx_kT", [128, S_ALL], BF16)
    Vp = sb("mx_Vp", [128, 64, 130], BF16)
    fb = sb("mx_fb", [128, 64], F32)
    MSET(P, "dve", Vp[:, :, 128:130], 1.0, ["mx_Vp"])
    hring = Ring(nc, es, "mx_h", 2, [128, 8, 512], BF16)
    f32r = Ring(nc, es, "mx_f", 11, [128, 4, 128], F32)
    bfr = Ring(nc, es, "mx_b", 11, [128, 4, 128], BF16)
    S32 = sb("mx_S32", [128, 128], F32)
    Sbf = Ring(nc, es, "mx_Sbf", 2, [128, 128], BF16)
    dT = Ring(nc, es, "mx_dT", 2, [128, 8], F32)
    ostage = Ring(nc, es, "mx_ost", 2, [128, 512], F32)
    crow = sb("mx_crow", [96, S_ALL], BF16)
    biasG = sb("mx_biasG", [128, 16, 64], F32)
    ccol = sb("mx_ccol", [128, 64], F32)
    offbc = sb("mx_offbc", [128, 64], F32)
    ls = sb("mx_ls", [128, 64], F32)
    totbc = sb("mx_totbc", [64, 128], F32)
    cd = sb("mx_cd", [128, 64], F32)
    c3 = sb("mx_c3", [128, 3, 64], BF16)
    c3T = sb("mx_c3T", [64, 3, 128], BF16)
    r1 = sb("mx_r1", [128, 64], F32)
    l3 = sb("mx_l3", [128, 3, 64], BF16)
    tb3 = sb("mx_tb3", [64, 3, 128], BF16)
    tbr = sb("mx_tbr", [64, 128], F32)
    pTr = Ring(nc, es, "mx_pT", 3, [128, 512], BF16)
    obr = Ring(nc, es, "mx_ob", 2, [128, 128], BF16)
    recr = Ring(nc, es, "mx_rec", 2, [128, 1], F32)
    obT = Ring(nc, es, "mx_obT", 2, [128, 512], F32)
    MSET(P, "dve", crow[:], 0.0, ["mx_crow"])
    ones96 = C.consts_bf[0:96, C_ONE:C_ONE + 128]
    psb = C.ps.bufs
    PS = lambda i: psb[i]
    scale = 1.0 / math.sqrt(128.0)

    for h in range(2):
        for s_ in range(4):
            CP(P, "dve", lbw[:, s_, :], lb[:, h * 128:(h + 1) * 128], ["mx_lb"], ["mx_lbw"])
            CP(P, "dve", omlw[:, s_, :], oml[:, h * 128:(h + 1) * 128], ["mx_oml"], ["mx_omlw"])
        MSET(P, "dve", S32[:], 0.0, ["mx_S32"])
        sb_t, sb_k = Sbf.next()
        MSET(P, "dve", sb_t[:], 0.0, [sb_k])
        for T in range(16):
            rr, t0 = T // 4, (T % 4) * 512
            ht, hk = hring.next()
            for ck in range(4):
                srch = ag1_out[ck][rr * 256:(rr + 1) * 256, t0:t0 + 512].rearrange("(k p) t -> p k t", p=128)
                DMA(P, "sp", ht[:, 2 * ck:2 * ck + 2, :], srch, ["ag1_out"], [hk], "h_" + hk)
            if not (MIX_PARTS & 1):
                continue
            for (w_, dst, nm, wn) in ((wfq, qT, "mx_qT", "mx_wfq"), (wfk, kT, "mx_kT", "mx_wfk")):
                ps, psk = PS(6)
                for kt in range(8):
                    MM(P, ps[:], w_[:, kt, h * 128:(h + 1) * 128], ht[:, kt, :], kt == 0, kt == 7,
                       [wn, hk], [psk])
                ACT(P, dst[:, T * 512:(T + 1) * 512], ps[:], AF.Copy, [psk], [nm])
            ps, psk = PS(7)
            for s_ in range(4):
                for kt in range(8):
                    MM(P, ps[:, s_ * 128:s_ * 128 + 128], ht[:, kt, s_ * 128:(s_ + 1) * 128],
                       wfv[:, kt, h * 129:h * 129 + 128], kt == 0, kt == 7, ["mx_wfv", hk], [psk])
            ACT(P, Vp[:, T * 4:(T + 1) * 4, 0:128], ps[:].rearrange("p (s c) -> p s c", s=4), AF.Copy,
                [psk], ["mx_Vp"])
            ps, psk = PS(7)
            for s_ in range(4):
                for kt in range(8):
                    MM(P, ps[:, s_:s_ + 1], ht[:, kt, s_ * 128:(s_ + 1) * 128],
                       wfv[:, kt, h * 129 + 128:h * 129 + 129], kt == 0, kt == 7, ["mx_wfv", hk], [psk])
            ACT(P, fb[:, T * 4:(T + 1) * 4], ps[:, 0:4], AF.Copy, [psk], ["mx_fb"])
            if not (MIX_PARTS & 2):
                continue
            zq, zqk = PS(0)
            zf, zfk = PS(1)
            zi, zik = PS(2)
            for (z_, zk_, off) in ((zq, zqk, 0), (zf, zfk, 128), (zi, zik, 256)):
                for s_ in range(4):
                    for kt in range(8):
                        MM(P, z_[:, s_ * 128:(s_ + 1) * 128], ht[:, kt, s_ * 128:(s_ + 1) * 128],
                           whg[:, kt, h * 384 + off:h * 384 + off + 128], kt == 0, kt == 7,
                           ["mx_whg", hk], [zk_])
            qs, qsk = f32r.next()
            sg, sgk = f32r.next()
            Ib, Ibk = bfr.next()
            W = lambda t: t[:].rearrange("p s c -> p (s c)")
            ACT(P, W(qs), zq[:], AF.Silu, [zqk], [qsk])
            ACT(P, W(sg), zf[:], AF.Sigmoid, [zfk], [sgk])
            ACT(P, W(Ib), zi[:], AF.Copy, [zik], [Ibk])
            f_, fk_ = f32r.next()
            TT(P, W(f_), W(sg), W(omlw), ALU.mult, [sgk, "mx_omlw"], [fk_])
            TT(P, W(f_), W(f_), W(lbw), ALU.add, [fk_, "mx_lbw"], [fk_])
            lf, lfk = f32r.next()
            ACT(P, W(lf), W(f_), AF.Ln, [fk_], [lfk])
            kk, kkk = f32r.next()
            TS(P, W(kk), W(f_), -1.0, 1.0, ALU.mult, ALU.add, [fk_], [kkk])
            lfh, lfhk = bfr.next()
            lfl, lflk = bfr.next()
            lfr, lfrk = f32r.next()
            CP(P, "dve", W(lfh), W(lf), [lfk], [lfhk])
            TT(P, W(lfr), W(lf), W(lfh), ALU.subtract, [lfk, lfhk], [lfrk])
            CP(P, "dve", W(lfl), W(lfr), [lfrk], [lflk])
            U_b = C.consts_bf[:, C_U:C_U + 128]
            BO_b = C.consts_bf[:, C_BO:C_BO + 128]
            SEL_b = C.consts_bf[:, C_SEL:C_SEL + 2]
            bps, bpsk = PS(3)
            MM(P, bps[:], U_b, W(lfh), True, False, [lfhk, "consts_bf"], [bpsk])
            MM(P, bps[:], U_b, W(lfl), False, True, [lflk, "consts_bf"], [bpsk])
            blps, blpsk = PS(5)
            MM(P, blps[:], BO_b, W(lfh), True, False, [lfhk, "consts_bf"], [blpsk])
            MM(P, blps[:], BO_b, W(lfl), False, True, [lflk, "consts_bf"], [blpsk])
            btps, btpsk = PS(6)
            for s_ in range(4):
                MM(P, btps[:, 2 * s_:2 * s_ + 2], lfh[:, s_, :], SEL_b, True, False, [lfhk, "consts_bf"], [btpsk])
                MM(P, btps[:, 2 * s_:2 * s_ + 2], lfl[:, s_, :], SEL_b, False, True, [lflk, "consts_bf"], [btpsk])
            eb, ebk = f32r.next()
            enb, enbk = f32r.next()
            ebl, eblk = f32r.next()
            d_t, dk_ = dT.next()
            ACT(P, W(eb), bps[:], AF.Exp, [bpsk], [ebk])
            ACT(P, W(enb), bps[:], AF.Exp, [bpsk], [enbk], scale=-1.0)
            ACT(P, W(ebl), blps[:], AF.Exp, [blpsk], [eblk])
            ACT(P, d_t[:], btps[:, 0:8], AF.Exp, [btpsk], [dk_])
            Qt, Qtk = bfr.next()
            Kt32, Kt32k = f32r.next()
            Ktb, Ktbk = bfr.next()
            Kh0, Kh0k = bfr.next()
            Kh1, Kh1k = bfr.next()
            TT(P, W(Qt), W(qs), W(eb), ALU.mult, [qsk, ebk], [Qtk])
            TT(P, W(Kt32), W(kk), W(enb), ALU.mult, [kkk, enbk], [Kt32k])
            CP(P, "dve", W(Ktb), W(Kt32), [Kt32k], [Ktbk])
            STT(P, W(Kh0), W(Kt32), SEL_f[:, 0:1], W(ebl), ALU.mult, ALU.mult, [Kt32k, eblk, "consts"], [Kh0k])
            STT(P, W(Kh1), W(Kt32), SEL_f[:, 1:2], W(ebl), ALU.mult, ALU.mult, [Kt32k, eblk, "consts"], [Kh1k])
            tp, tpk = PS(0)
            tpb = tp.bitcast(BF16)
            tp2, tp2k = PS(7)
            tpb2 = tp2.bitcast(BF16)
            for s_ in range(4):
                TR(P, tpb[:, s_ * 128:(s_ + 1) * 128], Qt[:, s_, :], C.ident_bf, [Qtk, "consts_bf"], [tpk])
                TR(P, tpb2[:, s_ * 128:(s_ + 1) * 128], Ktb[:, s_, :], C.ident_bf,
                   [Ktbk, "consts_bf"], [tp2k])
            QtT, QtTk = bfr.next()
            KtT, KtTk = bfr.next()
            ACT(P, W(QtT), tpb[:, 0:512], AF.Copy, [tpk], [QtTk])
            CP(P, "dve", W(KtT), tpb2[:, 0:512], [tp2k], [KtTk])
            aps, apsk = PS(1)
            for s_ in range(4):
                MM(P, aps[:, s_ * 128:(s_ + 1) * 128], KtT[:, s_, :], QtT[:, s_, :], True, True,
                   [KtTk, QtTk], [apsk])
            Am, Amk = bfr.next()
            TT(P, W(Am), aps[:], W(U4), ALU.mult, [apsk, "mx_U4"], [Amk])
            ops_, opsk = PS(4)
            for s_ in range(4):
                MM(P, ops_[:, s_ * 128:(s_ + 1) * 128], Ib[:, s_, :], Am[:, s_, :], True, False,
                   [Ibk, Amk], [opsk])
                for c_ in range(2):
                    col = s_ * 128 + c_ * 64
                    MM(P, ops_[:, col:col + 64], sb_t[:], QtT[:, s_, c_ * 64:(c_ + 1) * 64], False,
                       c_ == 1, [sb_k, QtTk], [opsk])
                    sups, supsk = PS(5)
                    Khc, Khck = (Kh0, Kh0k) if c_ == 0 else (Kh1, Kh1k)
                    MM(P, sups[:, 0:128], Khc[:, s_, :], Ib[:, s_, :], True, True, [Khck, Ibk], [supsk])
                    STT(P, S32[:], S32[:], d_t[:, 2 * s_ + c_:2 * s_ + c_ + 1], sups[:, 0:128],
                        ALU.mult, ALU.add, ["mx_S32", dk_, supsk], ["mx_S32"])
                    sb_t, sb_k = Sbf.next()
                    ACT(P, sb_t[:], S32[:], AF.Copy, ["mx_S32"], [sb_k])
            ot, otk = ostage.next()
            ACT(P, ot[:], ops_[:], AF.Copy, [opsk], [otk])
            DMA(P, "sp", ag2_in[h][T // 4][:, (T % 4) * 512:(T % 4 + 1) * 512], ot[:], [otk],
                ["ag2_in_%d_%d" % (h, T // 4)], "o_" + otk)
            if T % 4 == 3:
                gather(h, T // 4)
        if not (MIX_PARTS & 4):
            continue
        ACT(P, ls[:], fb[:], AF.Sigmoid, ["mx_fb", "mx_fbias"], ["mx_ls"], bias=fbias[:, h:h + 1])
        ACT(P, ls[:], ls[:], AF.Ln, ["mx_ls"], ["mx_ls"])
        def split3(dst, src, tmp, skey, dkey, tkey):
            CP(P, "dve", dst[:, 0, :], src, [skey], [dkey])
            TT(P, tmp, src, dst[:, 0, :], ALU.subtract, [skey, dkey], [tkey])
            CP(P, "dve", dst[:, 1, :], tmp, [tkey], [dkey])
            TT(P, tmp, tmp, dst[:, 1, :], ALU.subtract, [tkey, dkey], [tkey])
            CP(P, "dve", dst[:, 2, :], tmp, [tkey], [dkey])
        split3(l3, ls[:], r1[:], "mx_ls", "mx_l3", "mx_r1")
        UF_b = C.consts_bf[:, C_UF:C_UF + 128]
        SL_b = C.consts_bf[0:64, C_SL:C_SL + 64]
        ps, psk = PS(0)
        for a_ in range(3):
            MM(P, ps[0:64, 0:128], l3[:, a_, :], C.ones_bf, a_ == 0, a_ == 2, ["mx_l3", "consts_bf"], [psk])
        CP(P, "dve", totbc[:], ps[0:64, 0:128], [psk], ["mx_totbc"])
        split3(tb3, totbc[:], tbr[:], "mx_totbc", "mx_tb3", "mx_tbr")
        ps, psk = PS(1)
        for a_ in range(3):
            MM(P, ps[:, 0:64], tb3[:, a_, :], SL_b, a_ == 0, a_ == 2, ["mx_tb3", "consts_bf"], [psk])
        CP(P, "dve", offbc[:], ps[:, 0:64], [psk], ["mx_offbc"])
        ps2, ps2k = PS(2)
        for a_ in range(3):
            MM(P, ps2[:, 0:64], UF_b, l3[:, a_, :], a_ == 0, a_ == 2, ["mx_l3", "consts_bf"], [ps2k])
        TT(P, ccol[:], ps2[:, 0:64], offbc[:], ALU.add, [ps2k, "mx_offbc"], ["mx_ccol"])
        for G in range(16):
            nj = 4 * G + 4
            cref = offbc[:, 4 * G:4 * G + 1]
            TS(P, biasG[:, G, 0:nj], ccol[:, 0:nj], cref, -1.0, ALU.subtract, ALU.mult,
               ["mx_ccol", "mx_offbc"], ["mx_biasG"])
            TS(P, cd[:, 4 * G:4 * G + 4], ccol[:, 4 * G:4 * G + 4], cref, 1.0 / scale, ALU.subtract, ALU.mult,
               ["mx_ccol", "mx_offbc"], ["mx_cd"])
        CP(P, "dve", c3[:, 0, :], cd[:], ["mx_cd"], ["mx_c3"])
        TT(P, r1[:], cd[:], c3[:, 0, :], ALU.subtract, ["mx_cd", "mx_c3"], ["mx_r1"])
        CP(P, "dve", c3[:, 1, :], r1[:], ["mx_r1"], ["mx_c3"])
        TT(P, r1[:], r1[:], c3[:, 1, :], ALU.subtract, ["mx_r1", "mx_c3"], ["mx_r1"])
        CP(P, "dve", c3[:, 2, :], r1[:], ["mx_r1"], ["mx_c3"])
        tp, tpk = PS(3)
        tpb = tp.bitcast(BF16)
        for a in range(3):
            TR(P, tpb[0:64, a * 128:(a + 1) * 128], c3[:, a, :], C.ident_bf, ["mx_c3", "consts_bf"], [tpk])
        CP(P, "dve", c3T[:].rearrange("p a c -> p (a c)"), tpb[0:64, 0:384], [tpk], ["mx_c3T"])
        DMA(P, "sp", crow_d.rearrange("a (j t) -> j a t", j=64), c3T[:], ["mx_c3T"], ["crow_d"], "crow_d")
        for a in range(3):
            DMA(P, "sp", crow[32 * a:32 * a + 1, :], crow_d[a:a + 1, :], ["crow_d"], ["mx_crow"], "mx_crow")
        for G in range(16):
            q0 = G * 512
            o_ps = [PS(4), PS(5), PS(0), PS(1)]
            for j in range(4 * G + 4):
                i_ = j - 4 * G
                c0 = 0 if i_ < 0 else 128 * i_
                n = 512 - c0
                st, stk = PS(6 + (j % 2))
                MM(P, st[:, 0:n], kT[:, j * 128:(j + 1) * 128], qT[:, q0 + c0:q0 + 512], True, False,
                   ["mx_kT", "mx_qT"], [stk])
                MM(P, st[:, 0:n], ones96, crow[:, q0 + c0:q0 + 512], False, i_ < 0, ["mx_crow", "consts_bf"], [stk])
                if i_ >= 0:
                    MM(P, st[:, 0:n], C.ident_bf, MD_bf[:, 0:n], False, True, ["consts_bf"], [stk])
                p_t, pk = pTr.next()
                ACT(P, p_t[:, 0:n], st[:, 0:n], AF.Exp, [stk, "mx_biasG"], [pk], bias=biasG[:, G, j:j + 1], scale=scale)
                for qb in range(4):
                    if 128 * qb < c0:
                        continue
                    op_t, op_k = o_ps[qb]
                    oc = 0
                    MM(P, op_t[:, oc:oc + 129], p_t[:, 128 * qb - c0:128 * qb - c0 + 128], Vp[:, j, 0:129],
                       j == 0, j == 4 * G + qb, [pk, "mx_Vp"], [op_k])
                if i_ >= 0:
                    qb = i_
                    op_t, op_k = o_ps[qb]
                    oc = 0
                    okey = op_k
                    rc, rck = recr.next()
                    RECIP(P, rc[:], op_t[:, oc + 128:oc + 129], [okey], [rck])
                    ob_, obk = obr.next()
                    ACT(P, ob_[:], op_t[:, oc:oc + 128], AF.Identity, [okey, rck], [obk], scale=rc[:])
                    tp, tpk = PS(3)
                    tpb = tp.bitcast(BF16)
                    TR(P, tpb[:, 0:128], ob_[:], C.ident_bf, [obk, "consts_bf"], [tpk])
                    if qb == 0:
                        obT_t, obT_k = obT.next()
                    CP(P, "dve", obT_t[:, qb * 128:(qb + 1) * 128], tpb[:, 0:128], [tpk], [obT_k])
                    if qb == 3:
                        DMA(P, "sp", ag2_in[2 + h][G // 4][:, (G % 4) * 512:(G % 4 + 1) * 512], obT_t[:],
                            [obT_k], ["ag2_in_%d_%d" % (2 + h, G // 4)], "o_" + obT_k)
                        if G % 4 == 3:
                            gather(2 + h, G // 4)


_CACHE = {}


def _prep_inputs(inp):
    f = lambda a: np.ascontiguousarray(np.asarray(a, dtype=np.float32))
    x = f(inp["x"])
    mem = f(inp["mem"])
    w_in = f(inp["w_in"])[0]
    offs = [0, 1024, 2048, 3072, 4096, 5120, 6144, 7168, 7176, 9224]
    q_a, f_a, i_a, g_a, q_b, k_b, v_b, f_b, gates = [w_in[:, offs[i]:offs[i + 1]] for i in range(9)]
    gains = np.zeros((128, 96), np.float32)
    for i, n in enumerate(GAIN_NAMES):
        gains[:, 8 * i:8 * i + 8] = f(inp[n])[0].reshape(8, 128).T
    gains[:, 80:96] = f(inp["b_gate"])[0].reshape(16, 128).T
    consts = make_consts()
    w_loc = np.ascontiguousarray(np.concatenate([g_a, gates], axis=1))
    lbl_all = f(inp["hg_lb_logits"])
    fbias_all = f(inp["fox_f_bias"])[0]
    shared = {
        "ffn1_w_in": f(inp["ffn1_w_in"])[0], "ffn1_w_down": f(inp["ffn1_w_down"])[0],
        "ffn2_w_in": f(inp["ffn2_w_in"])[0], "ffn2_w_down": f(inp["ffn2_w_down"])[0],
        "w_loc": w_loc, "w_branch_a": f(inp["w_branch_a"])[0], "w_branch_b": f(inp["w_branch_b"])[0],
        "w_out": f(inp["w_out"])[0], "w_mq": f(inp["w_mq"])[0], "w_mkv": f(inp["w_mkv"])[0],
        "w_mo": f(inp["w_mo"])[0], "gains": gains, "consts": consts,
    }
    maps = []
    for c in range(8):
        b, r = c // 4, c % 4
        hs = [2 * r, 2 * r + 1]
        m = dict(shared)
        m["xT"] = np.ascontiguousarray(x[b, r * S_OWN:(r + 1) * S_OWN, :].T)
        m["memT"] = np.ascontiguousarray(mem[b].T)
        sl = lambda w, h: w[:, h * 128:(h + 1) * 128]
        m["w_hg"] = np.ascontiguousarray(np.concatenate(
            [np.concatenate([sl(q_a, h), sl(f_a, h), sl(i_a, h)], axis=1) for h in hs], axis=1))
        m["w_fq"] = np.ascontiguousarray(np.concatenate([sl(q_b, h) for h in hs], axis=1))
        m["w_fk"] = np.ascontiguousarray(np.concatenate([sl(k_b, h) for h in hs], axis=1))
        m["w_fv"] = np.ascontiguousarray(np.concatenate(
            [np.concatenate([sl(v_b, h), f_b[:, h:h + 1]], axis=1) for h in hs], axis=1))
        l0 = np.concatenate([lbl_all[0, h] for h in hs])
        l1 = np.concatenate([lbl_all[1, h] for h in hs])
        m["lbl"] = np.ascontiguousarray(np.tile(np.concatenate([l0, l1])[None, :], (128, 1)))
        m["fbias"] = np.ascontiguousarray(np.tile(fbias_all[hs][None, :], (128, 1)))
        maps.append(m)
    return maps


def kernel(**inputs):
    dbg = bool(inputs.pop("_dbg", False))
    key = ("nc", dbg)
    if key not in _CACHE:
        _CACHE[key] = build_program(dbg)
    nc = _CACHE[key]
    maps = _prep_inputs(inputs)
    res = run_bass_kernel_spmd(nc, maps, core_ids=list(range(8)))
    out = np.zeros((2, S_ALL, D), np.float32)
    for c in range(8):
        b, r = c // 4, c % 4
        out[b, r * S_OWN:(r + 1) * S_OWN, :] = np.asarray(res.results[c]["outT"]).T
    if dbg:
        return out, res.results
    return out
```

```python
import contextlib
import math
import numpy as np
import concourse.bass as bass
import concourse.mybir as mybir
from concourse.bass_utils import run_bass_kernel_spmd

F32 = mybir.dt.float32
BF16 = mybir.dt.bfloat16
AF = mybir.ActivationFunctionType
ALU = mybir.AluOpType

SAME_ENGINE_SYNC = False
STOP_AFTER = 9
MIX_PARTS = 7
NOAG2 = False
EPS = 1e-6
D = 1024
S_OWN = 2048
S_ALL = 8192
DFF = 2816
NEG = -1.0e9


class Op:
    __slots__ = ("eng", "fn", "deps", "is_dma", "semkey", "idx", "signal", "count",
                 "dma_waits", "inc", "seg")


class Prog:
    def __init__(self, nc, es):
        self.nc = nc
        self.es = es
        self.ops = []
        self.lastw = {}
        self.readers = {}
        self.dma_tot = {}
        self.final_waits = {}
        self.last_on = {}
        self.seg = 0

    def add(self, eng, fn, reads=(), writes=(), dma=None, inc=16):
        if dma is not None:
            dma = "cc" if inc == 1 else ("q_" + eng)
        deps = {}
        for r in reads:
            w = self.lastw.get(r)
            if w is not None:
                deps[w.idx] = w
        for r in writes:
            w = self.lastw.get(r)
            if w is not None:
                deps[w.idx] = w
            for rd in self.readers.get(r, {}).values():
                deps[rd.idx] = rd
        op = Op()
        op.eng = eng
        op.fn = fn
        op.is_dma = dma is not None
        op.semkey = dma
        op.inc = inc
        op.idx = len(self.ops)
        op.signal = False
        op.count = 0
        op.seg = self.seg
        dma_waits = {}
        cdeps = []
        for d in deps.values():
            if d.is_dma:
                dma_waits[d.semkey] = self.dma_tot[d.semkey]
            elif d.fn is not None:
                cdeps.append(d)
        op.deps = cdeps
        op.dma_waits = dma_waits
        if dma is not None:
            self.dma_tot[dma] = self.dma_tot.get(dma, 0) + inc
        for r in reads:
            self.readers.setdefault(r, {})[(eng, op.idx) if op.is_dma else eng] = op
        for r in writes:
            self.lastw[r] = op
            self.readers[r] = {}
        self.ops.append(op)
        if not op.is_dma and fn is not None:
            self.last_on[eng] = op
        return op

    def barrier(self):
        lasts = list(self.last_on.values())
        tot = dict(self.dma_tot)
        for e in ["pe", "act", "dve", "pool", "sp"]:
            op = Op()
            op.eng = e
            op.fn = None
            op.is_dma = False
            op.semkey = None
            op.inc = 0
            op.idx = len(self.ops)
            op.signal = False
            op.count = 0
            op.deps = [d for d in lasts if d.eng != e]
            op.dma_waits = dict(tot)
            op.seg = self.seg
            self.ops.append(op)
        self.lastw = {}
        self.readers = {}
        self.last_on = {}
        self.seg += 1

    def wait_dma_at_end(self, key):
        for k in self.dma_tot:
            self.final_waits[k] = self.dma_tot[k]

    def emit(self):
        nc = self.nc
        engs = ["pe", "act", "dve", "pool", "sp"]
        for op in self.ops:
            for d in op.deps:
                if d.eng == op.eng and (op.eng == "pe" or not SAME_ENGINE_SYNC):
                    continue
                d.signal = True
        cnt = {}
        for op in self.ops:
            if op.is_dma or op.fn is None:
                continue
            if op.signal:
                k_ = (0, op.eng)
                cnt[k_] = cnt.get(k_, 0) + 1
                op.count = cnt[k_]
        sems = {k_: self.es.enter_context(nc.semaphore("s_%d_%s" % k_)) for k_ in cnt}
        dsems = {}
        for k in self.dma_tot:
            dsems[k] = self.es.enter_context(nc.semaphore("d_" + str(len(dsems))))
        per_eng = {e: [o for o in self.ops if o.eng == e] for e in engs}
        self.stats = {e: len(per_eng[e]) for e in engs}
        self.stats["sems"] = len(dsems) + len(sems)
        self.stats["counts"] = max(cnt.values()) if cnt else 0
        final_waits = self.final_waits

        def run(e, engine):
            waited = {}
            dwaited = {}
            for op in per_eng[e]:
                need = {}
                for d in op.deps:
                    if d.eng == e and (e == "pe" or not SAME_ENGINE_SYNC):
                        continue
                    k_ = (0, d.eng)
                    if d.count > need.get(k_, 0):
                        need[k_] = d.count
                for x, v in need.items():
                    if v > waited.get(x, 0):
                        engine.wait_ge(sems[x], v)
                        waited[x] = v
                for k, v in op.dma_waits.items():
                    if v > dwaited.get(k, 0):
                        engine.wait_ge(dsems[k], v)
                        dwaited[k] = v
                if op.fn is None:
                    continue
                inst = op.fn(engine)
                if op.is_dma:
                    inst.then_inc(dsems[op.semkey], op.inc)
                elif op.signal:
                    inst.then_inc(sems[(0, e)], 1)
            if e == "sp":
                for k, v in final_waits.items():
                    engine.wait_ge(dsems[k], v)

        with nc.Block() as block:
            @block.tensor
            def _(t):
                run("pe", t)

            @block.scalar
            def _(t):
                run("act", t)

            @block.vector
            def _(t):
                run("dve", t)

            @block.gpsimd
            def _(t):
                run("pool", t)

            @block.sync
            def _(t):
                run("sp", t)


_uid = [0]


def _nm(n):
    _uid[0] += 1
    return "%s_u%d" % (n, _uid[0])


_RANK = {}


def _rank(e):
    if id(e) not in _RANK:
        _RANK[id(e)] = (e, (e.partition_id() % 4) * 512)
    return _RANK[id(e)][1]


class Ring:
    def __init__(self, nc, es, name, n, shape, dtype, psum=False):
        self.bufs = []
        for i in range(n):
            nm = "%s%d" % (name, i)
            if psum:
                t = es.enter_context(nc.psum_tensor(_nm(nm), shape, dtype))
            else:
                t = es.enter_context(nc.sbuf_tensor(_nm(nm), shape, dtype))
            self.bufs.append((t, nm))
        self.i = 0

    def next(self):
        b = self.bufs[self.i % len(self.bufs)]
        self.i += 1
        return b


def MM(P, out, lhsT, rhs, start, stop, reads, writes):
    P.add("pe", lambda e: e.matmul(out, lhsT, rhs, start=start, stop=stop), reads, writes)


def TR(P, out, in_, ident, reads, writes):
    P.add("pe", lambda e: e.transpose(out, in_, ident), reads, writes)


def ACT(P, out, in_, func, reads, writes, bias=None, scale=None):
    kw = {}
    if bias is not None:
        kw["bias"] = bias
    if scale is not None:
        kw["scale"] = scale
    P.add("act", lambda e: e.activation(out=out, in_=in_, func=func, **kw), reads, writes)


def TT(P, out, in0, in1, op, reads, writes, eng="dve"):
    P.add(eng, lambda e: e.tensor_tensor(out=out, in0=in0, in1=in1, op=op), reads, writes)


def TS(P, out, in0, s1, s2, op0, op1, reads, writes, eng="dve"):
    if op1 is None:
        P.add(eng, lambda e: e.tensor_scalar(out=out, in0=in0, scalar1=s1, scalar2=None, op0=op0),
              reads, writes)
    else:
        P.add(eng, lambda e: e.tensor_scalar(out=out, in0=in0, scalar1=s1, scalar2=s2, op0=op0, op1=op1),
              reads, writes)


def STT(P, out, in0, scalar, in1, op0, op1, reads, writes):
    P.add("dve", lambda e: e.scalar_tensor_tensor(out=out, in0=in0, scalar=scalar, in1=in1,
                                                  op0=op0, op1=op1), reads, writes)


def CP(P, eng, out, in_, reads, writes):
    P.add(eng, lambda e: e.tensor_copy(out=out, in_=in_), reads, writes)


def RECIP(P, out, in_, reads, writes):
    P.add("dve", lambda e: e.reciprocal(out=out, in_=in_), reads, writes)


def MSET(P, eng, ap, val, writes):
    P.add(eng, lambda e: e.memset(ap, val), (), writes)


def DMA(P, q, out, in_, reads, writes, key):
    P.add(q, lambda e: e.dma_start(out=out, in_=in_), reads, writes, dma=key)


GAIN_NAMES = ["ffn1_pre_g", "ffn1_post_g", "mix_pre_g", "hg_norm_g", "mix_post_g",
              "mem_pre_g", "mem_kv_g", "mem_post_g", "ffn2_pre_g", "ffn2_post_g"]
C_ID, C_U, C_BO, C_UF, C_ONE, C_SEL, C_SL, C_MD = 0, 128, 256, 384, 512, 640, 642, 706
C_TOT = 706 + 512


def make_consts():
    c = np.zeros((128, C_TOT), np.float32)
    i = np.arange(128)
    c[:, C_ID:C_ID + 128] = np.eye(128)
    same = (i[:, None] // 64) == (i[None, :] // 64)
    c[:, C_U:C_U + 128] = ((i[:, None] <= i[None, :]) & same)
    c[:, C_BO:C_BO + 128] = same
    c[:, C_UF:C_UF + 128] = (i[:, None] <= i[None, :])
    c[:, C_ONE:C_ONE + 128] = 1.0
    c[:, C_SEL + 0] = (i < 64)
    c[:, C_SEL + 1] = (i >= 64)
    j = np.arange(64)
    c[:64, C_SL:C_SL + 64] = (j[:, None] < j[None, :])
    jj = np.arange(512)
    c[:, C_MD:C_MD + 512] = np.where(jj[None, :] < i[:, None], NEG, 0.0)
    return c


class Ctx:
    pass


def rms_rstd(P, C, src, src_key, cols, ncols=512, dim=1024.0):
    sq, sqk = C.sq.next()
    ACT(P, sq[:, :, 0:ncols], src[:, :, cols], AF.Square, [src_key], [sqk])
    ps, psk = C.ps.next()
    for kt in range(8):
        MM(P, ps[:, 0:ncols], C.ones_bf[:], sq[:, kt, 0:ncols], kt == 0, kt == 7, [sqk, "consts_bf"], [psk])
    r, rk = C.rs.next()
    TS(P, r[:, 0:ncols], ps[:, 0:ncols], 1.0 / dim, EPS, ALU.mult, ALU.add, [psk], [rk])
    ACT(P, r[:, 0:ncols], r[:, 0:ncols], AF.Sqrt, [rk], [rk])
    RECIP(P, r[:, 0:ncols], r[:, 0:ncols], [rk], [rk])
    return r, rk


def load_w(P, C, w_d, kt_n, c0, ncols, dst_c0=0, buf=None):
    if buf is None:
        buf = C.wring.next()
    t, k = buf
    return buf


def wview(t, kt_n, width):
    return t[:, 0:kt_n * width].rearrange("p (k c) -> p k c", k=kt_n)


def wload(P, t, k, w_d, kt_n, width, c0, ncols, dst_c0):
    v = wview(t, kt_n, width)
    src = w_d[:, c0:c0 + ncols].rearrange("(k p) c -> p k c", p=128)
    DMA(P, "pool", v[:, :, dst_c0:dst_c0 + ncols], src, [], [k], "w_" + k)


def norm_h(P, C, hT, hk, hcols, xcols, g_col0, n=512):
    r, rk = rms_rstd(P, C, C.xT, "xT", xcols, n)
    for kt in range(8):
        STT(P, hT[:, kt, hcols], C.xT[:, kt, xcols], C.gains[:, g_col0 + kt:g_col0 + kt + 1],
            r[:, 0:n], ALU.mult, ALU.mult, ["xT", rk, "gains"], [hk])


def post_norm_add(P, C, yT, yk, ycols, xcols, g_col0, n=512):
    r, rk = rms_rstd(P, C, yT, yk, ycols, n)
    for kt in range(8):
        t, tk = C.tmp.next()
        STT(P, t[:, 0:n], yT[:, kt, ycols], C.gains[:, g_col0 + kt:g_col0 + kt + 1],
            r[:, 0:n], ALU.mult, ALU.mult, [yk, rk, "gains"], [tk])
        TT(P, C.xT[:, kt, xcols], C.xT[:, kt, xcols], t[:, 0:n], ALU.add, ["xT", tk], ["xT"])


def ffn(P, C, es, nc, w_in_d, w_dn_d, g_pre, g_post_h):
    yT = es.enter_context(nc.sbuf_tensor(_nm("ffn_hy"), [128, 8, 1024], F32))
    hT = yT.bitcast(BF16)
    actT = es.enter_context(nc.sbuf_tensor(_nm("ffn_actT"), [128, 22, 1024], BF16))
    sg_ring = Ring(nc, es, "ffn_sg", 2, [128, 512], F32)
    for half in range(2):
        t0 = half * 1024
        for tt in range(2):
            norm_h(P, C, hT, "ffn_hy", slice(tt * 512, (tt + 1) * 512),
                   slice(t0 + tt * 512, t0 + (tt + 1) * 512), g_pre)
        for ch in range(11):
            wt, wk = C.wring.next()
            wload(P, wt, wk, w_in_d, 8, 512, ch * 256, 256, 0)
            wload(P, wt, wk, w_in_d, 8, 512, DFF + ch * 256, 256, 256)
            wv = wview(wt, 8, 512)
            for jj in range(2):
                j = ch * 2 + jj
                for tt in range(2):
                    tc_ = slice(tt * 512, (tt + 1) * 512)
                    pg, pgk = C.ps.next()
                    for kt in range(8):
                        MM(P, pg[:], wv[:, kt, jj * 128:(jj + 1) * 128], hT[:, kt, tc_], kt == 0, kt == 7,
                           [wk, "ffn_hy"], [pgk])
                    pu, puk = C.ps.next()
                    for kt in range(8):
                        MM(P, pu[:], wv[:, kt, 256 + jj * 128:256 + (jj + 1) * 128], hT[:, kt, tc_],
                           kt == 0, kt == 7, [wk, "ffn_hy"], [puk])
                    sg, sgk = sg_ring.next()
                    ACT(P, sg[:], pg[:], AF.Silu, [pgk], [sgk])
                    TT(P, actT[:, j, tc_], pu[:], sg[:], ALU.mult, [puk, sgk], ["ffn_actT%d" % j])
        akeys = ["ffn_actT%d" % j for j in range(22)]
        for dc in range(4):
            wt, wk = C.wring.next()
            wload(P, wt, wk, w_dn_d, 22, 256, dc * 256, 256, 0)
            wv = wview(wt, 22, 256)
            for dd in range(2):
                d = dc * 2 + dd
                for tt in range(2):
                    tc_ = slice(tt * 512, (tt + 1) * 512)
                    py, pyk = C.ps.next()
                    for j in range(22):
                        MM(P, py[:], wv[:, j, dd * 128:(dd + 1) * 128], actT[:, j, tc_], j == 0, j == 21,
                           [wk] + akeys, [pyk])
                    ACT(P, yT[:, d, tc_], py[:], AF.Copy, [pyk], ["ffn_hy"])
        for tt in range(2):
            post_norm_add(P, C, yT, "ffn_hy", slice(tt * 512, (tt + 1) * 512),
                          slice(t0 + tt * 512, t0 + (tt + 1) * 512), g_post_h)


def build_program(dbg=False):
    _RANK.clear()
    nc = bass.Bass("TRN2", target_bir_lowering=False)

    def din(name, shape, dt=F32):
        return nc.dram_tensor(name, shape, dt, kind="ExternalInput").ap()

    xT_d = din("xT", [D, S_OWN])
    memT_d = din("memT", [D, 256])
    ffn1_in = din("ffn1_w_in", [D, 2 * DFF])
    ffn1_dn = din("ffn1_w_down", [DFF, D])
    ffn2_in = din("ffn2_w_in", [D, 2 * DFF])
    ffn2_dn = din("ffn2_w_down", [DFF, D])
    w_loc_d = din("w_loc", [D, 3072])
    w_hg_d = din("w_hg", [D, 768])
    w_fq_d = din("w_fq", [D, 256])
    w_fk_d = din("w_fk", [D, 256])
    w_fv_d = din("w_fv", [D, 258])
    w_a_d = din("w_branch_a", [D, D])
    w_b_d = din("w_branch_b", [D, D])
    w_o_d = din("w_out", [D, D])
    w_mq_d = din("w_mq", [D, D])
    w_mkv_d = din("w_mkv", [D, 2 * D])
    w_mo_d = din("w_mo", [D, D])
    gains_d = din("gains", [128, 96])
    consts_d = din("consts", [128, C_TOT])
    lbl_d = din("lbl", [128, 512])
    fbias_d = din("fbias", [128, 2])
    outT_d = nc.dram_tensor("outT", [D, S_OWN], F32, kind="ExternalOutput").ap()
    dbg_x1 = dbg_o = None
    if dbg:
        dbg_x1 = nc.dram_tensor("dbg_x1", [D, S_OWN], F32, kind="ExternalOutput").ap()
        dbg_o = nc.dram_tensor("dbg_o", [512, S_ALL], F32, kind="ExternalOutput").ap()
        dbg_x2 = nc.dram_tensor("dbg_x2", [D, S_OWN], F32, kind="ExternalOutput").ap()
        dbg_x3 = nc.dram_tensor("dbg_x3", [D, S_OWN], F32, kind="ExternalOutput").ap()

    ag1_in = [nc.dram_tensor("ag1_in%d" % i, [256, S_OWN], BF16).ap() for i in range(4)]
    ag1_out = [nc.dram_tensor("ag1_out%d" % i, [1024, S_OWN], BF16).ap() for i in range(4)]
    ag2_in = [[nc.dram_tensor("ag2_in%d_%d" % (q, tq), [128, S_OWN], F32).ap() for tq in range(4)] for q in range(4)]
    ag2_out = [nc.dram_tensor("ag2_out%d" % q, [4 * 512, S_OWN], F32).ap() for q in range(4)]
    x_spill = nc.dram_tensor("x_spill", [D, S_OWN], F32).ap()
    crow_d = nc.dram_tensor("crow_d", [3, 64 * 128], BF16).ap()
    RG = [[0, 1, 2, 3], [4, 5, 6, 7]]

    with contextlib.ExitStack() as es_all:
        P = Prog(nc, es_all)
        C = Ctx()
        C.consts = es_all.enter_context(nc.sbuf_tensor(_nm("consts_sb"), [128, C_TOT], F32))
        C.consts_bf = es_all.enter_context(nc.sbuf_tensor(_nm("consts_bf_sb"), [128, C_TOT], BF16))
        C.gains = es_all.enter_context(nc.sbuf_tensor(_nm("gains_sb"), [128, 96], F32))
        C.ones_bf = C.consts_bf[:, C_ONE:C_ONE + 128]
        C.ident_bf = C.consts_bf[:, C_ID:C_ID + 128]
        C.ps = Ring(nc, es_all, "ps", 8, [128, 512], F32, psum=True)
        DMA(P, "sp", C.consts[:], consts_d, [], ["consts"], "consts")
        DMA(P, "sp", C.gains[:], gains_d, [], ["gains"], "gains")
        CP(P, "dve", C.consts_bf[:], C.consts[:], ["consts"], ["consts_bf"])
        for g0 in (8, 72):
            TS(P, C.gains[:, g0:g0 + 8], C.gains[:, g0:g0 + 8], 0.5, None, ALU.mult, None, ["gains"], ["gains"])

        with contextlib.ExitStack() as es:
            C.xT = es.enter_context(nc.sbuf_tensor(_nm("xT_sb"), [128, 8, S_OWN], F32))
            DMA(P, "sp", C.xT[:], xT_d.rearrange("(k p) t -> p k t", p=128), [], ["xT"], "xT")
            C.sq = Ring(nc, es, "sq", 1, [128, 8, 512], BF16)
            C.rs = Ring(nc, es, "rs", 2, [128, 512], F32)
            C.tmp = Ring(nc, es, "tmp", 2, [128, 512], F32)
            C.wring = Ring(nc, es, "wr", 3, [128, 5632], BF16)
            with contextlib.ExitStack() as es2:
                ffn(P, C, es2, nc, ffn1_in, ffn1_dn, 0, 8)
            P.barrier()
            if dbg:
                DMA(P, "sp", dbg_x1.rearrange("(k p) t -> p k t", p=128), C.xT[:], ["xT"], [], "dbg")
            if STOP_AFTER == 1:
                DMA(P, "sp", outT_d.rearrange("(k p) t -> p k t", p=128), C.xT[:], ["xT"], [], "out")
                P.wait_dma_at_end("out")
                if dbg:
                    P.wait_dma_at_end("dbg")
                P.emit()
                build_program.stats = P.stats
                return nc
            with contextlib.ExitStack() as es2:
                hT = es2.enter_context(nc.sbuf_tensor(_nm("mix_hT"), [128, 8, S_OWN], BF16))
                for tt in range(4):
                    sl = slice(tt * 512, (tt + 1) * 512)
                    norm_h(P, C, hT, "mix_hT", sl, sl, 16)
                for ck in range(4):
                    DMA(P, "sp", ag1_in[ck].rearrange("(k p) t -> p k t", p=128), hT[:, 2 * ck:2 * ck + 2, :],
                        ["mix_hT"], ["ag1_in"], "ag1_in")
                DMA(P, "sp", x_spill.rearrange("(k p) t -> p k t", p=128), C.xT[:], ["xT"], ["x_spill"], "x_spill")
                for ck in range(4):
                    P.add("pool", lambda e, ck=ck: e.collective_compute(
                        "AllGather", ALU.bypass, replica_groups=RG, ins=[ag1_in[ck].opt()], outs=[ag1_out[ck].opt()]),
                        ["ag1_in"], ["ag1_out"], dma="agc", inc=1)
                P.barrier()
        if STOP_AFTER == 2:
            DMA(P, "sp", outT_d, x_spill, [], [], "out")
            P.wait_dma_at_end("out")
            if dbg:
                P.wait_dma_at_end("dbg")
            P.emit()
            build_program.stats = P.stats
            return nc
        with contextlib.ExitStack() as es:
            mixers(P, C, es, nc, ag1_out, ag2_in, w_hg_d, w_fq_d, w_fk_d, w_fv_d, lbl_d, fbias_d, crow_d, ag2_out, RG)
            P.barrier()
        if dbg:
            for q in range(4):
                for tq in range(4):
                    DMA(P, "sp", dbg_o[q * 128:(q + 1) * 128, tq * 2048:(tq + 1) * 2048], ag2_in[q][tq], [], [], "dbg")
        if STOP_AFTER == 3:
            DMA(P, "sp", outT_d, x_spill, [], [], "out")
            P.wait_dma_at_end("out")
            if dbg:
                P.wait_dma_at_end("dbg")
            P.emit()
            build_program.stats = P.stats
            return nc
        with contextlib.ExitStack() as es:
            C.xT = es.enter_context(nc.sbuf_tensor(_nm("xTb"), [128, 8, S_OWN], F32))
            DMA(P, "sp", C.xT[:], x_spill.rearrange("(k p) t -> p k t", p=128), [], ["xT"], "xT2")
            C.sq = Ring(nc, es, "sqb", 1, [128, 8, 512], BF16)
            C.rs = Ring(nc, es, "rsb", 2, [128, 512], F32)
            C.tmp = Ring(nc, es, "tmpb", 2, [128, 512], F32)
            C.wring = Ring(nc, es, "wrb", 3, [128, 5632], BF16)
            with contextlib.ExitStack() as es2:
                post_mixer(P, C, es2, nc, ag2_out, w_loc_d, w_a_d, w_b_d, w_o_d)
            P.barrier()
            if dbg:
                DMA(P, "sp", dbg_x2.rearrange("(k p) t -> p k t", p=128), C.xT[:], ["xT"], [], "dbg")
            with contextlib.ExitStack() as es2:
                mem_attn(P, C, es2, nc, memT_d, w_mq_d, w_mkv_d, w_mo_d)
            P.barrier()
            if dbg:
                DMA(P, "sp", dbg_x3.rearrange("(k p) t -> p k t", p=128), C.xT[:], ["xT"], [], "dbg")
            with contextlib.ExitStack() as es2:
                ffn(P, C, es2, nc, ffn2_in, ffn2_dn, 64, 72)
            DMA(P, "sp", outT_d.rearrange("(k p) t -> p k t", p=128), C.xT[:], ["xT"], [], "out")
            P.wait_dma_at_end("out")
            if dbg:
                P.wait_dma_at_end("dbg")
        P.emit()
        build_program.stats = P.stats
    return nc


def stream_proj(P, C, w_d, c0, ncols_total, rhs, rhs_keys, epilogue, kt_n=8):
    done = 0
    i = 0
    while done < ncols_total:
        n = min(512, ncols_total - done)
        wt, wk = C.wring.next()
        wload(P, wt, wk, w_d, kt_n, 512, c0 + done, n, 0)
        wv = wview(wt, kt_n, 512)
        for jj in range(n // 128):
            ps, psk = C.ps.next()
            for kt in range(kt_n):
                MM(P, ps[:], wv[:, kt, jj * 128:(jj + 1) * 128], rhs[:, kt, :], kt == 0, kt == kt_n - 1,
                   [wk] + rhs_keys, [psk])
            epilogue(i, ps, psk)
            i += 1
        done += n


def post_mixer(P, C, es, nc, ag2_out, w_loc_d, w_a_d, w_b_d, w_o_d):
    sb = lambda n, s, d: es.enter_context(nc.sbuf_tensor(_nm(n), s, d))
    oa = sb("pm_oa", [128, 8, 512], F32)
    ob = sb("pm_ob", [128, 8, 512], BF16)
    hT = sb("pm_hT", [128, 8, 512], BF16)
    ga = sb("pm_ga", [128, 8, 512], F32)
    gB = sb("pm_gB", [128, 8, 512], F32)
    oan = sb("pm_oan", [128, 8, 512], BF16)
    yg = sb("pm_yg", [128, 8, 512], BF16)
    ya = ga
    zT = oa
    o_own = nc.dram_tensor("o_own", [2048, S_OWN], F32).ap()
    dst = o_own.rearrange("(a r q p) t -> a q r p t", a=2, r=4, q=2, p=128)
    for q in range(4):
        def ld(e, q=q):
            roff = _rank(e)
            return e.dma_start(out=dst[q // 2, q % 2],
                               in_=ag2_out[q][bass.ds(roff, 512), :].rearrange("(r p) t -> r p t", p=128))
        P.add("pool", ld, [], ["o_own"], dma="o_own")
    for tt in range(4):
        sl = slice(tt * 512, (tt + 1) * 512)

        DMA(P, "sp", oa[:], o_own[0:1024, sl].rearrange("(h p) t -> p h t", p=128), ["o_own"], ["pm_oa"], "pm_oa")
        DMA(P, "pool", ob[:], o_own[1024:2048, sl].rearrange("(h p) t -> p h t", p=128), ["o_own"], ["pm_ob"], "pm_ob")
        norm_h(P, C, hT, "pm_hT", slice(0, 512), sl, 16)

        def ep_ga(i, ps, psk):
            ACT(P, ga[:, i, :], ps[:], AF.Silu, [psk], ["pm_ga"])
        stream_proj(P, C, w_loc_d, 0, 1024, hT, ["pm_hT"], ep_ga)
        r, rk = rms_rstd(P, C, oa, "pm_oa", slice(0, 512))
        for kt in range(8):
            t, tk = C.tmp.next()
            STT(P, t[:], oa[:, kt, :], C.gains[:, 24 + kt:25 + kt], r[:], ALU.mult, ALU.mult,
                ["pm_oa", rk, "gains"], [tk])
            TT(P, oan[:, kt, :], t[:], ga[:, kt, :], ALU.mult, [tk, "pm_ga"], ["pm_oan"])

        def ep_a(i, ps, psk):
            ACT(P, ya[:, i, :], ps[:], AF.Copy, [psk], ["pm_ga"])
        stream_proj(P, C, w_a_d, 0, 1024, oan, ["pm_oan"], ep_a)

        def ep_gA(i, ps, psk):
            t, tk = C.tmp.next()
            ACT(P, t[:], ps[:], AF.Sigmoid, [psk, "gains"], [tk], bias=C.gains[:, 80 + i:81 + i])
            TT(P, ya[:, i, :], ya[:, i, :], t[:], ALU.mult, [tk, "pm_ga"], ["pm_ga"])
        stream_proj(P, C, w_loc_d, 1024, 1024, hT, ["pm_hT"], ep_gA)

        def ep_gB(i, ps, psk):
            ACT(P, gB[:, i, :], ps[:], AF.Sigmoid, [psk, "gains"], ["pm_gB"], bias=C.gains[:, 88 + i:89 + i])
        stream_proj(P, C, w_loc_d, 2048, 1024, hT, ["pm_hT"], ep_gB)

        def ep_b(i, ps, psk):
            t, tk = C.tmp.next()
            TT(P, t[:], ps[:], gB[:, i, :], ALU.mult, [psk, "pm_gB"], [tk])
            TT(P, yg[:, i, :], t[:], ya[:, i, :], ALU.add, [tk, "pm_ga"], ["pm_yg"])
        stream_proj(P, C, w_b_d, 0, 1024, ob, ["pm_ob"], ep_b)

        def ep_o(i, ps, psk):
            ACT(P, zT[:, i, :], ps[:], AF.Copy, [psk], ["pm_oa"])
        stream_proj(P, C, w_o_d, 0, 1024, yg, ["pm_yg"], ep_o)
        post_norm_add(P, C, zT, "pm_oa", slice(0, 512), sl, 32)


def mem_attn(P, C, es, nc, memT_d, w_mq_d, w_mkv_d, w_mo_d):
    sb = lambda n, s, d: es.enter_context(nc.sbuf_tensor(_nm(n), s, d))
    memT = sb("ma_memT", [128, 8, 256], F32)
    memn = sb("ma_memn", [128, 8, 256], BF16)
    kmT = sb("ma_kmT", [128, 8, 256], BF16)
    vm = sb("ma_vm", [128, 2, 1024], BF16)
    hT = sb("ma_hT", [128, 8, 512], BF16)
    qmT = sb("ma_qmT", [128, 8, 512], BF16)
    omT = sb("ma_omT", [128, 8, 512], BF16)
    yT = sb("ma_yT", [128, 8, 512], F32)
    pT = Ring(nc, es, "ma_pT", 2, [128, 2, 512], BF16)
    rec = Ring(nc, es, "ma_rec", 2, [128, 512], F32)
    DMA(P, "sp", memT[:], memT_d.rearrange("(k p) t -> p k t", p=128), [], ["ma_memT"], "ma_memT")
    sq, sqk = C.sq.next()
    ACT(P, sq[:, :, 0:256], memT[:], AF.Square, ["ma_memT"], [sqk])
    ps, psk = C.ps.next()
    for kt in range(8):
        MM(P, ps[:, 0:256], C.ones_bf[:], sq[:, kt, 0:256], kt == 0, kt == 7, [sqk, "consts_bf"], [psk])
    r, rk = C.rs.next()
    TS(P, r[:, 0:256], ps[:, 0:256], 1.0 / 1024.0, EPS, ALU.mult, ALU.add, [psk], [rk])
    ACT(P, r[:, 0:256], r[:, 0:256], AF.Sqrt, [rk], [rk])
    RECIP(P, r[:, 0:256], r[:, 0:256], [rk], [rk])
    for kt in range(8):
        STT(P, memn[:, kt, :], memT[:, kt, :], C.gains[:, 48 + kt:49 + kt], r[:, 0:256], ALU.mult, ALU.mult,
            ["ma_memT", rk, "gains"], ["ma_memn"])
    for ch in range(2):
        wt, wk = C.wring.next()
        wload(P, wt, wk, w_mkv_d, 8, 512, ch * 512, 512, 0)
        wv = wview(wt, 8, 512)
        for jj in range(4):
            ps, psk = C.ps.next()
            for kt in range(8):
                MM(P, ps[:, 0:256], wv[:, kt, jj * 128:(jj + 1) * 128], memn[:, kt, :], kt == 0, kt == 7,
                   [wk, "ma_memn"], [psk])
            ACT(P, kmT[:, ch * 4 + jj, :], ps[:, 0:256], AF.Copy, [psk], ["ma_kmT"])
    for ch in range(2):
        wt, wk = C.wring.next()
        wload(P, wt, wk, w_mkv_d, 8, 512, 1024 + ch * 512, 512, 0)
        wv = wview(wt, 8, 512)
        for mt in range(2):
            ps, psk = C.ps.next()
            for kt in range(8):
                MM(P, ps[:], memn[:, kt, mt * 128:(mt + 1) * 128], wv[:, kt, :], kt == 0, kt == 7,
                   [wk, "ma_memn"], [psk])
            ACT(P, vm[:, mt, ch * 512:(ch + 1) * 512], ps[:], AF.Copy, [psk], ["ma_vm"])
    scale = 1.0 / math.sqrt(256.0)
    for tt in range(4):
        sl = slice(tt * 512, (tt + 1) * 512)
        norm_h(P, C, hT, "ma_hT", slice(0, 512), sl, 40)

        def ep_q(i, ps, psk):
            ACT(P, qmT[:, i, :], ps[:], AF.Copy, [psk], ["ma_qmT"])
        stream_proj(P, C, w_mq_d, 0, 1024, hT, ["ma_hT"], ep_q)
        for hh in range(4):
            p_t, pk = pT.next()
            for mt in range(2):
                ps, psk = C.ps.next()
                for a in range(2):
                    MM(P, ps[:], kmT[:, 2 * hh + a, mt * 128:(mt + 1) * 128], qmT[:, 2 * hh + a, :],
                       a == 0, a == 1, ["ma_kmT", "ma_qmT"], [psk])
                ACT(P, p_t[:, mt, :], ps[:], AF.Exp, [psk], [pk], scale=scale)
            ps, psk = C.ps.next()
            for mt in range(2):
                MM(P, ps[:], C.ones_bf[:], p_t[:, mt, :], mt == 0, mt == 1, [pk, "consts_bf"], [psk])
            rc, rck = rec.next()
            RECIP(P, rc[:], ps[:], [psk], [rck])
            for a in range(2):
                ps, psk = C.ps.next()
                for mt in range(2):
                    MM(P, ps[:], vm[:, mt, (2 * hh + a) * 128:(2 * hh + a + 1) * 128], p_t[:, mt, :],
                       mt == 0, mt == 1, [pk, "ma_vm"], [psk])
                TT(P, omT[:, 2 * hh + a, :], ps[:], rc[:], ALU.mult, [psk, rck], ["ma_omT"])

        def ep_o(i, ps, psk):
            ACT(P, yT[:, i, :], ps[:], AF.Copy, [psk], ["ma_yT"])
        stream_proj(P, C, w_mo_d, 0, 1024, omT, ["ma_omT"], ep_o)
        post_norm_add(P, C, yT, "ma_yT", slice(0, 512), sl, 56)


def mixers(P, C, es, nc, ag1_out, ag2_in, w_hg_d, w_fq_d, w_fk_d, w_fv_d, lbl_d, fbias_d, crow_d, ag2_out, RG):
    def gather(q, tq):
        P.add("pool", lambda e: e.collective_compute(
            "AllGather", ALU.bypass, replica_groups=RG, ins=[ag2_in[q][tq].opt()],
            outs=[ag2_out[q][tq * 512:(tq + 1) * 512, :].opt()]),
            ["ag2_in_%d_%d" % (q, tq)], ["ag2_out"], dma="agc", inc=1)

    sb = lambda n, s, d: es.enter_context(nc.sbuf_tensor(_nm(n), s, d))
    cf = C.consts
    U_f = cf[:, C_U:C_U + 128]
    BO_f = cf[:, C_BO:C_BO + 128]
    UF_f = cf[:, C_UF:C_UF + 128]
    ONE_f = cf[:, C_ONE:C_ONE + 128]
    SEL_f = cf[:, C_SEL:C_SEL + 2]
    SL_f = cf[0:64, C_SL:C_SL + 64]
    MD_bf = C.consts_bf[:, C_MD:C_MD + 512]
    whg = sb("mx_whg", [128, 8, 768], BF16)
    wfq = sb("mx_wfq", [128, 8, 256], BF16)
    wfk = sb("mx_wfk", [128, 8, 256], BF16)
    wfv = sb("mx_wfv", [128, 8, 258], BF16)
    for t, d_, n in ((whg, w_hg_d, "mx_whg"), (wfq, w_fq_d, "mx_wfq"), (wfk, w_fk_d, "mx_wfk"), (wfv, w_fv_d, "mx_wfv")):
        DMA(P, "pool", t[:], d_.rearrange("(k p) c -> p k c", p=128), [], [n], n)
    lbl = sb("mx_lbl", [128, 512], F32)
    fbias = sb("mx_fbias", [128, 2], F32)
    DMA(P, "sp", lbl[:], lbl_d, [], ["mx_lbl"], "mx_lbl")
    DMA(P, "sp", fbias[:], fbias_d, [], ["mx_fbias"], "mx_fbias")
    lb = sb("mx_lb", [128, 256], F32)
    oml = sb("mx_oml", [128, 256], F32)
    TT(P, lb[:], lbl[:, 0:256], lbl[:, 256:512], ALU.subtract, ["mx_lbl"], ["mx_lb"])
    ACT(P, lb[:], lb[:], AF.Sigmoid, ["mx_lb"], ["mx_lb"])
    TS(P, oml[:], lb[:], -1.0, 1.0, ALU.mult, ALU.add, ["mx_lb"], ["mx_oml"])
    lbw = sb("mx_lbw", [128, 4, 128], F32)
    omlw = sb("mx_omlw", [128, 4, 128], F32)
    U4 = sb("mx_U4", [128, 4, 128], F32)
    for s_ in range(4):
        CP(P, "dve", U4[:, s_, :], U_f, ["consts"], ["mx_U4"])
    qT = sb("mx_qT", [128, S_ALL], BF16)
    kT = sb("mx_kT", [128, S_ALL], BF16)
    Vp = sb("mx_Vp", [128, 64, 130], BF16)
    fb = sb("mx_fb", [128, 64], F32)
    MSET(P, "dve", Vp[:, :, 128:130], 1.0, ["mx_Vp"])
    hring = Ring(nc, es, "mx_h", 2, [128, 8, 512], BF16)
    f32r = Ring(nc, es, "mx_f", 11, [128, 4, 128], F32)
    bfr = Ring(nc, es, "mx_b", 11, [128, 4, 128], BF16)
    S32 = sb("mx_S32", [128, 128], F32)
    Sbf = Ring(nc, es, "mx_Sbf", 2, [128, 128], BF16)
    dT = Ring(nc, es, "mx_dT", 2, [128, 8], F32)
    ostage = Ring(nc, es, "mx_ost", 2, [128, 512], F32)
    crow = sb("mx_crow", [96, S_ALL], BF16)
    biasG = sb("mx_biasG", [128, 16, 64], F32)
    ccol = sb("mx_ccol", [128, 64], F32)
    offbc = sb("mx_offbc", [128, 64], F32)
    ls = sb("mx_ls", [128, 64], F32)
    totbc = sb("mx_totbc", [64, 128], F32)
    cd = sb("mx_cd", [128, 64], F32)
    c3 = sb("mx_c3", [128, 3, 64], BF16)
    c3T = sb("mx_c3T", [64, 3, 128], BF16)
    r1 = sb("mx_r1", [128, 64], F32)
    l3 = sb("mx_l3", [128, 3, 64], BF16)
    tb3 = sb("mx_tb3", [64, 3, 128], BF16)
    tbr = sb("mx_tbr", [64, 128], F32)
    pTr = Ring(nc, es, "mx_pT", 3, [128, 512], BF16)
    obr = Ring(nc, es, "mx_ob", 2, [128, 128], BF16)
    recr = Ring(nc, es, "mx_rec", 2, [128, 1], F32)
    obT = Ring(nc, es, "mx_obT", 2, [128, 512], F32)
    MSET(P, "dve", crow[:], 0.0, ["mx_crow"])
    ones96 = C.consts_bf[0:96, C_ONE:C_ONE + 128]
    psb = C.ps.bufs
    PS = lambda i: psb[i]
    scale = 1.0 / math.sqrt(128.0)

    for h in range(2):
        for s_ in range(4):
            CP(P, "dve", lbw[:, s_, :], lb[:, h * 128:(h + 1) * 128], ["mx_lb"], ["mx_lbw"])
            CP(P, "dve", omlw[:, s_, :], oml[:, h * 128:(h + 1) * 128], ["mx_oml"], ["mx_omlw"])
        MSET(P, "dve", S32[:], 0.0, ["mx_S32"])
        sb_t, sb_k = Sbf.next()
        MSET(P, "dve", sb_t[:], 0.0, [sb_k])
        for T in range(16):
            rr, t0 = T // 4, (T % 4) * 512
            ht, hk = hring.next()
            for ck in range(4):
                srch = ag1_out[ck][rr * 256:(rr + 1) * 256, t0:t0 + 512].rearrange("(k p) t -> p k t", p=128)
                DMA(P, "sp", ht[:, 2 * ck:2 * ck + 2, :], srch, ["ag1_out"], [hk], "h_" + hk)
            if not (MIX_PARTS & 1):
                continue
            for (w_, dst, nm, wn) in ((wfq, qT, "mx_qT", "mx_wfq"), (wfk, kT, "mx_kT", "mx_wfk")):
                ps, psk = PS(6)
                for kt in range(8):
                    MM(P, ps[:], w_[:, kt, h * 128:(h + 1) * 128], ht[:, kt, :], kt == 0, kt == 7,
                       [wn, hk], [psk])
                ACT(P, dst[:, T * 512:(T + 1) * 512], ps[:], AF.Copy, [psk], [nm])
            ps, psk = PS(7)
            for s_ in range(4):
                for kt in range(8):
                    MM(P, ps[:, s_ * 128:s_ * 128 + 128], ht[:, kt, s_ * 128:(s_ + 1) * 128],
                       wfv[:, kt, h * 129:h * 129 + 128], kt == 0, kt == 7, ["mx_wfv", hk], [psk])
            ACT(P, Vp[:, T * 4:(T + 1) * 4, 0:128], ps[:].rearrange("p (s c) -> p s c", s=4), AF.Copy,
                [psk], ["mx_Vp"])
            ps, psk = PS(7)
            for s_ in range(4):
                for kt in range(8):
                    MM(P, ps[:, s_:s_ + 1], ht[:, kt, s_ * 128:(s_ + 1) * 128],
                       wfv[:, kt, h * 129 + 128:h * 129 + 129], kt == 0, kt == 7, ["mx_wfv", hk], [psk])
            ACT(P, fb[:, T * 4:(T + 1) * 4], ps[:, 0:4], AF.Copy, [psk], ["mx_fb"])
            if not (MIX_PARTS & 2):
                continue
            zq, zqk = PS(0)
            zf, zfk = PS(1)
            zi, zik = PS(2)
            for (z_, zk_, off) in ((zq, zqk, 0), (zf, zfk, 128), (zi, zik, 256)):
                for s_ in range(4):
                    for kt in range(8):
                        MM(P, z_[:, s_ * 128:(s_ + 1) * 128], ht[:, kt, s_ * 128:(s_ + 1) * 128],
                           whg[:, kt, h * 384 + off:h * 384 + off + 128], kt == 0, kt == 7,
                           ["mx_whg", hk], [zk_])
            qs, qsk = f32r.next()
            sg, sgk = f32r.next()
            Ib, Ibk = bfr.next()
            W = lambda t: t[:].rearrange("p s c -> p (s c)")
            ACT(P, W(qs), zq[:], AF.Silu, [zqk], [qsk])
            ACT(P, W(sg), zf[:], AF.Sigmoid, [zfk], [sgk])
            ACT(P, W(Ib), zi[:], AF.Copy, [zik], [Ibk])
            f_, fk_ = f32r.next()
            TT(P, W(f_), W(sg), W(omlw), ALU.mult, [sgk, "mx_omlw"], [fk_])
            TT(P, W(f_), W(f_), W(lbw), ALU.add, [fk_, "mx_lbw"], [fk_])
            lf, lfk = f32r.next()
            ACT(P, W(lf), W(f_), AF.Ln, [fk_], [lfk])
            kk, kkk = f32r.next()
            TS(P, W(kk), W(f_), -1.0, 1.0, ALU.mult, ALU.add, [fk_], [kkk])
            lfh, lfhk = bfr.next()
            lfl, lflk = bfr.next()
            lfr, lfrk = f32r.next()
            CP(P, "dve", W(lfh), W(lf), [lfk], [lfhk])
            TT(P, W(lfr), W(lf), W(lfh), ALU.subtract, [lfk, lfhk], [lfrk])
            CP(P, "dve", W(lfl), W(lfr), [lfrk], [lflk])
            U_b = C.consts_bf[:, C_U:C_U + 128]
            BO_b = C.consts_bf[:, C_BO:C_BO + 128]
            SEL_b = C.consts_bf[:, C_SEL:C_SEL + 2]
            bps, bpsk = PS(3)
            MM(P, bps[:], U_b, W(lfh), True, False, [lfhk, "consts_bf"], [bpsk])
            MM(P, bps[:], U_b, W(lfl), False, True, [lflk, "consts_bf"], [bpsk])
            blps, blpsk = PS(5)
            MM(P, blps[:], BO_b, W(lfh), True, False, [lfhk, "consts_bf"], [blpsk])
            MM(P, blps[:], BO_b, W(lfl), False, True, [lflk, "consts_bf"], [blpsk])
            btps, btpsk = PS(6)
            for s_ in range(4):
                MM(P, btps[:, 2 * s_:2 * s_ + 2], lfh[:, s_, :], SEL_b, True, False, [lfhk, "consts_bf"], [btpsk])
                MM(P, btps[:, 2 * s_:2 * s_ + 2], lfl[:, s_, :], SEL_b, False, True, [lflk, "consts_bf"], [btpsk])
            eb, ebk = f32r.next()
            enb, enbk = f32r.next()
            ebl, eblk = f32r.next()
            d_t, dk_ = dT.next()
            ACT(P, W(eb), bps[:], AF.Exp, [bpsk], [ebk])
            ACT(P, W(enb), bps[:], AF.Exp, [bpsk], [enbk], scale=-1.0)
            ACT(P, W(ebl), blps[:], AF.Exp, [blpsk], [eblk])
            ACT(P, d_t[:], btps[:, 0:8], AF.Exp, [btpsk], [dk_])
            Qt, Qtk = bfr.next()
            Kt32, Kt32k = f32r.next()
            Ktb, Ktbk = bfr.next()
            Kh0, Kh0k = bfr.next()
            Kh1, Kh1k = bfr.next()
            TT(P, W(Qt), W(qs), W(eb), ALU.mult, [qsk, ebk], [Qtk])
            TT(P, W(Kt32), W(kk), W(enb), ALU.mult, [kkk, enbk], [Kt32k])
            CP(P, "dve", W(Ktb), W(Kt32), [Kt32k], [Ktbk])
            STT(P, W(Kh0), W(Kt32), SEL_f[:, 0:1], W(ebl), ALU.mult, ALU.mult, [Kt32k, eblk, "consts"], [Kh0k])
            STT(P, W(Kh1), W(Kt32), SEL_f[:, 1:2], W(ebl), ALU.mult, ALU.mult, [Kt32k, eblk, "consts"], [Kh1k])
            tp, tpk = PS(0)
            tpb = tp.bitcast(BF16)
            tp2, tp2k = PS(7)
            tpb2 = tp2.bitcast(BF16)
            for s_ in range(4):
                TR(P, tpb[:, s_ * 128:(s_ + 1) * 128], Qt[:, s_, :], C.ident_bf, [Qtk, "consts_bf"], [tpk])
                TR(P, tpb2[:, s_ * 128:(s_ + 1) * 128], Ktb[:, s_, :], C.ident_bf,
                   [Ktbk, "consts_bf"], [tp2k])
            QtT, QtTk = bfr.next()
            KtT, KtTk = bfr.next()
            ACT(P, W(QtT), tpb[:, 0:512], AF.Copy, [tpk], [QtTk])
            CP(P, "dve", W(KtT), tpb2[:, 0:512], [tp2k], [KtTk])
            aps, apsk = PS(1)
            for s_ in range(4):
                MM(P, aps[:, s_ * 128:(s_ + 1) * 128], KtT[:, s_, :], QtT[:, s_, :], True, True,
                   [KtTk, QtTk], [apsk])
            Am, Amk = bfr.next()
            TT(P, W(Am), aps[:], W(U4), ALU.mult, [apsk, "mx_U4"], [Amk])
            ops_, opsk = PS(4)
            for s_ in range(4):
                MM(P, ops_[:, s_ * 128:(s_ + 1) * 128], Ib[:, s_, :], Am[:, s_, :], True, False,
                   [Ibk, Amk], [opsk])
                for c_ in range(2):
                    col = s_ * 128 + c_ * 64
                    MM(P, ops_[:, col:col + 64], sb_t[:], QtT[:, s_, c_ * 64:(c_ + 1) * 64], False,
                       c_ == 1, [sb_k, QtTk], [opsk])
                    sups, supsk = PS(5)
                    Khc, Khck = (Kh0, Kh0k) if c_ == 0 else (Kh1, Kh1k)
                    MM(P, sups[:, 0:128], Khc[:, s_, :], Ib[:, s_, :], True, True, [Khck, Ibk], [supsk])
                    STT(P, S32[:], S32[:], d_t[:, 2 * s_ + c_:2 * s_ + c_ + 1], sups[:, 0:128],
                        ALU.mult, ALU.add, ["mx_S32", dk_, supsk], ["mx_S32"])
                    sb_t, sb_k = Sbf.next()
                    ACT(P, sb_t[:], S32[:], AF.Copy, ["mx_S32"], [sb_k])
            ot, otk = ostage.next()
            ACT(P, ot[:], ops_[:], AF.Copy, [opsk], [otk])
            DMA(P, "sp", ag2_in[h][T // 4][:, (T % 4) * 512:(T % 4 + 1) * 512], ot[:], [otk],
                ["ag2_in_%d_%d" % (h, T // 4)], "o_" + otk)
            if T % 4 == 3:
                gather(h, T // 4)
        if not (MIX_PARTS & 4):
            continue
        ACT(P, ls[:], fb[:], AF.Sigmoid, ["mx_fb", "mx_fbias"], ["mx_ls"], bias=fbias[:, h:h + 1])
        ACT(P, ls[:], ls[:], AF.Ln, ["mx_ls"], ["mx_ls"])
        def split3(dst, src, tmp, skey, dkey, tkey):
            CP(P, "dve", dst[:, 0, :], src, [skey], [dkey])
            TT(P, tmp, src, dst[:, 0, :], ALU.subtract, [skey, dkey], [tkey])
            CP(P, "dve", dst[:, 1, :], tmp, [tkey], [dkey])
            TT(P, tmp, tmp, dst[:, 1, :], ALU.subtract, [tkey, dkey], [tkey])
            CP(P, "dve", dst[:, 2, :], tmp, [tkey], [dkey])
        split3(l3, ls[:], r1[:], "mx_ls", "mx_l3", "mx_r1")
        UF_b = C.consts_bf[:, C_UF:C_UF + 128]
        SL_b = C.consts_bf[0:64, C_SL:C_SL + 64]
        ps, psk = PS(0)
        for a_ in range(3):
            MM(P, ps[0:64, 0:128], l3[:, a_, :], C.ones_bf, a_ == 0, a_ == 2, ["mx_l3", "consts_bf"], [psk])
        CP(P, "dve", totbc[:], ps[0:64, 0:128], [psk], ["mx_totbc"])
        split3(tb3, totbc[:], tbr[:], "mx_totbc", "mx_tb3", "mx_tbr")
        ps, psk = PS(1)
        for a_ in range(3):
            MM(P, ps[:, 0:64], tb3[:, a_, :], SL_b, a_ == 0, a_ == 2, ["mx_tb3", "consts_bf"], [psk])
        CP(P, "dve", offbc[:], ps[:, 0:64], [psk], ["mx_offbc"])
        ps2, ps2k = PS(2)
        for a_ in range(3):
            MM(P, ps2[:, 0:64], UF_b, l3[:, a_, :], a_ == 0, a_ == 2, ["mx_l3", "consts_bf"], [ps2k])
        TT(P, ccol[:], ps2[:, 0:64], offbc[:], ALU.add, [ps2k, "mx_offbc"], ["mx_ccol"])
        for G in range(16):
            nj = 4 * G + 4
            cref = offbc[:, 4 * G:4 * G + 1]
            TS(P, biasG[:, G, 0:nj], ccol[:, 0:nj], cref, -1.0, ALU.subtract, ALU.mult,
               ["mx_ccol", "mx_offbc"], ["mx_biasG"])
            TS(P, cd[:, 4 * G:4 * G + 4], ccol[:, 4 * G:4 * G + 4], cref, 1.0 / scale, ALU.subtract, ALU.mult,
               ["mx_ccol", "mx_offbc"], ["mx_cd"])
        CP(P, "dve", c3[:, 0, :], cd[:], ["mx_cd"], ["mx_c3"])
        TT(P, r1[:], cd[:], c3[:, 0, :], ALU.subtract, ["mx_cd", "mx_c3"], ["mx_r1"])
        CP(P, "dve", c3[:, 1, :], r1[:], ["mx_r1"], ["mx_c3"])
        TT(P, r1[:], r1[:], c3[:, 1, :], ALU.subtract, ["mx_r1", "mx_c3"], ["mx_r1"])
        CP(P, "dve", c3[:, 2, :], r1[:], ["mx_r1"], ["mx_c3"])
        tp, tpk = PS(3)
        tpb = tp.bitcast(BF16)
        for a in range(3):
            TR(P, tpb[0:64, a * 128:(a + 1) * 128], c3[:, a, :], C.ident_bf, ["mx_c3", "consts_bf"], [tpk])
        CP(P, "dve", c3T[:].rearrange("p a c -> p (a c)"), tpb[0:64, 0:384], [tpk], ["mx_c3T"])
        DMA(P, "sp", crow_d.rearrange("a (j t) -> j a t", j=64), c3T[:], ["mx_c3T"], ["crow_d"], "crow_d")
        for a in range(3):
            DMA(P, "sp", crow[32 * a:32 * a + 1, :], crow_d[a:a + 1, :], ["crow_d"], ["mx_crow"], "mx_crow")
        for G in range(16):
            q0 = G * 512
            o_ps = [PS(4), PS(5), PS(0), PS(1)]
            for j in range(4 * G + 4):
                i_ = j - 4 * G
                c0 = 0 if i_ < 0 else 128 * i_
                n = 512 - c0
                st, stk = PS((6, 7, 2)[j % 3])
                MM(P, st[:, 0:n], kT[:, j * 128:(j + 1) * 128], qT[:, q0 + c0:q0 + 512], True, False,
                   ["mx_kT", "mx_qT"], [stk])
                MM(P, st[:, 0:n], ones96, crow[:, q0 + c0:q0 + 512], False, i_ < 0, ["mx_crow", "consts_bf"], [stk])
                if i_ >= 0:
                    MM(P, st[:, 0:n], C.ident_bf, MD_bf[:, 0:n], False, True, ["consts_bf"], [stk])
                p_t, pk = pTr.next()
                ACT(P, p_t[:, 0:n], st[:, 0:n], AF.Exp, [stk, "mx_biasG"], [pk], bias=biasG[:, G, j:j + 1], scale=scale)
                for qb in range(4):
                    if 128 * qb < c0:
                        continue
                    op_t, op_k = o_ps[qb]
                    oc = 0
                    MM(P, op_t[:, oc:oc + 129], p_t[:, 128 * qb - c0:128 * qb - c0 + 128], Vp[:, j, 0:129],
                       j == 0, j == 4 * G + qb, [pk, "mx_Vp"], [op_k])
                if i_ >= 0:
                    qb = i_
                    op_t, op_k = o_ps[qb]
                    oc = 0
                    okey = op_k
                    rc, rck = recr.next()
                    RECIP(P, rc[:], op_t[:, oc + 128:oc + 129], [okey], [rck])
                    ob_, obk = obr.next()
                    ACT(P, ob_[:], op_t[:, oc:oc + 128], AF.Identity, [okey, rck], [obk], scale=rc[:])
                    tp, tpk = PS(3)
                    tpb = tp.bitcast(BF16)
                    TR(P, tpb[:, 0:128], ob_[:], C.ident_bf, [obk, "consts_bf"], [tpk])
                    if qb == 0:
                        obT_t, obT_k = obT.next()
                    CP(P, "dve", obT_t[:, qb * 128:(qb + 1) * 128], tpb[:, 0:128], [tpk], [obT_k])
                    if qb == 3:
                        DMA(P, "sp", ag2_in[2 + h][G // 4][:, (G % 4) * 512:(G % 4 + 1) * 512], obT_t[:],
                            [obT_k], ["ag2_in_%d_%d" % (2 + h, G // 4)], "o_" + obT_k)
                        if G % 4 == 3:
                            gather(2 + h, G // 4)


_CACHE = {}


def _prep_inputs(inp):
    f = lambda a: np.ascontiguousarray(np.asarray(a, dtype=np.float32))
    x = f(inp["x"])
    mem = f(inp["mem"])
    w_in = f(inp["w_in"])[0]
    offs = [0, 1024, 2048, 3072, 4096, 5120, 6144, 7168, 7176, 9224]
    q_a, f_a, i_a, g_a, q_b, k_b, v_b, f_b, gates = [w_in[:, offs[i]:offs[i + 1]] for i in range(9)]
    gains = np.zeros((128, 96), np.float32)
    for i, n in enumerate(GAIN_NAMES):
        gains[:, 8 * i:8 * i + 8] = f(inp[n])[0].reshape(8, 128).T
    gains[:, 80:96] = f(inp["b_gate"])[0].reshape(16, 128).T
    consts = make_consts()
    w_loc = np.ascontiguousarray(np.concatenate([g_a, gates], axis=1))
    lbl_all = f(inp["hg_lb_logits"])
    fbias_all = f(inp["fox_f_bias"])[0]
    shared = {
        "ffn1_w_in": f(inp["ffn1_w_in"])[0], "ffn1_w_down": f(inp["ffn1_w_down"])[0],
        "ffn2_w_in": f(inp["ffn2_w_in"])[0], "ffn2_w_down": f(inp["ffn2_w_down"])[0],
        "w_loc": w_loc, "w_branch_a": f(inp["w_branch_a"])[0], "w_branch_b": f(inp["w_branch_b"])[0],
        "w_out": f(inp["w_out"])[0], "w_mq": f(inp["w_mq"])[0], "w_mkv": f(inp["w_mkv"])[0],
        "w_mo": f(inp["w_mo"])[0], "gains": gains, "consts": consts,
    }
    maps = []
    for c in range(8):
        b, r = c // 4, c % 4
        hs = [2 * r, 2 * r + 1]
        m = dict(shared)
        m["xT"] = np.ascontiguousarray(x[b, r * S_OWN:(r + 1) * S_OWN, :].T)
        m["memT"] = np.ascontiguousarray(mem[b].T)
        sl = lambda w, h: w[:, h * 128:(h + 1) * 128]
        m["w_hg"] = np.ascontiguousarray(np.concatenate(
            [np.concatenate([sl(q_a, h), sl(f_a, h), sl(i_a, h)], axis=1) for h in hs], axis=1))
        m["w_fq"] = np.ascontiguousarray(np.concatenate([sl(q_b, h) for h in hs], axis=1))
        m["w_fk"] = np.ascontiguousarray(np.concatenate([sl(k_b, h) for h in hs], axis=1))
        m["w_fv"] = np.ascontiguousarray(np.concatenate(
            [np.concatenate([sl(v_b, h), f_b[:, h:h + 1]], axis=1) for h in hs], axis=1))
        l0 = np.concatenate([lbl_all[0, h] for h in hs])
        l1 = np.concatenate([lbl_all[1, h] for h in hs])
        m["lbl"] = np.ascontiguousarray(np.tile(np.concatenate([l0, l1])[None, :], (128, 1)))
        m["fbias"] = np.ascontiguousarray(np.tile(fbias_all[hs][None, :], (128, 1)))
        maps.append(m)
    return maps


def kernel(**inputs):
    dbg = bool(inputs.pop("_dbg", False))
    key = ("nc", dbg)
    if key not in _CACHE:
        _CACHE[key] = build_program(dbg)
    nc = _CACHE[key]
    maps = _prep_inputs(inputs)
    res = run_bass_kernel_spmd(nc, maps, core_ids=list(range(8)))
    out = np.zeros((2, S_ALL, D), np.float32)
    for c in range(8):
        b, r = c // 4, c % 4
        out[b, r * S_OWN:(r + 1) * S_OWN, :] = np.asarray(res.results[c]["outT"]).T
    if dbg:
        return out, res.results
    return out
```
